# Optimizing a Trainium2 kernel written in Bass

```python
import math
import jax
import jax.numpy as jnp
from jax import lax
import numpy as np

D_MODEL = 1024
BATCH = 8
SEQ = 4096
DEPTH = 4

MEM_LEN = 256
SB_HEADS = 8
SB_HEAD_DIM = 64
SB_WIDTH = SB_HEADS * SB_HEAD_DIM
SB_BLOCK = 128
GDN_HEADS = 4
GDN_HEAD_DIM = 128
GDN_WIDTH = GDN_HEADS * GDN_HEAD_DIM
GDN_CHUNK = 64
CONV_WIDTH = 4
MIX_WIDTH = SB_WIDTH + GDN_WIDTH
OFF_SB = 3 * SB_WIDTH
OFF_GQKV = OFF_SB + 3 * GDN_WIDTH
OFF_GZ = OFF_GQKV + GDN_WIDTH
OFF_GA = OFF_GZ + GDN_HEADS
N_IN = OFF_GA + GDN_HEADS
XATTN_HEADS = 4
XATTN_HEAD_DIM = D_MODEL // XATTN_HEADS
D_FF = 2816
RMS_EPS = 1e-6
L2_EPS = 1e-6

kernel_name = "hymba_sb_gdn_macaron_memory_trunk"

F32 = jnp.float32


def rms_norm(x, gain, eps=RMS_EPS):
    xf = x.astype(F32)
    y = xf * lax.rsqrt(jnp.mean(xf * xf, axis=-1, keepdims=True) + eps)
    return (y * gain.astype(F32)).astype(x.dtype)


def l2_normalize(x, eps=L2_EPS):
    xf = x.astype(F32)
    return xf * lax.rsqrt(jnp.sum(xf * xf, axis=-1, keepdims=True) + eps)


def swiglu_ffn(x, w_in, w_out):
    gate, up = jnp.split(x @ w_in, 2, axis=-1)
    return (jax.nn.silu(gate) * up) @ w_out


def stick_breaking_attention(q, k, v):
    B, H, S, Dh = q.shape
    scale = Dh ** -0.5
    outs = []
    for blk in range(S // SB_BLOCK):
        q0 = blk * SB_BLOCK
        q1 = q0 + SB_BLOCK
        qb = q[:, :, q0:q1]
        kb = k[:, :, :q1]
        vb = v[:, :, :q1]
        z = jnp.einsum('bhtd,bhsd->bhts', qb, kb, preferred_element_type=F32) * scale
        t_idx = q0 + jnp.arange(SB_BLOCK)[:, None]
        s_idx = jnp.arange(q1)[None, :]
        causal = s_idx < t_idx
        log_beta = jax.nn.log_sigmoid(z)
        log_1m_beta = jnp.where(causal, log_beta - z, 0.0)
        between = lax.cumsum(log_1m_beta, axis=3, reverse=True) - log_1m_beta
        a = jnp.where(causal, jnp.exp(log_beta + between), 0.0)
        outs.append(jnp.einsum('bhts,bhsd->bhtd', a.astype(v.dtype), vb))
    return jnp.concatenate(outs, axis=2)


def causal_depthwise_conv(x, w):
    K, C = w.shape
    return lax.conv_general_dilated(
        x, w[:, None, :].astype(x.dtype), window_strides=(1,), padding=[(K - 1, 0)],
        dimension_numbers=('NWC', 'WIO', 'NWC'), feature_group_count=C)


def gated_delta_rule(q, k, v, g, beta):
    B, H, S, Dk = q.shape
    Dv = v.shape[-1]
    C = GDN_CHUNK
    N = S // C
    q = q * (Dk ** -0.5)

    def chunks(t):
        return t.reshape(B, H, N, C, *t.shape[3:])

    q, k, v, g, beta = chunks(q), chunks(k), chunks(v), chunks(g), chunks(beta)
    g = jnp.cumsum(g, axis=-1)
    tril_incl = jnp.tril(jnp.ones((C, C), dtype=bool))
    strict = jnp.tril(jnp.ones((C, C), dtype=bool), -1)
    diff = g[..., :, None] - g[..., None, :]
    decay = jnp.where(tril_incl, jnp.exp(jnp.where(tril_incl, diff, 0.0)), 0.0)

    kbeta = k * beta[..., None]
    lower = jnp.where(strict, jnp.einsum('bhnck,bhnek->bhnce', kbeta, k) * decay, 0.0)
    eye = jnp.eye(C, dtype=F32)
    rhs = jnp.concatenate([v * beta[..., None], kbeta * jnp.exp(g)[..., None]], axis=-1)
    sol = lax.linalg.triangular_solve(lower + eye, rhs, left_side=True, lower=True)
    u = sol[..., :Dv]
    w = sol[..., Dv:]

    qk = jnp.where(tril_incl, jnp.einsum('bhnck,bhnek->bhnce', q, k) * decay, 0.0)
    q_dec = q * jnp.exp(g)[..., None]
    g_last = g[..., -1]
    k_to_end = k * jnp.exp(g_last[..., None] - g)[..., None]

    def step(state, xs):
        q_i, qk_i, u_i, w_i, gl_i, kend_i = xs
        v_new = u_i - jnp.einsum('bhck,bhkv->bhcv', w_i, state)
        o_i = (jnp.einsum('bhck,bhkv->bhcv', q_i, state)
               + jnp.einsum('bhce,bhev->bhcv', qk_i, v_new))
        state = (state * jnp.exp(gl_i)[..., None, None]
                 + jnp.einsum('bhck,bhcv->bhkv', kend_i, v_new))
        return state, o_i

    xs = tuple(jnp.moveaxis(t, 2, 0) for t in (q_dec, qk, u, w, g_last, k_to_end))
    state0 = jnp.zeros((B, H, Dk, Dv), F32)
    _, o = lax.scan(step, state0, xs)
    return jnp.moveaxis(o, 0, 2).reshape(B, H, S, Dv)


def hybrid_mixer(xn, w_in, conv_w, a_log, dt_bias, sb_out_norm, gdn_out_norm, w_out):
    B, S, _ = xn.shape
    proj = xn @ w_in
    sb_qkv, gdn_qkv, gdn_z, gdn_a, gdn_b = jnp.split(
        proj, [OFF_SB, OFF_GQKV, OFF_GZ, OFF_GA], axis=-1)

    def to_heads(t, h, d):
        return t.reshape(B, S, h, d).transpose(0, 2, 1, 3)

    sq, sk, sv = (to_heads(t, SB_HEADS, SB_HEAD_DIM) for t in jnp.split(sb_qkv, 3, axis=-1))
    sb = stick_breaking_attention(sq, sk, sv)
    sb = rms_norm(sb, sb_out_norm.reshape(SB_HEADS, 1, SB_HEAD_DIM))
    sb = sb.transpose(0, 2, 1, 3).reshape(B, S, SB_WIDTH)

    gdn_qkv = jax.nn.silu(causal_depthwise_conv(gdn_qkv, conv_w))
    gq, gk, gv = (to_heads(t, GDN_HEADS, GDN_HEAD_DIM) for t in jnp.split(gdn_qkv, 3, axis=-1))
    gq, gk = l2_normalize(gq), l2_normalize(gk)
    beta = jax.nn.sigmoid(gdn_b.astype(F32)).transpose(0, 2, 1)
    g = (-jnp.exp(a_log.astype(F32))
         * jax.nn.softplus(gdn_a.astype(F32) + dt_bias.astype(F32))).transpose(0, 2, 1)
    go = gated_delta_rule(gq, gk, gv.astype(F32), g, beta)
    go = go.transpose(0, 2, 1, 3)
    go = rms_norm(go, gdn_out_norm) * jax.nn.silu(
        gdn_z.reshape(B, S, GDN_HEADS, GDN_HEAD_DIM).astype(F32))
    go = go.reshape(B, S, GDN_WIDTH).astype(xn.dtype)

    return jnp.concatenate([sb, go], axis=-1) @ w_out


def memory_cross_attention(xn, memn, w_q, w_kv, w_o):
    B, S, D = xn.shape
    M = memn.shape[1]
    q = (xn @ w_q).reshape(B, S, XATTN_HEADS, XATTN_HEAD_DIM)
    k, v = jnp.split(memn @ w_kv, 2, axis=-1)
    k = k.reshape(B, M, XATTN_HEADS, XATTN_HEAD_DIM)
    v = v.reshape(B, M, XATTN_HEADS, XATTN_HEAD_DIM)
    s = jnp.einsum('bshd,bmhd->bhsm', q, k, preferred_element_type=F32) * (XATTN_HEAD_DIM ** -0.5)
    p = jax.nn.softmax(s, axis=-1).astype(v.dtype)
    o = jnp.einsum('bhsm,bmhd->bshd', p, v).reshape(B, S, D)
    return o @ w_o


def setup_inputs(seed: int = 0) -> dict:
    key = jax.random.key(seed)
    ks = jax.random.split(key, 24)
    L, D = DEPTH, D_MODEL

    def normal(k, shape, scale):
        return jax.random.normal(k, shape, F32) * scale

    def gain(k, shape):
        return 1.0 + 0.02 * jax.random.normal(k, shape, F32)

    dt = jnp.exp(jax.random.uniform(ks[9], (L, GDN_HEADS), F32, math.log(1e-3), math.log(1e-1)))
    return {
        "x": normal(ks[0], (BATCH, SEQ, D), 1.0),
        "mem": normal(ks[1], (BATCH, MEM_LEN, D), 1.0),
        "ffn1_norm": gain(ks[2], (L, D)),
        "ffn1_w_in": normal(ks[3], (L, D, 2 * D_FF), D ** -0.5),
        "ffn1_w_out": normal(ks[4], (L, D_FF, D), D_FF ** -0.5),
        "mix_norm": gain(ks[5], (L, D)),
        "w_in": normal(ks[6], (L, D, N_IN), D ** -0.5),
        "conv_w": normal(ks[7], (L, CONV_WIDTH, 3 * GDN_WIDTH), CONV_WIDTH ** -0.5),
        "a_log": jnp.log(jax.random.uniform(ks[8], (L, GDN_HEADS), F32, 1.0, 16.0)),
        "dt_bias": dt + jnp.log(-jnp.expm1(-dt)),
        "sb_out_norm": gain(ks[10], (L, SB_WIDTH)),
        "gdn_out_norm": gain(ks[11], (L, GDN_HEAD_DIM)),
        "w_out": normal(ks[12], (L, MIX_WIDTH, D), MIX_WIDTH ** -0.5),
        "xattn_norm": gain(ks[13], (L, D)),
        "mem_norm": gain(ks[14], (L, D)),
        "xattn_w_q": normal(ks[15], (L, D, D), D ** -0.5),
        "xattn_w_kv": normal(ks[16], (L, D, 2 * D), D ** -0.5),
        "xattn_w_o": normal(ks[17], (L, D, D), D ** -0.5),
        "ffn2_norm": gain(ks[18], (L, D)),
        "ffn2_w_in": normal(ks[19], (L, D, 2 * D_FF), D ** -0.5),
        "ffn2_w_out": normal(ks[20], (L, D_FF, D), D_FF ** -0.5),
        "final_norm": gain(ks[21], (D,)),
    }


def reference(x, mem, ffn1_norm, ffn1_w_in, ffn1_w_out, mix_norm, w_in, conv_w, a_log, dt_bias,
              sb_out_norm, gdn_out_norm, w_out, xattn_norm, mem_norm, xattn_w_q, xattn_w_kv,
              xattn_w_o, ffn2_norm, ffn2_w_in, ffn2_w_out, final_norm):
    h = x
    for l in range(DEPTH):
        h = h + 0.5 * swiglu_ffn(rms_norm(h, ffn1_norm[l]), ffn1_w_in[l], ffn1_w_out[l])
        h = h + hybrid_mixer(rms_norm(h, mix_norm[l]), w_in[l], conv_w[l], a_log[l], dt_bias[l],
                             sb_out_norm[l], gdn_out_norm[l], w_out[l])
        h = h + memory_cross_attention(rms_norm(h, xattn_norm[l]), rms_norm(mem, mem_norm[l]),
                                       xattn_w_q[l], xattn_w_kv[l], xattn_w_o[l])
        h = h + 0.5 * swiglu_ffn(rms_norm(h, ffn2_norm[l]), ffn2_w_in[l], ffn2_w_out[l])
    return rms_norm(h, final_norm)
```

```python
import numpy as np
import concourse.bass as bass
import concourse.mybir as mybir
from concourse.bass_utils import run_bass_kernel_spmd
from contextlib import ExitStack

F32 = mybir.dt.float32
BF16 = mybir.dt.bfloat16
ALU = mybir.AluOpType
AF = mybir.ActivationFunctionType
AX = mybir.AxisListType

D = 1024
S = 4096
DEPTH = 4
MEM = 256
DFF = 2816
NFC = DFF // 128
N_IN = 3592
SBH, SBD = 8, 64
GH, GD = 4, 128
XH, XD = 4, 256
OFF_SBQ, OFF_SBK, OFF_SBV = 0, 512, 1024
OFF_GQ, OFF_GK, OFF_GV = 1536, 2048, 2560
OFF_GZ, OFF_GA, OFF_GB = 3072, 3584, 3588
RMS_EPS = 1e-6
L2_EPS = 1e-6
NT = S // 128

ENGS = ("pe", "act", "dve", "pool", "sp")
NDMA = 16
NDMA_SP = 12


class Res:
    __slots__ = ("name", "lastw", "readers", "excl")

    def __init__(self, name):
        self.name = name
        self.lastw = None
        self.readers = {}
        self.excl = (name[0] in ("ps", "ps0", "ps1", "ps3", "ps4", "ps5"))


class Prog:
    def __init__(self, nc):
        self.nc = nc
        self.ops = {e: [] for e in ENGS}
        self.known = {e: {} for e in ENGS}
        self.dma_rr = 0
        self.dma_rr2 = 0
        self.dma_cnt = [0] * NDMA
        self.dma_last = [None] * NDMA
        self.res = {}

    def R(self, *key):
        r = self.res.get(key)
        if r is None:
            r = self.res[key] = Res(key)
        return r

    def _waits(self, eng, raw, other):
        out = []
        kn = self.known[eng]
        for tok, is_raw in [(t, True) for t in raw] + [(t, False) for t in other]:
            if tok is None:
                continue
            kind, key, val = tok
            if kind == "e" and key == eng:
                if eng in ("pe", "sp"):
                    continue
            k = (kind, key)
            if kn.get(k, -1) >= val:
                continue
            kn[k] = val
            out.append(tok)
            if kind == "e":
                self.ops[key][val][2] = True
        return out

    def _deps(self, r, w, eng=None):
        raw = [x.lastw for x in r]
        other = []
        for x in r:
            if x.excl:
                other.extend(t for k, t in x.readers.items() if k != eng)
        for x in w:
            other.append(x.lastw)
            other.extend(x.readers.values())
        return raw, other

    def _commit(self, tok, eng, r, w):
        for x in r:
            x.readers[eng] = tok
        for x in w:
            x.lastw = tok
            x.readers = {}

    def op(self, eng, fn, r=(), w=()):
        raw, other = self._deps(r, w, eng)
        waits = self._waits(eng, raw, other)
        idx = len(self.ops[eng])
        self.ops[eng].append([fn, waits, False, None])
        tok = ("e", eng, idx)
        self._commit(tok, eng, r, w)
        return tok

    def dma(self, out_ap, in_ap, r=(), w=(), q="sp", **kw):
        if q == "sp":
            slot = self.dma_rr
            self.dma_rr = (slot + 1) % NDMA_SP
        else:
            slot = NDMA_SP + self.dma_rr2
            self.dma_rr2 = (self.dma_rr2 + 1) % (NDMA - NDMA_SP)
        raw, other = self._deps(r, w)
        other = list(other) + [self.dma_last[slot]]
        waits = self._waits(q, raw, other)
        self.dma_cnt[slot] += 16
        tok = ("d", slot, self.dma_cnt[slot])
        self.dma_last[slot] = tok

        def fn(e, out_ap=out_ap, in_ap=in_ap, kw=kw):
            return e.dma_start(out=out_ap, in_=in_ap, **kw)

        self.ops[q].append([fn, waits, False, slot])
        for x in r:
            x.readers[("dma", slot)] = tok
        for x in w:
            x.lastw = tok
            x.readers = {}
        return tok

    def _last_toks(self):
        toks = []
        for e in ENGS:
            if e == "sp":
                continue
            for i in range(len(self.ops[e]) - 1, -1, -1):
                o = self.ops[e][i]
                if o[0] is not None and o[3] is None:
                    toks.append(("e", e, i))
                    break
        toks += [t for t in self.dma_last if t is not None]
        return toks

    def barrier(self):
        toks = self._last_toks()
        for e in ENGS:
            waits = self._waits(e, toks, [])
            if waits:
                self.ops[e].append([None, waits, False, None])

    def finish(self):
        toks = self._last_toks()
        waits = self._waits("sp", toks, [])
        self.ops["sp"].append([None, waits, False, None])

    def emit(self, stack):
        nc = self.nc
        esem = {e: stack.enter_context(nc.semaphore("s_" + e)) for e in ENGS if e != "sp"}
        dsem = [stack.enter_context(nc.semaphore("d%d" % i)) for i in range(NDMA)]
        val = {}
        for e in ENGS:
            if e == "sp":
                continue
            c = 0
            for i, o in enumerate(self.ops[e]):
                if o[2] and o[0] is not None and o[3] is None:
                    c += 1
                    val[(e, i)] = c
                elif o[2]:
                    raise RuntimeError("flagged a non-instruction op")

        def replay(name, eng):
            for i, (fn, waits, flag, slot) in enumerate(self.ops[name]):
                for kind, key, v in waits:
                    if kind == "e":
                        eng.wait_ge(esem[key], val[(key, v)])
                    else:
                        eng.wait_ge(dsem[key], v)
                if fn is None:
                    continue
                ins = fn(eng)
                if slot is not None:
                    ins.then_inc(dsem[slot], 16)
                elif flag:
                    ins.then_inc(esem[name], 1)

        block = stack.enter_context(nc.Block())

        @block.tensor
        def _(eng):
            replay("pe", eng)

        @block.scalar
        def _(eng):
            replay("act", eng)

        @block.vector
        def _(eng):
            replay("dve", eng)

        @block.gpsimd
        def _(eng):
            replay("pool", eng)

        @block.sync
        def _(eng):
            replay("sp", eng)


class Arena:
    def __init__(self, nc, base, limit):
        self.nc = nc
        self.top = base
        self.limit = limit
        self.hi = limit
        self.n = 0

    def alloc_hi(self, name, shape, dtype):
        esz = 4 if dtype == F32 else 2
        per = esz
        for s in shape[1:]:
            per *= s
        off = (self.hi - per) // 64 * 64
        if off < self.top:
            raise RuntimeError("SBUF arena overflow (hi) at %s" % name)
        self.hi = off
        self.n += 1
        return self.nc.alloc_sbuf_tensor_at("%s_%d" % (name, self.n), list(shape), dtype, offset=off)

    def release_hi(self):
        self.hi = self.limit

    def mark(self):
        return self.top

    def release(self, m):
        self.top = m

    def alloc(self, name, shape, dtype):
        esz = 4 if dtype == F32 else 2
        per = esz
        for s in shape[1:]:
            per *= s
        off = (self.top + 63) // 64 * 64
        if off + per > self.hi:
            raise RuntimeError("SBUF arena overflow at %s: %d + %d > %d" % (name, off, per, self.hi))
        self.top = off + per
        self.n += 1
        return self.nc.alloc_sbuf_tensor_at("%s_%d" % (name, self.n), list(shape), dtype, offset=off)


def build_nc(cfg=None):
    cfg = cfg or {}
    n_layers = cfg.get("layers", DEPTH)
    phases = cfg.get("phases", ("ffn1", "mix", "xattn", "ffn2"))
    do_sb = cfg.get("sb", True)
    do_gdn = cfg.get("gdn", True)
    nc = bass.Bass("TRN2", target_bir_lowering=False)
    stack = ExitStack()
    P = Prog(nc)
    L = n_layers

    def din(name, shape):
        return nc.dram_tensor(name, list(shape), F32, kind="ExternalInput").ap()

    x_d = din("x", [S, D])
    mem_d = din("mem", [MEM, D])
    wd = {}
    for name, shape in [("ffn1_norm", [L, D]), ("ffn1_w_in", [L, D, 2 * DFF]), ("ffn1_w_out", [L, DFF, D]),
                        ("mix_norm", [L, D]), ("w_in", [L, D, N_IN]), ("conv_w", [L, 4, 1536]),
                        ("a_log", [L, 4]), ("dt_bias", [L, 4]), ("sb_out_norm", [L, 512]),
                        ("gdn_out_norm", [L, 128]), ("w_out", [L, D, D]), ("xattn_norm", [L, D]),
                        ("mem_norm", [L, D]), ("xattn_w_q", [L, D, D]), ("xattn_w_kv", [L, D, 2 * D]),
                        ("xattn_w_o", [L, D, D]), ("ffn2_norm", [L, D]), ("ffn2_w_in", [L, D, 2 * DFF]),
                        ("ffn2_w_out", [L, DFF, D]), ("final_norm", [1, D])]:
        wd[name] = din(name, shape)
    out_d = nc.dram_tensor("out", [S, D], F32, kind="ExternalOutput").ap()
    hbuf = nc.dram_tensor("hbuf", [S, D], F32, kind="Internal").ap()
    mixT_d = nc.dram_tensor("mixT_d", [8, 128, S], BF16, kind="Internal").ap()
    zs_d = nc.dram_tensor("zs_d", [S, 512], F32, kind="Internal").ap()
    wb = {}
    for name in ("ffn1_w_in", "ffn1_w_out", "w_in", "w_out", "xattn_w_q", "xattn_w_kv", "xattn_w_o",
                 "ffn2_w_in", "ffn2_w_out"):
        wb[name] = nc.dram_tensor(name + "_bf", list(wd[name].shape), BF16, kind="Internal").ap()

    need = set()
    for ph in phases:
        need |= {"ffn1": {"ffn1_w_in", "ffn1_w_out"}, "ffn2": {"ffn2_w_in", "ffn2_w_out"},
                 "mix": {"w_in", "w_out"}, "xattn": {"xattn_w_q", "xattn_w_kv", "xattn_w_o"}}[ph]
    for l in range(n_layers):
        for name in wb:
            if name not in need:
                continue
            rows = wd[name].shape[1]
            for r0 in range(0, rows, 256):
                r1 = min(rows, r0 + 256)
                P.dma(wb[name][l, r0:r1, :], wd[name][l, r0:r1, :], w=[P.R("wb", name, l)], q="pool",
                      max_dma_last_dim=8192)

    ar = Arena(nc, 16640, 229376 - 128)
    ident = ar.alloc("ident", [128, 128], BF16)
    identf = ar.alloc("identf", [128, 128], F32)
    onesf = ar.alloc("onesf", [128, 128], F32)
    onesb = ar.alloc("onesb", [128, 128], BF16)
    zerob = ar.alloc("zerob", [128, 128], BF16)
    triuf = ar.alloc("triuf", [128, 128], F32)
    ltri = ar.alloc("ltri", [128, 128], BF16)
    utri = ar.alloc("utri", [128, 128], BF16)
    m01 = ar.alloc("m01", [128, 128], BF16)
    blk64 = ar.alloc("blk64", [128, 128], BF16)
    gain = [ar.alloc("gain", [128, D], F32) for _ in range(1)]
    stat = ar.alloc("stat", [128, 64], F32)
    eps_col = ar.alloc("eps", [128, 1], F32)
    xn_junk = ar.alloc("xn_junk", [128, D], BF16)
    psall = stack.enter_context(nc.psum_tensor("psall", [128, 4096], F32))
    psbf_all = psall.bitcast(BF16)

    def ps(i, c0=0, c1=512):
        return psall[:, i * 512 + c0:i * 512 + c1]

    def psb(i, c0=0, c1=1024):
        return psbf_all[:, i * 1024 + c0:i * 1024 + c1]

    psr = [P.R("ps", i) for i in range(8)]
    CR = P.R("consts")

    def cop(fn):
        P.op("pool", fn, r=[CR], w=[CR])

    cop(lambda e: e.memset(identf[:], 0.0))
    cop(lambda e: e.affine_select(identf[:], identf[:], [[-1, 128]], ALU.not_equal, 1.0, base=0,
                                  channel_multiplier=1))
    cop(lambda e: e.tensor_copy(ident[:], identf[:]))
    cop(lambda e: e.memset(onesf[:], 1.0))
    cop(lambda e: e.memset(onesb[:], 1.0))
    cop(lambda e: e.memset(zerob[:], 0.0))
    cop(lambda e: e.memset(eps_col[:], RMS_EPS))
    cop(lambda e: e.affine_select(triuf[:], onesf[:], [[1, 128]], ALU.is_ge, 0.0, base=0, channel_multiplier=-1))
    cop(lambda e: e.affine_select(ltri[:], onesb[:], [[-1, 128]], ALU.is_ge, 0.0, base=0, channel_multiplier=1))
    cop(lambda e: e.affine_select(utri[:], onesb[:], [[1, 128]], ALU.is_gt, 0.0, base=0, channel_multiplier=-1))
    cop(lambda e: e.affine_select(m01[:], onesb[:], [[1, 128]], ALU.is_gt, 0.0, base=0, channel_multiplier=-1))
    cop(lambda e: e.memset(blk64[:], 0.0))
    cop(lambda e: e.memset(blk64[0:64, 0:64], 1.0))
    cop(lambda e: e.memset(blk64[64:128, 64:128], 1.0))

    gain_i = [0]

    def load_gain(ap_row):
        i = 0
        r = P.R("gain", i)
        P.dma(gain[i][:], ap_row.partition_broadcast(128), w=[r])
        return gain[i], r

    stat_i = [0]

    def stat_col():
        i = stat_i[0] % 64
        stat_i[0] += 1
        return stat[:, i:i + 1], P.R("stat", i)

    def rstd_from_ss(ss, ss_r, n, eps_ap=None):
        rs, rs_r = stat_col()
        ea = eps_col if eps_ap is None else eps_ap
        P.op("act", lambda e: e.activation(rs, ss, AF.Ln, bias=ea[:], scale=1.0 / n), r=[ss_r, CR], w=[rs_r])
        P.op("act", lambda e: e.activation(rs, rs, AF.Exp, scale=-0.5), r=[rs_r], w=[rs_r])
        return rs, rs_r

    def rms_rstd(src_ap, src_res, n):
        ss, ss_r = stat_col()
        P.op("act", lambda e: e.activation(xn_junk[:, 0:n], src_ap, AF.Square, accum_out=ss),
             r=[src_res], w=[P.R("xn_junk"), ss_r])
        return rstd_from_ss(ss, ss_r, n)

    def norm_tile_to_xnT(h_ap, h_res, g_t, g_r, xn_t, xn_r, pbank, xnT_dst, xnT_res, evac_eng):
        rs, rs_r = rms_rstd(h_ap, h_res, D)
        P.op("dve", lambda e: e.scalar_tensor_tensor(xn_t[:], h_ap, rs, g_t[:], ALU.mult, ALU.mult),
             r=[h_res, rs_r, g_r], w=[xn_r])
        for k in range(8):
            P.op("pe", lambda e, k=k: e.transpose(psb(pbank, k * 128, (k + 1) * 128),
                                                  xn_t[:, k * 128:(k + 1) * 128], ident[:]),
                 r=[xn_r, CR], w=[psr[pbank]])
        src = psb(pbank).rearrange("p (k t) -> p k t", k=8)
        if evac_eng == "act":
            P.op("act", lambda e: e.copy(xnT_dst, src), r=[psr[pbank]], w=[xnT_res])
        else:
            P.op("dve", lambda e: e.tensor_copy(xnT_dst, src), r=[psr[pbank]], w=[xnT_res])

    def bmid(ap2, n):
        a = ap2.ap
        return bass.AP(ap2.tensor, ap2.offset, [list(a[0]), [0, n], list(a[1])])

    def mm(out_ap, lhsT, rhs, start, stop, r, w, skip=False):
        if skip:
            P.op("pe", lambda e: e.matmul(out_ap, lhsT, rhs, start=start, stop=stop, skip_group_check=True), r=r, w=w)
        else:
            P.op("pe", lambda e: e.matmul(out_ap, lhsT, rhs, start=start, stop=stop), r=r, w=w)

    def load_w(dst, src2d, name, kc=2):
        nk = src2d.shape[0] // 128
        for k0 in range(0, nk, kc):
            k1 = min(nk, k0 + kc)
            P.dma(dst[:, k0:k1, :], src2d[k0 * 128:k1 * 128, :].rearrange("(k p) n -> p k n", p=128),
                  r=[], w=[P.R(name, k0 // kc)])
        return lambda k: P.R(name, k // kc)

    def ffn_phase(l, which, h_src, h_dst):
        G = 1024
        NCH = G // 512
        TPG = G // 128
        m = ar.mark()
        xnT = ar.alloc("xnT", [128, 8, G], BF16)
        actT = ar.alloc("actT", [128, NFC, G], BF16)
        wout = ar.alloc("wout", [128, NFC, D], BF16)
        htile = [ar.alloc("htile", [128, D], F32) for _ in range(TPG)]
        xn = [ar.alloc("xn", [128, D], BF16) for _ in range(2)]
        NWB = 3
        wg = [ar.alloc("wg", [128, 8, 256], BF16) for _ in range(NWB)]
        wu = [ar.alloc("wu", [128, 8, 256], BF16) for _ in range(NWB)]
        sg = [ar.alloc("sg", [128, 512], BF16) for _ in range(2)]
        hout = [ar.alloc("hout", [128, D], F32) for _ in range(2)]
        w_in_b = wb[which + "_w_in"]
        w_out_b = wb[which + "_w_out"]
        g_t, g_r = load_gain(wd[which + "_norm"][l:l + 1, :])
        wout_r = load_w(wout, w_out_b[l], "wout")
        wbi = 0
        for g in range(S // G):
            for t in range(TPG):
                row0 = g * G + t * 128
                hr = P.R("htile", t)
                P.dma(htile[t][:], h_src[row0:row0 + 128, :], w=[hr], r=[P.R("hdram", row0)])
                i = t % 2
                norm_tile_to_xnT(htile[t][:], hr, g_t, g_r, xn[i], P.R("xn", i), 4 + i,
                                 xnT[:, :, t * 128:(t + 1) * 128], P.R("xnT", t // 4), "act" if t % 2 else "dve")
            for fb in range(NFC // 2):
                i = wbi % NWB
                wbi += 1
                c0 = fb * 256
                P.dma(wg[i][:], w_in_b[l, :, c0:c0 + 256].rearrange("(k p) f -> p k f", p=128),
                      w=[P.R("wg", i)])
                P.dma(wu[i][:], w_in_b[l, :, DFF + c0:DFF + c0 + 256].rearrange("(k p) f -> p k f", p=128),
                      w=[P.R("wu", i)])
                for fj in range(2):
                    f = fb * 2 + fj
                    for c in range(NCH):
                        u = (f * NCH + c) % 2
                        bg, bu = 2 * u, 2 * u + 1
                        for k in range(8):
                            mm(ps(bg), wg[i][:, k, fj * 128:(fj + 1) * 128], xnT[:, k, c * 512:(c + 1) * 512],
                               k == 0, k == 7, [P.R("wg", i), P.R("xnT", c)], [psr[bg]])
                        for k in range(8):
                            mm(ps(bu), wu[i][:, k, fj * 128:(fj + 1) * 128], xnT[:, k, c * 512:(c + 1) * 512],
                               k == 0, k == 7, [P.R("wu", i), P.R("xnT", c)], [psr[bu]])
                        P.op("act", lambda e, u=u, bg=bg: e.activation(sg[u][:], ps(bg), AF.Silu),
                             r=[psr[bg]], w=[P.R("sg", u)])
                        P.op("dve", lambda e, u=u, bu=bu, f=f, c=c: e.tensor_tensor(
                            actT[:, f, c * 512:(c + 1) * 512], ps(bu), sg[u][:], ALU.mult),
                            r=[psr[bu], P.R("sg", u)], w=[P.R("actT", f, c)])
            for t in range(TPG):
                row0 = g * G + t * 128
                c = t // 4
                o = t % 2
                for hf in range(2):
                    b = 4 + (t * 2 + hf) % 4
                    for f in range(NFC):
                        mm(ps(b), actT[:, f, t * 128:(t + 1) * 128], wout[:, f, hf * 512:(hf + 1) * 512],
                           f == 0, f == NFC - 1, [P.R("actT", f, c), wout_r(f)], [psr[b]])
                    P.op("dve", lambda e, t=t, hf=hf, b=b, o=o: e.scalar_tensor_tensor(
                        hout[o][:, hf * 512:(hf + 1) * 512], ps(b), 0.5, htile[t][:, hf * 512:(hf + 1) * 512],
                        ALU.mult, ALU.add),
                        r=[psr[b], P.R("htile", t)], w=[P.R("hout", o, hf)])
                P.dma(h_dst[row0:row0 + 128, :], hout[o][:], r=[P.R("hout", o, 0), P.R("hout", o, 1)],
                      w=[P.R("hdram", row0)])
        P.barrier()
        ar.release(m)

    def xattn_phase(l, h_src, h_dst):
        m = ar.mark()
        wq = ar.alloc("xwq", [128, 8, D], BF16)
        wkv = ar.alloc("xwkv", [128, 8, 2 * D], BF16)
        wo = ar.alloc("xwo", [128, 8, D], BF16)
        wq_r = load_w(wq, wb["xattn_w_q"][l], "xwq")
        wkv_r = load_w(wkv, wb["xattn_w_kv"][l], "xwkv")
        wo_r = load_w(wo, wb["xattn_w_o"][l], "xwo")
        mt = [ar.alloc("memt", [128, D], F32) for _ in range(2)]
        xn = [ar.alloc("xn", [128, D], BF16) for _ in range(2)]
        memnT = ar.alloc("memnT", [128, 8, MEM], BF16)
        kTx = ar.alloc("kTx", [128, 8, MEM], BF16)
        vx = ar.alloc("vx", [128, 2, D], BF16)
        htile = [ar.alloc("htile", [128, D], F32) for _ in range(4)]
        xnT = [ar.alloc("xnT", [128, 8, 512], BF16) for _ in range(2)]
        qT = [ar.alloc("qT", [128, 8, 512], BF16) for _ in range(2)]
        pe_t = [ar.alloc("pe_t", [128, 4, MEM], BF16) for _ in range(2)]
        pn = [ar.alloc("pn", [128, 4, MEM], BF16) for _ in range(2)]
        pT = [ar.alloc("pT", [128, 8, 512], BF16) for _ in range(2)]
        oT = [ar.alloc("oT", [128, 8, 512], BF16) for _ in range(2)]
        hout = [ar.alloc("hout", [128, D], F32) for _ in range(2)]
        sm = ar.alloc("sm", [128, 16, 12], F32)
        g_t, g_r = load_gain(wd["mem_norm"][l:l + 1, :])
        for mi in range(2):
            P.dma(mt[mi][:], mem_d[mi * 128:(mi + 1) * 128, :], w=[P.R("memt", mi)])
            norm_tile_to_xnT(mt[mi][:], P.R("memt", mi), g_t, g_r, xn[mi], P.R("xn", mi), 4 + mi,
                             memnT[:, :, mi * 128:(mi + 1) * 128], P.R("memnT", mi), "dve")
        mres = [P.R("memnT", 0), P.R("memnT", 1)]
        for hj in range(8):
            b = hj % 2
            for k in range(8):
                mm(ps(b, 0, MEM), wkv[:, k, hj * 128:(hj + 1) * 128], memnT[:, k, :], k == 0, k == 7,
                   [wkv_r(k)] + mres, [psr[b]])
            P.op("act", lambda e, hj=hj, b=b: e.mul(kTx[:, hj, :], ps(b, 0, MEM), float(XD) ** -0.5),
                 r=[psr[b]], w=[P.R("kTx")])
        for mi in range(2):
            for hf in range(2):
                b = 2 + hf
                for k in range(8):
                    mm(ps(b), memnT[:, k, mi * 128:(mi + 1) * 128], wkv[:, k, D + hf * 512:D + (hf + 1) * 512],
                       k == 0, k == 7, [wkv_r(k), mres[mi]], [psr[b]])
                P.op("dve", lambda e, mi=mi, hf=hf, b=b: e.tensor_copy(vx[:, mi, hf * 512:(hf + 1) * 512], ps(b)),
                     r=[psr[b]], w=[P.R("vx")])
        g_t, g_r = load_gain(wd["xattn_norm"][l:l + 1, :])
        for g in range(S // 512):
            gi = g % 2
            for t in range(4):
                row0 = g * 512 + t * 128
                hr = P.R("htile", t)
                P.dma(htile[t][:], h_src[row0:row0 + 128, :], w=[hr], r=[P.R("hdram", row0)])
                i = t % 2
                norm_tile_to_xnT(htile[t][:], hr, g_t, g_r, xn[i], P.R("xn", i), 4 + i,
                                 xnT[gi][:, :, t * 128:(t + 1) * 128], P.R("xnTx", gi), "act" if t % 2 else "dve")
            for hj in range(8):
                b = hj % 2
                for k in range(8):
                    mm(ps(b), wq[:, k, hj * 128:(hj + 1) * 128], xnT[gi][:, k, :], k == 0, k == 7,
                       [wq_r(k), P.R("xnTx", gi)], [psr[b]])
                if hj % 2:
                    P.op("act", lambda e, hj=hj, b=b, gi=gi: e.copy(qT[gi][:, hj, :], ps(b)),
                         r=[psr[b]], w=[P.R("qTx", gi)])
                else:
                    P.op("dve", lambda e, hj=hj, b=b, gi=gi: e.tensor_copy(qT[gi][:, hj, :], ps(b)),
                         r=[psr[b]], w=[P.R("qTx", gi)])
            for t in range(4):
                u = t % 2
                si = (g * 4 + t) % 16
                for h in range(4):
                    b = 2 + h // 2
                    c0 = (h % 2) * MEM
                    for j in range(2):
                        mm(ps(b, c0, c0 + MEM), qT[gi][:, h * 2 + j, t * 128:(t + 1) * 128], kTx[:, h * 2 + j, :],
                           j == 0, j == 1, [P.R("qTx", gi), P.R("kTx")], [psr[b]])
                sc = psall[:, 2 * 512:4 * 512].rearrange("p (h m) -> p h m", h=4)
                P.op("dve", lambda e, si=si, sc=sc: e.tensor_reduce(sm[:, si, 0:4], sc, AX.X, ALU.max, negate=True),
                     r=[psr[2], psr[3]], w=[P.R("sm", si)])
                for h in range(4):
                    b = 2 + h // 2
                    c0 = (h % 2) * MEM
                    P.op("act", lambda e, h=h, b=b, c0=c0, u=u, si=si: e.activation(
                        pe_t[u][:, h, :], ps(b, c0, c0 + MEM), AF.Exp, bias=sm[:, si, h:h + 1], scale=1.0,
                        accum_out=sm[:, si, 4 + h:5 + h]),
                        r=[psr[b], P.R("sm", si)], w=[P.R("pe_t", u, h), P.R("smz", si, h)])
                P.op("dve", lambda e, si=si: e.reciprocal(sm[:, si, 8:12], sm[:, si, 4:8]),
                     r=[P.R("smz", si, h) for h in range(4)], w=[P.R("smr", si)])
                for h in range(4):
                    P.op("dve", lambda e, h=h, u=u, si=si: e.tensor_scalar(
                        pn[u][:, h, :], pe_t[u][:, h, :], sm[:, si, 8 + h:9 + h], None, ALU.mult),
                        r=[P.R("pe_t", u, h), P.R("smr", si)], w=[P.R("pn", u)])
                tb = 6 + u
                for h in range(4):
                    for mc in range(2):
                        P.op("pe", lambda e, h=h, mc=mc, u=u, tb=tb: e.transpose(
                            psb(tb, (h * 2 + mc) * 128, (h * 2 + mc + 1) * 128),
                            pn[u][:, h, mc * 128:(mc + 1) * 128], ident[:]),
                            r=[P.R("pn", u), CR], w=[psr[tb]])
                srcT = psb(tb).rearrange("p (k t) -> p k t", k=8)
                if t % 2:
                    P.op("act", lambda e, t=t, gi=gi, srcT=srcT: e.copy(pT[gi][:, :, t * 128:(t + 1) * 128], srcT),
                         r=[psr[tb]], w=[P.R("pT", gi)])
                else:
                    P.op("dve", lambda e, t=t, gi=gi, srcT=srcT: e.tensor_copy(pT[gi][:, :, t * 128:(t + 1) * 128],
                                                                              srcT),
                         r=[psr[tb]], w=[P.R("pT", gi)])
            for hc in range(8):
                b = hc % 2
                h = hc // 2
                for mc in range(2):
                    mm(ps(b), vx[:, mc, hc * 128:(hc + 1) * 128], pT[gi][:, h * 2 + mc, :], mc == 0, mc == 1,
                       [P.R("vx"), P.R("pT", gi)], [psr[b]])
                if hc % 2:
                    P.op("act", lambda e, hc=hc, b=b, gi=gi: e.copy(oT[gi][:, hc, :], ps(b)),
                         r=[psr[b]], w=[P.R("oT", gi)])
                else:
                    P.op("dve", lambda e, hc=hc, b=b, gi=gi: e.tensor_copy(oT[gi][:, hc, :], ps(b)),
                         r=[psr[b]], w=[P.R("oT", gi)])
            for t in range(4):
                row0 = g * 512 + t * 128
                o = t % 2
                for hf in range(2):
                    b = 4 + (t * 2 + hf) % 2
                    for c in range(8):
                        mm(ps(b), oT[gi][:, c, t * 128:(t + 1) * 128], wo[:, c, hf * 512:(hf + 1) * 512],
                           c == 0, c == 7, [P.R("oT", gi), wo_r(c)], [psr[b]])
                    P.op("dve", lambda e, t=t, hf=hf, b=b, o=o: e.tensor_tensor(
                        hout[o][:, hf * 512:(hf + 1) * 512], ps(b), htile[t][:, hf * 512:(hf + 1) * 512], ALU.add),
                        r=[psr[b], P.R("htile", t)], w=[P.R("hout", o, hf)])
                P.dma(h_dst[row0:row0 + 128, :], hout[o][:], r=[P.R("hout", o, 0), P.R("hout", o, 1)],
                      w=[P.R("hdram", row0)])
        P.barrier()
        ar.release(m)

    def mixer_phase(l, h_src, h_dst):
        m0 = ar.mark()
        xnT = ar.alloc_hi("mxnT", [128, 8, S], BF16)
        w_in_b = wb["w_in"][l]
        mA = ar.mark()
        ht = [ar.alloc("mht", [128, D], F32) for _ in range(3)]
        xn = [ar.alloc("xn", [128, D], BF16) for _ in range(2)]
        g_t, g_r = load_gain(wd["mix_norm"][l:l + 1, :])
        for t in range(NT):
            i3 = t % 3
            hr = P.R("mht", i3)
            P.dma(ht[i3][:], h_src[t * 128:(t + 1) * 128, :], w=[hr], r=[P.R("hdram", t * 128)])
            i = t % 2
            norm_tile_to_xnT(ht[i3][:], hr, g_t, g_r, xn[i], P.R("xn", i), 4 + i,
                             xnT[:, :, t * 128:(t + 1) * 128], P.R("mxnT", t // 4), "act" if t % 2 else "dve")
        xr = lambda c: P.R("mxnT", c)
        allx = [xr(c) for c in range(8)]
        P.barrier()
        ar.release(mA)
        if do_sb:
            sb_part(l, xnT, xr, w_in_b)
        else:
            zero_mix(0)
        if do_gdn:
            gdn_part(l, xnT, xr, allx, w_in_b)
        else:
            zero_mix(4)
        ar.release(m0)
        ar.release_hi()
        outproj_part(l, h_src, h_dst)

    def zero_mix(c0):
        m = ar.mark()
        z = ar.alloc("zmix", [128, S], BF16)
        P.op("pool", lambda e: e.memset(z[:], 0.0), w=[P.R("zmix")])
        for c in range(c0, c0 + 4):
            P.dma(mixT_d[c], z[:], r=[P.R("zmix")], w=[P.R("mixT_d", c, g) for g in range(8)])
        P.barrier()
        ar.release(m)

    def sb_part(l, xnT, xr, w_in_b):
        m = ar.mark()
        wqk = ar.alloc("wqk", [128, 8, 1024], BF16)
        wv = ar.alloc("wv", [128, 8, 512], BF16)
        wqk_r = load_w(wqk, w_in_b[:, 0:1024], "wqk")
        wv_r = load_w(wv, w_in_b[:, 1024:1536], "wv")
        v_all = ar.alloc("v_all", [128, NT, 512], BF16)
        qT2 = ar.alloc("qT2", [128, S], BF16)
        kT2 = ar.alloc("kT2", [128, S], BF16)
        nkT2 = ar.alloc("nkT2", [128, S], BF16)
        Eb = [ar.alloc("Eb", [128, 2, 512], F32) for _ in range(2)]
        SP = [ar.alloc("SP", [128, 2, 512], BF16) for _ in range(3)]
        AT = [ar.alloc("AT", [128, 2, 512], BF16) for _ in range(2)]
        osb = ar.alloc("osb", [128, 512], F32)
        osq = ar.alloc("osq", [128, 512], BF16)
        rsd = ar.alloc("rsd", [128, 512], F32)
        sbo = [ar.alloc("sbo", [128, 512], BF16) for _ in range(2)]
        sbg = ar.alloc("sbg", [128, 4], F32)
        eps64 = ar.alloc("eps64", [128, 1], F32)
        P.op("pool", lambda e: e.memset(eps64[:], RMS_EPS), w=[P.R("eps64")])
        P.dma(sbg[:], wd["sb_out_norm"][l].rearrange("(j p) -> p j", p=128), w=[P.R("sbg")],
              allow_slow_non_contiguous=True)
        for t in range(NT):
            b = t % 2
            for k in range(8):
                mm(ps(b), xnT[:, k, t * 128:(t + 1) * 128], wv[:, k, :], k == 0, k == 7, [xr(t // 4), wv_r(k)], [psr[b]])
            if t % 2:
                P.op("act", lambda e, t=t, b=b: e.copy(v_all[:, t, :], ps(b)), r=[psr[b]], w=[P.R("v_all", t)])
            else:
                P.op("dve", lambda e, t=t, b=b: e.tensor_copy(v_all[:, t, :], ps(b)), r=[psr[b]], w=[P.R("v_all", t)])
        for j in range(cfg.get("sb_pairs", 4)):
            for c in range(8):
                cs = slice(c * 512, (c + 1) * 512)
                b = 0
                for k in range(8):
                    mm(ps(b), wqk[:, k, j * 128:(j + 1) * 128], xnT[:, k, cs], k == 0, k == 7, [wqk_r(k), xr(c)],
                       [psr[b]])
                if cfg.get("sb_proj", 7) & 1:
                    P.op("dve", lambda e, cs=cs, b=b: e.tensor_copy(qT2[:, cs], ps(b)), r=[psr[b]], w=[P.R("qT2", c)])
                b = 1
                for k in range(8):
                    mm(ps(b), wqk[:, k, 512 + j * 128:512 + (j + 1) * 128], xnT[:, k, cs], k == 0, k == 7,
                       [wqk_r(k), xr(c)], [psr[b]])
                if cfg.get("sb_proj", 7) & 2:
                    P.op("act", lambda e, cs=cs, b=b: e.mul(kT2[:, cs], ps(b), float(SBD) ** -0.5),
                         r=[psr[b]], w=[P.R("kT2", c)])
                if cfg.get("sb_proj", 7) & 4:
                    P.op("dve", lambda e, cs=cs, b=b: e.tensor_scalar(nkT2[:, cs], ps(b), -(float(SBD) ** -0.5), None,
                                                                     ALU.mult),
                         r=[psr[b]], w=[P.R("nkT2", c)])
            units = []
            for c in range(cfg.get("sb_chunks", 8)):
                for kb in range(4 * c + 3, -1, -1):
                    units.append((c, kb))
            NU = len(units)

            def geom(u):
                c, kb = units[u]
                i = kb - 4 * c
                col0 = 128 * i if i > 0 else 0
                return c, kb, col0, (i >= 0)

            def rq(c):
                return P.R("qT2", c)

            def rk(kb):
                return P.R("kT2", kb // 4)

            def rnk(kb):
                return P.R("nkT2", kb // 4)

            def do_Z(u):
                c, kb, col0, diag = geom(u)
                zb = 2 * (u % 2)
                for hd in range(2):
                    pp = slice(hd * 64, (hd + 1) * 64)
                    mm(ps(zb + hd, col0, 512), kT2[pp, kb * 128:(kb + 1) * 128],
                       qT2[pp, c * 512 + col0:(c + 1) * 512], True, True, [rk(kb), rq(c)], [psr[zb + hd]])

            def do_ESP(u):
                c, kb, col0, diag = geom(u)
                zb = 2 * (u % 2)
                eb = u % 2
                sp = u % 3
                zin = psall[:, zb * 512:(zb + 2) * 512].rearrange("p (h t) -> p h t", h=2)[:, :, col0:512]
                P.op("act", lambda e: e.activation(Eb[eb][:, :, col0:512], zin, AF.Exp),
                     r=[psr[zb], psr[zb + 1]], w=[P.R("Eb", eb)])
                P.op("act", lambda e: e.activation(SP[sp][:, :, col0:512], Eb[eb][:, :, col0:512], AF.Ln, bias=1.0),
                     r=[P.R("Eb", eb)], w=[P.R("SP", sp)])
                if diag:
                    for hd in range(2):
                        P.op("pool", lambda e, hd=hd: e.tensor_tensor(SP[sp][:, hd, col0:col0 + 128],
                                                                      SP[sp][:, hd, col0:col0 + 128], m01[:], ALU.mult),
                             r=[P.R("SP", sp), CR], w=[P.R("SP", sp)])

            def do_G1(u):
                c, kb, col0, diag = geom(u)
                sp = u % 3
                for hd in range(2):
                    pp = slice(hd * 64, (hd + 1) * 64)
                    mm(ps(4 + hd, col0, 512), ltri[:], SP[sp][:, hd, col0:512], False, True, [CR, P.R("SP", sp)],
                       [psr[4 + hd]], skip=True)
                    mm(ps(4 + hd, col0, 512), nkT2[pp, kb * 128:(kb + 1) * 128],
                       qT2[pp, c * 512 + col0:(c + 1) * 512], False, True, [rnk(kb), rq(c)], [psr[4 + hd]], skip=True)

            def do_AT(u):
                c, kb, col0, diag = geom(u)
                a = u % 2
                gin = psall[:, 4 * 512:6 * 512].rearrange("p (h t) -> p h t", h=2)[:, :, col0:512]
                P.op("act", lambda e: e.activation(AT[a][:, :, col0:512], gin, AF.Exp, scale=-1.0),
                     r=[psr[4], psr[5]], w=[P.R("AT", a)])
                if diag:
                    for hd in range(2):
                        P.op("pool", lambda e, hd=hd: e.tensor_tensor(AT[a][:, hd, col0:col0 + 128],
                                                                      AT[a][:, hd, col0:col0 + 128], m01[:], ALU.mult),
                             r=[P.R("AT", a), CR], w=[P.R("AT", a)])

            def do_G2AV(u):
                c, kb, col0, diag = geom(u)
                sp = u % 3
                a = u % 2
                ob = 6 + (c % 2)
                for hd in range(2):
                    pp = slice(hd * 64, (hd + 1) * 64)
                    mm(ps(4 + hd, col0, 512), utri[:], SP[sp][:, hd, col0:512], False, True, [CR, P.R("SP", sp)],
                       [psr[4 + hd]], skip=True)
                    mm(ps(4 + hd, col0, 512), kT2[pp, kb * 128:(kb + 1) * 128],
                       qT2[pp, c * 512 + col0:(c + 1) * 512], False, True, [rk(kb), rq(c)], [psr[4 + hd]], skip=True)
                for hd in range(2):
                    hcol = (2 * j + hd) * 64
                    mm(psall[hd * 64:(hd + 1) * 64, ob * 512 + col0:ob * 512 + 512],
                       v_all[:, kb, hcol:hcol + 64], AT[a][:, hd, col0:512], False, True,
                       [P.R("v_all", kb), P.R("AT", a)], [psr[ob]], skip=True)

            def chain_init(c):
                ob = 6 + (c % 2)
                for b in (4, 5, ob):
                    mm(ps(b), zerob[:], qT2[:, c * 512:(c + 1) * 512], True, True, [CR, rq(c)], [psr[b]], skip=True)

            def chain_fin(c):
                ob = 6 + (c % 2)
                o = c % 2
                P.op("dve", lambda e: e.tensor_copy(osb[:], ps(ob)), r=[psr[ob]], w=[P.R("osb")])
                P.op("act", lambda e: e.activation(osq[:], ps(ob), AF.Square), r=[psr[ob]], w=[P.R("osq")])
                mm(ps(ob), blk64[:], osq[:], True, True, [CR, P.R("osq")], [psr[ob]])
                P.op("act", lambda e: e.activation(rsd[:], ps(ob), AF.Ln, bias=eps64[:], scale=1.0 / SBD),
                     r=[psr[ob], P.R("eps64")], w=[P.R("rsd")])
                P.op("act", lambda e: e.activation(rsd[:], rsd[:], AF.Exp, scale=-0.5), r=[P.R("rsd")], w=[P.R("rsd")])
                P.op("dve", lambda e, j=j: e.scalar_tensor_tensor(sbo[o][:], osb[:], sbg[:, j:j + 1], rsd[:], ALU.mult,
                                                                  ALU.mult),
                     r=[P.R("osb"), P.R("sbg"), P.R("rsd")], w=[P.R("sbo", o)])
                P.dma(mixT_d[j, :, c * 512:(c + 1) * 512], sbo[o][:], r=[P.R("sbo", o)], w=[P.R("mixT_d", j, c)])

            stg = cfg.get("sb_stage", 9)
            if stg >= 1:
                do_Z(0)
                do_ESP(0)
                if NU > 1:
                    do_Z(1)
                    do_ESP(1)
            for u in range(NU):
                c, kb, col0, diag = geom(u)
                if kb == 4 * c + 3 and stg >= 2:
                    chain_init(c)
                if stg >= 2:
                    do_G1(u)
                    do_AT(u)
                if u + 2 < NU and stg >= 1:
                    do_Z(u + 2)
                    do_ESP(u + 2)
                if stg >= 3:
                    do_G2AV(u)
                if kb == 0 and stg >= 4:
                    chain_fin(c)
        P.barrier()
        ar.release(m)

    def gdn_part(l, xnT, xr, allx, w_in_b):
        m = ar.mark()
        qT = ar.alloc("gqT", [128, 4, S], BF16)
        kT = ar.alloc("gkT", [128, 4, S], BF16)
        vT = ar.alloc("gvT", [128, 4, S], BF16)
        beta = ar.alloc("g_beta", [128, 128], F32)
        gcs = ar.alloc("g_gc", [128, 128], F32)
        ebg = ar.alloc("g_ebg", [128, 128], F32)
        ekend = ar.alloc("g_ekend", [128, 128], F32)
        bgs = ar.alloc("g_bg", [128, 128], F32)
        egl = ar.alloc("g_egl", [128, 128], F32)
        gnrep = ar.alloc("g_gn", [128, 128], F32)
        eps128 = ar.alloc("eps128", [128, 1], F32)
        P.op("pool", lambda e: e.memset(eps128[:], L2_EPS), w=[P.R("eps128")])
        P.dma(gnrep[:], wd["gdn_out_norm"][l:l + 1, :].partition_broadcast(128), w=[P.R("g_gn")])
        m1 = ar.mark()
        cw = ar.alloc("cw", [128, 4, 12], F32)
        for jt in range(4):
            P.dma(cw[:, jt, :], wd["conv_w"][l, jt, :].rearrange("(g p) -> p g", p=128), w=[P.R("cw", jt)],
                  allow_slow_non_contiguous=True)
        cwr = [P.R("cw", jt) for jt in range(4)]
        wsl = [ar.alloc("wsl", [128, 8, 128], BF16) for _ in range(2)]
        wz = ar.alloc("wz", [128, 8, 512], BF16)
        wab = ar.alloc("wab", [128, 8, 8], BF16)
        wz_r = load_w(wz, w_in_b[:, OFF_GZ:OFF_GZ + 512], "wz")
        wab_r = load_w(wab, w_in_b[:, OFF_GA:OFF_GA + 8], "wab", kc=8)
        raw = [ar.alloc("raw", [128, 515], F32) for _ in range(2)]
        acc = ar.alloc("acc", [128, 512], F32)
        sil = ar.alloc("sil", [128, 512], F32)
        sqb = ar.alloc("sqb", [128, 512], BF16)
        rinv = ar.alloc("rinv", [128, 512], F32)
        zsb = [ar.alloc("zsb", [128, 512], F32) for _ in range(2)]
        dtb = ar.alloc("dtb", [128, 4], F32)
        nA = ar.alloc("nA", [128, 4], F32)
        t1 = ar.alloc("g_t1", [128, 128], F32)
        t2 = ar.alloc("g_t2", [128, 128], F32)
        t3 = ar.alloc("g_t3", [128, 128], F32)
        P.dma(dtb[:], wd["dt_bias"][l:l + 1, :].partition_broadcast(128), w=[P.R("dtb")])
        P.dma(nA[:], wd["a_log"][l:l + 1, :].partition_broadcast(128), w=[P.R("nA")])
        P.op("act", lambda e: e.activation(nA[:], nA[:], AF.Exp), r=[P.R("nA")], w=[P.R("nA")])
        P.op("dve", lambda e: e.tensor_scalar(nA[:], nA[:], -1.0, None, ALU.mult), r=[P.R("nA")], w=[P.R("nA")])
        wi = 0
        for hg in range(4):
            for X, (off, dstT) in enumerate(((OFF_GQ, qT), (OFF_GK, kT), (OFF_GV, vT))):
                ws = wsl[wi % 2]
                wsr = P.R("wsl", wi % 2)
                wi += 1
                P.dma(ws[:], w_in_b[:, off + hg * 128:off + (hg + 1) * 128].rearrange("(k p) n -> p k n", p=128),
                      w=[wsr])
                gidx = X * 4 + hg
                for c in range(8):
                    b = c % 2
                    rw = raw[c % 2]
                    rwr = P.R("raw", c % 2)
                    for k in range(8):
                        mm(ps(b), ws[:, k, :], xnT[:, k, c * 512:(c + 1) * 512], k == 0, k == 7, [wsr, xr(c)],
                           [psr[b]])
                    if c == 0:
                        P.op("pool", lambda e, rw=rw: e.memset(rw[:, 0:3], 0.0), w=[rwr])
                    else:
                        P.op("pool", lambda e, rw=rw, c=c: e.tensor_copy(rw[:, 0:3], raw[(c - 1) % 2][:, 512:515]),
                             r=[P.R("raw", (c - 1) % 2)], w=[rwr])
                    P.op("act", lambda e, rw=rw, b=b: e.copy(rw[:, 3:515], ps(b)), r=[psr[b]], w=[rwr])
                    P.op("dve", lambda e, rw=rw, gidx=gidx: e.tensor_scalar(acc[:], rw[:, 3:515], cw[:, 3, gidx:gidx + 1], None,
                                                                           ALU.mult),
                         r=[rwr] + cwr, w=[P.R("acc")])
                    for jj in (2, 1, 0):
                        P.op("dve", lambda e, rw=rw, gidx=gidx, jj=jj: e.scalar_tensor_tensor(
                            acc[:], rw[:, jj:jj + 512], cw[:, jj, gidx:gidx + 1], acc[:], ALU.mult, ALU.add),
                            r=[rwr, P.R("acc")] + cwr, w=[P.R("acc")])
                    dst = dstT[:, hg, c * 512:(c + 1) * 512]
                    dres = P.R("gqkv", X, hg, c)
                    if X == 2:
                        P.op("act", lambda e, dst=dst: e.activation(dst, acc[:], AF.Silu), r=[P.R("acc")], w=[dres])
                    else:
                        P.op("act", lambda e: e.activation(sil[:], acc[:], AF.Silu), r=[P.R("acc")], w=[P.R("sil")])
                        P.op("act", lambda e: e.activation(sqb[:], sil[:], AF.Square), r=[P.R("sil")], w=[P.R("sqb")])
                        b2 = 2 + c % 2
                        mm(ps(b2), onesb[:], sqb[:], True, True, [CR, P.R("sqb")], [psr[b2]])
                        P.op("act", lambda e, b2=b2: e.activation(rinv[:], ps(b2), AF.Ln, bias=eps128[:]),
                             r=[psr[b2], P.R("eps128")], w=[P.R("rinv")])
                        P.op("act", lambda e: e.activation(rinv[:], rinv[:], AF.Exp, scale=-0.5),
                             r=[P.R("rinv")], w=[P.R("rinv")])
                        sc = float(GD) ** -0.5 if X == 0 else 1.0
                        P.op("dve", lambda e, dst=dst, sc=sc: e.scalar_tensor_tensor(dst, sil[:], sc, rinv[:], ALU.mult,
                                                                                    ALU.mult),
                             r=[P.R("sil"), P.R("rinv")], w=[dres])
        for t in range(NT):
            b = 4 + t % 2
            for k in range(8):
                mm(ps(b), xnT[:, k, t * 128:(t + 1) * 128], wz[:, k, :], k == 0, k == 7, [xr(t // 4), wz_r(k)], [psr[b]])
            o = t % 2
            P.op("act", lambda e, b=b, o=o: e.activation(zsb[o][:], ps(b), AF.Silu), r=[psr[b]], w=[P.R("zsb", o)])
            P.dma(zs_d[t * 128:(t + 1) * 128, :], zsb[o][:], r=[P.R("zsb", o)], w=[P.R("zs_d", t)])
        for t in range(NT):
            for k in range(8):
                mm(ps(6, t * 8, t * 8 + 8), xnT[:, k, t * 128:(t + 1) * 128], wab[:, k, :], k == 0, k == 7,
                   [xr(t // 4), wab_r(k)], [psr[6]])
        ab = ps(6, 0, 256).rearrange("p (t j) -> p t j", j=8)
        t1v = t1[:].rearrange("p (t h) -> p t h", h=4)
        t2v = t2[:].rearrange("p (t h) -> p t h", h=4)
        P.op("dve", lambda e: e.tensor_tensor(t1v, ab[:, :, 0:4], bmid(dtb[:], NT),
                                              ALU.add), r=[psr[6], P.R("dtb")], w=[P.R("g_t1")])
        P.op("act", lambda e: e.activation(t1[:], t1[:], AF.Exp), r=[P.R("g_t1")], w=[P.R("g_t1")])
        P.op("act", lambda e: e.activation(t1[:], t1[:], AF.Ln, bias=1.0), r=[P.R("g_t1")], w=[P.R("g_t1")])
        P.op("dve", lambda e: e.tensor_tensor(t1v, t1v, bmid(nA[:], NT), ALU.mult),
             r=[P.R("g_t1"), P.R("nA")], w=[P.R("g_t1")])
        P.op("act", lambda e: e.activation(t2v, ab[:, :, 4:8], AF.Exp, scale=-1.0), r=[psr[6]], w=[P.R("g_t2")])
        P.op("act", lambda e: e.activation(t2[:], t2[:], AF.Ln, bias=1.0), r=[P.R("g_t2")], w=[P.R("g_t2")])
        P.op("act", lambda e: e.activation(beta[:], t2[:], AF.Exp, scale=-1.0), r=[P.R("g_t2")], w=[P.R("g_beta")])
        mm(ps(7, 0, 128), triuf[:], t1[:], True, True, [CR, P.R("g_t1")], [psr[7]])
        mm(ps(7, 128, 256), onesf[:], t1[:], True, True, [CR, P.R("g_t1")], [psr[7]])
        P.op("dve", lambda e: e.tensor_copy(gcs[:], ps(7, 0, 128)), r=[psr[7]], w=[P.R("g_gc")])
        P.op("dve", lambda e: e.tensor_tensor(bgs[:], ps(7, 0, 128), t2[:], ALU.subtract),
             r=[psr[7], P.R("g_t2")], w=[P.R("g_bg")])
        P.op("act", lambda e: e.activation(ebg[:], bgs[:], AF.Exp), r=[P.R("g_bg")], w=[P.R("g_ebg")])
        P.op("dve", lambda e: e.tensor_tensor(t3[:], ps(7, 128, 256), gcs[:], ALU.subtract),
             r=[psr[7], P.R("g_gc")], w=[P.R("g_t3")])
        P.op("act", lambda e: e.activation(ekend[:], t3[:], AF.Exp), r=[P.R("g_t3")], w=[P.R("g_ekend")])
        P.op("act", lambda e: e.activation(egl[:], ps(7, 128, 256), AF.Exp), r=[psr[7]], w=[P.R("g_egl")])
        P.barrier()
        ar.release(m1)
        ar.release_hi()
        H4 = range(4)
        Xg = [ar.alloc("Xg", [128, 256], F32) for _ in H4]
        arg = [ar.alloc("arg", [128, 256], F32) for _ in H4]
        Wm = [ar.alloc("Wm", [128, 256], F32) for _ in H4]
        egr = [ar.alloc("egr", [128, 128], F32) for _ in H4]
        QM = [ar.alloc("QM", [128, 256], BF16) for _ in H4]
        qdec = [ar.alloc("qdec", [128, 128], BF16) for _ in H4]
        kvt = [ar.alloc("kvt", [128, 128], BF16) for _ in H4]
        kvf = [ar.alloc("kvf", [128, 256], F32) for _ in H4]
        Mf = [ar.alloc("Mf", [128, 128], F32) for _ in H4]
        PQ = [[ar.alloc("PQ", [128, 256], F32) for _ in range(2)] for _ in H4]
        Y = [[ar.alloc("Y", [128, 128], F32) for _ in range(2)] for _ in H4]
        nwT = [ar.alloc("nwT", [128, 128], BF16) for _ in H4]
        vnew = [ar.alloc("vnew", [128, 128], BF16) for _ in H4]
        st = [ar.alloc("st", [128, 128], F32) for _ in H4]
        stb = [ar.alloc("stb", [128, 128], BF16) for _ in H4]
        got = [ar.alloc("got", [128, 128], F32) for _ in H4]
        gob = [ar.alloc("gob", [128, 128], BF16) for _ in H4]
        goT = [ar.alloc("goT", [128, 4, 512], BF16) for _ in range(2)]
        zt = [ar.alloc("zt", [128, 512], F32) for _ in range(2)]
        gsm = ar.alloc("gsm", [128, 8, 8], F32)
        for h in H4:
            P.op("pool", lambda e, h=h: e.memset(st[h][:], 0.0), w=[P.R("st", h)])
            P.op("pool", lambda e, h=h: e.memset(stb[h][:], 0.0), w=[P.R("stb", h)])

        def R_(n, h):
            return P.R(n, h)

        for T in range(NT):
            cs = slice(T * 128, (T + 1) * 128)
            cc = T // 4
            zi = T % 2
            P.dma(zt[zi][:], zs_d[cs, :], r=[P.R("zs_d", T)], w=[P.R("zt", zi)])
            bank = lambda h: h
            for h in H4:
                col = T * 4 + h
                P.op("dve", lambda e, h=h, col=col: e.tensor_scalar(Xg[h][:, 0:128], onesf[:], gcs[:, col:col + 1], None,
                                                                    ALU.mult),
                     r=[CR, P.R("g_gc")], w=[R_("Xg", h)])
                P.op("dve", lambda e, h=h, col=col: e.tensor_scalar(Xg[h][:, 128:256], onesf[:], bgs[:, col:col + 1],
                                                                    None, ALU.mult),
                     r=[CR, P.R("g_bg")], w=[R_("Xg", h)])
            for h in H4:
                mm(ps(h, 0, 128), Xg[h][:, 0:128], identf[:], True, True, [R_("Xg", h), CR], [psr[h]])
                mm(ps(h, 128, 256), Xg[h][:, 128:256], identf[:], True, True, [R_("Xg", h), CR], [psr[h]])
            for h in H4:
                col = T * 4 + h
                P.op("dve", lambda e, h=h, col=col: e.tensor_scalar(arg[h][:], ps(h, 0, 256), gcs[:, col:col + 1], 0.0,
                                                                    ALU.subtract, ALU.min),
                     r=[psr[h], P.R("g_gc")], w=[R_("arg", h)])
                P.op("act", lambda e, h=h: e.activation(egr[h][:], ps(h, 0, 128), AF.Exp), r=[psr[h]], w=[R_("egr", h)])
            for h in H4:
                P.op("act", lambda e, h=h: e.activation(Wm[h][:], arg[h][:], AF.Exp), r=[R_("arg", h)], w=[R_("Wm", h)])
                P.op("pool", lambda e, h=h: e.affine_select(Wm[h][:, 0:128], Wm[h][:, 0:128], [[1, 128]], ALU.is_ge, 0.0,
                                                            base=0, channel_multiplier=-1),
                     r=[R_("Wm", h)], w=[R_("Wm", h)])
                P.op("pool", lambda e, h=h: e.affine_select(Wm[h][:, 128:256], Wm[h][:, 128:256], [[1, 128]], ALU.is_gt,
                                                            0.0, base=0, channel_multiplier=-1),
                     r=[R_("Wm", h)], w=[R_("Wm", h)])
            for h in H4:
                rr = [P.R("gqkv", 0, h, cc), P.R("gqkv", 1, h, cc), P.R("gqkv", 2, h, cc)]
                mm(ps(4 + h, 0, 128), kT[:, h, cs], qT[:, h, cs], True, True, rr, [psr[4 + h]])
                mm(ps(4 + h, 128, 256), kT[:, h, cs], kT[:, h, cs], True, True, rr, [psr[4 + h]])
                P.op("pe", lambda e, h=h, cs=cs: e.transpose(psb(4 + h, 512, 640), kT[:, h, cs], ident[:]),
                     r=rr + [CR], w=[psr[4 + h]])
                P.op("pe", lambda e, h=h, cs=cs: e.transpose(psb(4 + h, 640, 768), vT[:, h, cs], ident[:]),
                     r=rr + [CR], w=[psr[4 + h]])
            for h in H4:
                col = T * 4 + h
                P.op("dve", lambda e, h=h: e.tensor_tensor(QM[h][:, 0:128], ps(4 + h, 0, 128), Wm[h][:, 0:128], ALU.mult),
                     r=[psr[4 + h], R_("Wm", h)], w=[R_("QM", h)])
                P.op("dve", lambda e, h=h: e.tensor_tensor(Mf[h][:], ps(4 + h, 128, 256), Wm[h][:, 128:256], ALU.mult),
                     r=[psr[4 + h], R_("Wm", h)], w=[R_("Mf", h)])
                P.op("dve", lambda e, h=h, cs=cs: e.tensor_tensor(qdec[h][:], qT[:, h, cs], egr[h][:], ALU.mult),
                     r=[P.R("gqkv", 0, h, cc), R_("egr", h)], w=[R_("qdec", h)])
                P.op("dve", lambda e, h=h, col=col: e.tensor_scalar(kvf[h][:, 0:128], psb(4 + h, 512, 640),
                                                                    ebg[:, col:col + 1], None, ALU.mult),
                     r=[psr[4 + h], P.R("g_ebg")], w=[R_("kvf", h)])
                P.op("dve", lambda e, h=h, col=col: e.tensor_scalar(kvt[h][:], psb(4 + h, 512, 640),
                                                                    ekend[:, col:col + 1], None, ALU.mult),
                     r=[psr[4 + h], P.R("g_ekend")], w=[R_("kvt", h)])
                P.op("dve", lambda e, h=h, col=col: e.tensor_scalar(kvf[h][:, 128:256], psb(4 + h, 640, 768),
                                                                    beta[:, col:col + 1], None, ALU.mult),
                     r=[psr[4 + h], P.R("g_beta")], w=[R_("kvf", h)])
            for h in H4:
                P.op("pe", lambda e, h=h: e.transpose(ps(h, 0, 128), Mf[h][:], identf[:]),
                     r=[R_("Mf", h), CR], w=[psr[h]])
            for h in H4:
                P.op("act", lambda e, h=h: e.copy(PQ[h][0][:, 128:256], ps(h, 0, 128)), r=[psr[h]], w=[R_("PQ0", h)])
                P.op("pool", lambda e, h=h: e.tensor_copy(PQ[h][0][:, 0:128], Mf[h][:]),
                     r=[R_("Mf", h)], w=[R_("PQ0", h)])
                P.op("dve", lambda e, h=h: e.tensor_tensor(Y[h][0][:], identf[:], Mf[h][:], ALU.subtract),
                     r=[CR, R_("Mf", h)], w=[R_("Y0", h)])
            for lv in range(6):
                a, bnx = lv % 2, (lv + 1) % 2
                pa, pb_ = "PQ%d" % a, "PQ%d" % bnx
                ya, yb = "Y%d" % a, "Y%d" % bnx
                for h in H4:
                    mm(ps(h, 0, 128), PQ[h][a][:, 128:256], PQ[h][a][:, 0:128], True, True, [R_(pa, h)], [psr[h]])
                    mm(ps(h, 128, 256), PQ[h][a][:, 0:128], PQ[h][a][:, 128:256], True, True, [R_(pa, h)], [psr[h]])
                for h in H4:
                    if h % 2:
                        P.op("act", lambda e, h=h, bnx=bnx: e.copy(PQ[h][bnx][:], ps(h, 0, 256)),
                             r=[psr[h]], w=[R_(pb_, h)])
                    else:
                        P.op("dve", lambda e, h=h, bnx=bnx: e.tensor_copy(PQ[h][bnx][:], ps(h, 0, 256)),
                             r=[psr[h]], w=[R_(pb_, h)])
                for h in H4:
                    mm(ps(h, 256, 384), PQ[h][bnx][:, 128:256], Y[h][a][:], True, True, [R_(pb_, h), R_(ya, h)],
                       [psr[h]])
                for h in H4:
                    P.op("dve", lambda e, h=h, a=a, bnx=bnx: e.tensor_tensor(Y[h][bnx][:], ps(h, 256, 384), Y[h][a][:],
                                                                             ALU.add),
                         r=[psr[h], R_(ya, h)], w=[R_(yb, h)])
            YF = [Y[h][0] for h in H4]
            yf = "Y0"
            for h in H4:
                mm(ps(h, 0, 128), kvf[h][:, 0:128], YF[h][:], True, True, [R_("kvf", h), R_(yf, h)], [psr[h]])
            for h in H4:
                P.op("act", lambda e, h=h: e.mul(nwT[h][:], ps(h, 0, 128), -1.0), r=[psr[h]], w=[R_("nwT", h)])
            for h in H4:
                mm(ps(h, 128, 256), YF[h][:], kvf[h][:, 128:256], True, False, [R_(yf, h), R_("kvf", h)], [psr[h]])
                mm(ps(h, 128, 256), nwT[h][:], stb[h][:], False, True, [R_("nwT", h), R_("stb", h)], [psr[h]])
            for h in H4:
                if h % 2:
                    P.op("act", lambda e, h=h: e.copy(vnew[h][:], ps(h, 128, 256)), r=[psr[h]], w=[R_("vnew", h)])
                else:
                    P.op("dve", lambda e, h=h: e.tensor_copy(vnew[h][:], ps(h, 128, 256)), r=[psr[h]], w=[R_("vnew", h)])
            for h in H4:
                mm(ps(4 + h, 256, 384), qdec[h][:], stb[h][:], True, False, [R_("qdec", h), R_("stb", h)], [psr[4 + h]])
                mm(ps(4 + h, 256, 384), QM[h][:, 0:128], vnew[h][:], False, True, [R_("QM", h), R_("vnew", h)],
                   [psr[4 + h]])
                mm(ps(4 + h, 384, 512), kvt[h][:], vnew[h][:], True, True, [R_("kvt", h), R_("vnew", h)],
                   [psr[4 + h]])
            for h in H4:
                col = T * 4 + h
                P.op("dve", lambda e, h=h, col=col: e.scalar_tensor_tensor(st[h][:], st[h][:], egl[:, col:col + 1],
                                                                           ps(4 + h, 384, 512), ALU.mult, ALU.add),
                     r=[R_("st", h), P.R("g_egl"), psr[4 + h]], w=[R_("st", h)])
                P.op("act", lambda e, h=h: e.copy(stb[h][:], st[h][:]), r=[R_("st", h)], w=[R_("stb", h)])
            si = T % 8
            for h in H4:
                P.op("act", lambda e, h=h, si=si: e.activation(xn_junk[:, 0:128], ps(4 + h, 256, 384), AF.Square,
                                                               accum_out=gsm[:, si, h:h + 1]),
                     r=[psr[4 + h]], w=[P.R("xn_junk"), P.R("gsm", si, h)])
            P.op("act", lambda e, si=si: e.activation(gsm[:, si, 4:8], gsm[:, si, 0:4], AF.Ln, bias=eps_col[:],
                                                      scale=1.0 / GD),
                 r=[P.R("gsm", si, h) for h in H4] + [CR], w=[P.R("gsmr", si)])
            P.op("act", lambda e, si=si: e.activation(gsm[:, si, 4:8], gsm[:, si, 4:8], AF.Exp, scale=-0.5),
                 r=[P.R("gsmr", si)], w=[P.R("gsmr", si)])
            for h in H4:
                P.op("dve", lambda e, h=h, si=si: e.scalar_tensor_tensor(got[h][:], ps(4 + h, 256, 384),
                                                                         gsm[:, si, 4 + h:5 + h], gnrep[:], ALU.mult,
                                                                         ALU.mult),
                     r=[psr[4 + h], P.R("gsmr", si), P.R("g_gn")], w=[R_("got", h)])
                P.op("pool", lambda e, h=h, zi=zi: e.tensor_tensor(gob[h][:], got[h][:], zt[zi][:, h * 128:(h + 1) * 128],
                                                                   ALU.mult),
                     r=[R_("got", h), P.R("zt", zi)], w=[R_("gob", h)])
            gi = (T // 4) % 2
            tq = T % 4
            for h in H4:
                P.op("pe", lambda e, h=h: e.transpose(psb(h, 768, 896), gob[h][:], ident[:]),
                     r=[R_("gob", h), CR], w=[psr[h]])
            for h in H4:
                if h % 2:
                    P.op("act", lambda e, h=h, gi=gi, tq=tq: e.copy(goT[gi][:, h, tq * 128:(tq + 1) * 128],
                                                                    psb(h, 768, 896)),
                         r=[psr[h]], w=[P.R("goT", gi, h)])
                else:
                    P.op("dve", lambda e, h=h, gi=gi, tq=tq: e.tensor_copy(goT[gi][:, h, tq * 128:(tq + 1) * 128],
                                                                           psb(h, 768, 896)),
                         r=[psr[h]], w=[P.R("goT", gi, h)])
            if tq == 3:
                c = T // 4
                for h in H4:
                    P.dma(mixT_d[4 + h, :, c * 512:(c + 1) * 512], goT[gi][:, h, :], r=[P.R("goT", gi, h)],
                          w=[P.R("mixT_d", 4 + h, c)])
        P.barrier()
        ar.release(m)

    def outproj_part(l, h_src, h_dst):
        m = ar.mark()
        wo = ar.alloc("mwo", [128, 8, D], BF16)
        wo_r = load_w(wo, wb["w_out"][l], "mwo")
        mx = [ar.alloc("mx", [128, 8, 512], BF16) for _ in range(2)]
        htile = [ar.alloc("htile", [128, D], F32) for _ in range(4)]
        hout = [ar.alloc("hout", [128, D], F32) for _ in range(2)]
        for g in range(8):
            gi = g % 2
            for c in range(8):
                P.dma(mx[gi][:, c, :], mixT_d[c, :, g * 512:(g + 1) * 512], r=[P.R("mixT_d", c, g)],
                      w=[P.R("mx", gi, c)])
            for t in range(4):
                T = g * 4 + t
                row0 = T * 128
                ti = T % 4
                P.dma(htile[ti][:], h_src[row0:row0 + 128, :], w=[P.R("htile", ti)], r=[P.R("hdram", row0)])
                o = T % 2
                for hf in range(2):
                    b = (T * 2 + hf) % 4
                    for c in range(8):
                        mm(ps(b), mx[gi][:, c, t * 128:(t + 1) * 128], wo[:, c, hf * 512:(hf + 1) * 512], c == 0, c == 7,
                           [P.R("mx", gi, c), wo_r(c)], [psr[b]])
                    P.op("dve", lambda e, ti=ti, hf=hf, b=b, o=o: e.tensor_tensor(
                        hout[o][:, hf * 512:(hf + 1) * 512], ps(b), htile[ti][:, hf * 512:(hf + 1) * 512], ALU.add),
                        r=[psr[b], P.R("htile", ti)], w=[P.R("hout", o, hf)])
                P.dma(h_dst[row0:row0 + 128, :], hout[o][:], r=[P.R("hout", o, 0), P.R("hout", o, 1)],
                      w=[P.R("hdram", row0)])
        P.barrier()
        ar.release(m)

    def final_phase(h_src):
        m = ar.mark()
        ht = [ar.alloc("fht", [128, D], F32) for _ in range(3)]
        ot = [ar.alloc("fot", [128, D], F32) for _ in range(3)]
        g_t, g_r = load_gain(wd["final_norm"][0:1, :])
        for t in range(NT):
            i = t % 3
            hr = P.R("fht", i)
            P.dma(ht[i][:], h_src[t * 128:(t + 1) * 128, :], w=[hr], r=[P.R("hdram", t * 128)])
            rs, rs_r = rms_rstd(ht[i][:], hr, D)
            P.op("dve", lambda e, i=i, rs=rs: e.scalar_tensor_tensor(ot[i][:], ht[i][:], rs, g_t[:], ALU.mult,
                                                                    ALU.mult),
                 r=[hr, rs_r, g_r], w=[P.R("fot", i)])
            P.dma(out_d[t * 128:(t + 1) * 128, :], ot[i][:], r=[P.R("fot", i)], w=[P.R("out", t)])
        P.barrier()
        ar.release(m)

    P.barrier()
    cur = x_d
    for l in range(n_layers):
        if "ffn1" in phases:
            ffn_phase(l, "ffn1", cur, hbuf)
            cur = hbuf
        if "mix" in phases:
            mixer_phase(l, cur, hbuf)
            cur = hbuf
        if "xattn" in phases:
            xattn_phase(l, cur, hbuf)
            cur = hbuf
        if "ffn2" in phases:
            ffn_phase(l, "ffn2", cur, hbuf)
            cur = hbuf
    final_phase(cur)
    P.finish()
    P.emit(stack)
    stack.close()
    global _LAST_PROG
    _LAST_PROG = P
    return nc


_NC_CACHE = {}
_LAST_PROG = None


def kernel(**inputs):
    cfg = inputs.pop("_cfg", None)
    key = repr(cfg)
    if key not in _NC_CACHE:
        _NC_CACHE[key] = build_nc(cfg)
    nc = _NC_CACHE[key]
    nl = (cfg or {}).get("layers", DEPTH)
    x = np.ascontiguousarray(inputs["x"], dtype=np.float32)
    mem = np.ascontiguousarray(inputs["mem"], dtype=np.float32)
    shared = {}
    for k, v in inputs.items():
        if k in ("x", "mem"):
            continue
        a = np.asarray(v, dtype=np.float32)
        if k == "final_norm":
            a = a.reshape(1, D)
        else:
            a = a[:nl]
        shared[k] = np.ascontiguousarray(a)
    in_maps = []
    for c in range(8):
        mp = dict(shared)
        mp["x"] = x[c]
        mp["mem"] = mem[c]
        in_maps.append(mp)
    res = run_bass_kernel_spmd(nc, in_maps, core_ids=list(range(8)))
    return np.stack([np.asarray(r["out"], dtype=np.float32) for r in res.results], axis=0)
```

```python
import numpy as np
import concourse.bass as bass
import concourse.mybir as mybir
from concourse.bass_utils import run_bass_kernel_spmd
from contextlib import ExitStack

F32 = mybir.dt.float32
BF16 = mybir.dt.bfloat16
ALU = mybir.AluOpType
AF = mybir.ActivationFunctionType
AX = mybir.AxisListType

D = 1024
S = 4096
DEPTH = 4
MEM = 256
DFF = 2816
NFC = DFF // 128
N_IN = 3592
SBH, SBD = 8, 64
GH, GD = 4, 128
XH, XD = 4, 256
OFF_SBQ, OFF_SBK, OFF_SBV = 0, 512, 1024
OFF_GQ, OFF_GK, OFF_GV = 1536, 2048, 2560
OFF_GZ, OFF_GA, OFF_GB = 3072, 3584, 3588
RMS_EPS = 1e-6
L2_EPS = 1e-6
NT = S // 128

ENGS = ("pe", "act", "dve", "pool", "sp")
NDMA = 16
NDMA_SP = 12


class Res:
    __slots__ = ("name", "lastw", "readers", "excl")

    def __init__(self, name):
        self.name = name
        self.lastw = None
        self.readers = {}
        self.excl = (name[0] in ("ps", "ps0", "ps1", "ps3", "ps4", "ps5"))


class Prog:
    def __init__(self, nc):
        self.nc = nc
        self.ops = {e: [] for e in ENGS}
        self.known = {e: {} for e in ENGS}
        self.dma_rr = 0
        self.dma_rr2 = 0
        self.dma_cnt = [0] * NDMA
        self.dma_last = [None] * NDMA
        self.res = {}

    def R(self, *key):
        r = self.res.get(key)
        if r is None:
            r = self.res[key] = Res(key)
        return r

    def _waits(self, eng, raw, other):
        out = []
        kn = self.known[eng]
        for tok, is_raw in [(t, True) for t in raw] + [(t, False) for t in other]:
            if tok is None:
                continue
            kind, key, val = tok
            if kind == "e" and key == eng:
                if eng in ("pe", "sp"):
                    continue
            k = (kind, key)
            if kn.get(k, -1) >= val:
                continue
            kn[k] = val
            out.append(tok)
            if kind == "e":
                self.ops[key][val][2] = True
        return out

    def _deps(self, r, w, eng=None):
        raw = [x.lastw for x in r]
        other = []
        for x in r:
            if x.excl:
                other.extend(t for k, t in x.readers.items() if k != eng)
        for x in w:
            other.append(x.lastw)
            other.extend(x.readers.values())
        return raw, other

    def _commit(self, tok, eng, r, w):
        for x in r:
            x.readers[eng] = tok
        for x in w:
            x.lastw = tok
            x.readers = {}

    def op(self, eng, fn, r=(), w=()):
        raw, other = self._deps(r, w, eng)
        waits = self._waits(eng, raw, other)
        idx = len(self.ops[eng])
        self.ops[eng].append([fn, waits, False, None])
        tok = ("e", eng, idx)
        self._commit(tok, eng, r, w)
        return tok

    def dma(self, out_ap, in_ap, r=(), w=(), q="sp", **kw):
        if q == "sp":
            slot = self.dma_rr
            self.dma_rr = (slot + 1) % NDMA_SP
        else:
            slot = NDMA_SP + self.dma_rr2
            self.dma_rr2 = (self.dma_rr2 + 1) % (NDMA - NDMA_SP)
        raw, other = self._deps(r, w)
        other = list(other) + [self.dma_last[slot]]
        waits = self._waits(q, raw, other)
        self.dma_cnt[slot] += 16
        tok = ("d", slot, self.dma_cnt[slot])
        self.dma_last[slot] = tok

        def fn(e, out_ap=out_ap, in_ap=in_ap, kw=kw):
            return e.dma_start(out=out_ap, in_=in_ap, **kw)

        self.ops[q].append([fn, waits, False, slot])
        for x in r:
            x.readers[("dma", slot)] = tok
        for x in w:
            x.lastw = tok
            x.readers = {}
        return tok

    def _last_toks(self):
        toks = []
        for e in ENGS:
            if e == "sp":
                continue
            for i in range(len(self.ops[e]) - 1, -1, -1):
                o = self.ops[e][i]
                if o[0] is not None and o[3] is None:
                    toks.append(("e", e, i))
                    break
        toks += [t for t in self.dma_last if t is not None]
        return toks

    def barrier(self):
        toks = self._last_toks()
        for e in ENGS:
            waits = self._waits(e, toks, [])
            if waits:
                self.ops[e].append([None, waits, False, None])

    def finish(self):
        toks = self._last_toks()
        waits = self._waits("sp", toks, [])
        self.ops["sp"].append([None, waits, False, None])

    def emit(self, stack):
        nc = self.nc
        esem = {e: stack.enter_context(nc.semaphore("s_" + e)) for e in ENGS if e != "sp"}
        dsem = [stack.enter_context(nc.semaphore("d%d" % i)) for i in range(NDMA)]
        val = {}
        for e in ENGS:
            if e == "sp":
                continue
            c = 0
            for i, o in enumerate(self.ops[e]):
                if o[2] and o[0] is not None and o[3] is None:
                    c += 1
                    val[(e, i)] = c
                elif o[2]:
                    raise RuntimeError("flagged a non-instruction op")

        def replay(name, eng):
            for i, (fn, waits, flag, slot) in enumerate(self.ops[name]):
                for kind, key, v in waits:
                    if kind == "e":
                        eng.wait_ge(esem[key], val[(key, v)])
                    else:
                        eng.wait_ge(dsem[key], v)
                if fn is None:
                    continue
                ins = fn(eng)
                if slot is not None:
                    ins.then_inc(dsem[slot], 16)
                elif flag:
                    ins.then_inc(esem[name], 1)

        block = stack.enter_context(nc.Block())

        @block.tensor
        def _(eng):
            replay("pe", eng)

        @block.scalar
        def _(eng):
            replay("act", eng)

        @block.vector
        def _(eng):
            replay("dve", eng)

        @block.gpsimd
        def _(eng):
            replay("pool", eng)

        @block.sync
        def _(eng):
            replay("sp", eng)


class Arena:
    def __init__(self, nc, base, limit):
        self.nc = nc
        self.top = base
        self.limit = limit
        self.hi = limit
        self.n = 0

    def alloc_hi(self, name, shape, dtype):
        esz = 4 if dtype == F32 else 2
        per = esz
        for s in shape[1:]:
            per *= s
        off = (self.hi - per) // 64 * 64
        if off < self.top:
            raise RuntimeError("SBUF arena overflow (hi) at %s" % name)
        self.hi = off
        self.n += 1
        return self.nc.alloc_sbuf_tensor_at("%s_%d" % (name, self.n), list(shape), dtype, offset=off)

    def release_hi(self):
        self.hi = self.limit

    def mark(self):
        return self.top

    def release(self, m):
        self.top = m

    def alloc(self, name, shape, dtype):
        esz = 4 if dtype == F32 else 2
        per = esz
        for s in shape[1:]:
            per *= s
        off = (self.top + 63) // 64 * 64
        if off + per > self.hi:
            raise RuntimeError("SBUF arena overflow at %s: %d + %d > %d" % (name, off, per, self.hi))
        self.top = off + per
        self.n += 1
        return self.nc.alloc_sbuf_tensor_at("%s_%d" % (name, self.n), list(shape), dtype, offset=off)


def build_nc(cfg=None):
    cfg = cfg or {}
    n_layers = cfg.get("layers", DEPTH)
    phases = cfg.get("phases", ("ffn1", "mix", "xattn", "ffn2"))
    do_sb = cfg.get("sb", True)
    do_gdn = cfg.get("gdn", True)
    nc = bass.Bass("TRN2", target_bir_lowering=False)
    stack = ExitStack()
    P = Prog(nc)
    L = n_layers

    def din(name, shape):
        return nc.dram_tensor(name, list(shape), F32, kind="ExternalInput").ap()

    x_d = din("x", [S, D])
    mem_d = din("mem", [MEM, D])
    wd = {}
    for name, shape in [("ffn1_norm", [L, D]), ("ffn1_w_in", [L, D, 2 * DFF]), ("ffn1_w_out", [L, DFF, D]),
                        ("mix_norm", [L, D]), ("w_in", [L, D, N_IN]), ("conv_w", [L, 4, 1536]),
                        ("a_log", [L, 4]), ("dt_bias", [L, 4]), ("sb_out_norm", [L, 512]),
                        ("gdn_out_norm", [L, 128]), ("w_out", [L, D, D]), ("xattn_norm", [L, D]),
                        ("mem_norm", [L, D]), ("xattn_w_q", [L, D, D]), ("xattn_w_kv", [L, D, 2 * D]),
                        ("xattn_w_o", [L, D, D]), ("ffn2_norm", [L, D]), ("ffn2_w_in", [L, D, 2 * DFF]),
                        ("ffn2_w_out", [L, DFF, D]), ("final_norm", [1, D])]:
        wd[name] = din(name, shape)
    out_d = nc.dram_tensor("out", [S, D], F32, kind="ExternalOutput").ap()
    hbuf = nc.dram_tensor("hbuf", [S, D], F32, kind="Internal").ap()
    mixT_d = nc.dram_tensor("mixT_d", [8, 128, S], BF16, kind="Internal").ap()
    zs_d = nc.dram_tensor("zs_d", [S, 512], F32, kind="Internal").ap()
    wb = {}
    for name in ("ffn1_w_in", "ffn1_w_out", "w_in", "w_out", "xattn_w_q", "xattn_w_kv", "xattn_w_o",
                 "ffn2_w_in", "ffn2_w_out"):
        wb[name] = nc.dram_tensor(name + "_bf", list(wd[name].shape), BF16, kind="Internal").ap()

    need = set()
    for ph in phases:
        need |= {"ffn1": {"ffn1_w_in", "ffn1_w_out"}, "ffn2": {"ffn2_w_in", "ffn2_w_out"},
                 "mix": {"w_in", "w_out"}, "xattn": {"xattn_w_q", "xattn_w_kv", "xattn_w_o"}}[ph]
    def cast_layer(l):
        for name in wb:
            if name not in need:
                continue
            rows = wd[name].shape[1]
            for r0 in range(0, rows, 256):
                r1 = min(rows, r0 + 256)
                P.dma(wb[name][l, r0:r1, :], wd[name][l, r0:r1, :], w=[P.R("wb", name, l)], q="pool",
                      max_dma_last_dim=8192)

    cast_layer(0)

    ar = Arena(nc, 16640, 229376 - 128)
    ident = ar.alloc("ident", [128, 128], BF16)
    identf = ar.alloc("identf", [128, 128], F32)
    onesf = ar.alloc("onesf", [128, 128], F32)
    onesb = ar.alloc("onesb", [128, 128], BF16)
    zerob = ar.alloc("zerob", [128, 128], BF16)
    triuf = ar.alloc("triuf", [128, 128], F32)
    ltri = ar.alloc("ltri", [128, 128], BF16)
    utri = ar.alloc("utri", [128, 128], BF16)
    m01 = ar.alloc("m01", [128, 128], BF16)
    blk64 = ar.alloc("blk64", [128, 128], BF16)
    gain = [ar.alloc("gain", [128, D], F32) for _ in range(1)]
    stat = ar.alloc("stat", [128, 64], F32)
    eps_col = ar.alloc("eps", [128, 1], F32)
    xn_junk = ar.alloc("xn_junk", [128, D], BF16)
    psall = stack.enter_context(nc.psum_tensor("psall", [128, 4096], F32))
    psbf_all = psall.bitcast(BF16)

    def ps(i, c0=0, c1=512):
        return psall[:, i * 512 + c0:i * 512 + c1]

    def psb(i, c0=0, c1=1024):
        return psbf_all[:, i * 1024 + c0:i * 1024 + c1]

    psr = [P.R("ps", i) for i in range(8)]
    CR = P.R("consts")

    def cop(fn):
        P.op("pool", fn, r=[CR], w=[CR])

    cop(lambda e: e.memset(identf[:], 0.0))
    cop(lambda e: e.affine_select(identf[:], identf[:], [[-1, 128]], ALU.not_equal, 1.0, base=0,
                                  channel_multiplier=1))
    cop(lambda e: e.tensor_copy(ident[:], identf[:]))
    cop(lambda e: e.memset(onesf[:], 1.0))
    cop(lambda e: e.memset(onesb[:], 1.0))
    cop(lambda e: e.memset(zerob[:], 0.0))
    cop(lambda e: e.memset(eps_col[:], RMS_EPS))
    cop(lambda e: e.affine_select(triuf[:], onesf[:], [[1, 128]], ALU.is_ge, 0.0, base=0, channel_multiplier=-1))
    cop(lambda e: e.affine_select(ltri[:], onesb[:], [[-1, 128]], ALU.is_ge, 0.0, base=0, channel_multiplier=1))
    cop(lambda e: e.affine_select(utri[:], onesb[:], [[1, 128]], ALU.is_gt, 0.0, base=0, channel_multiplier=-1))
    cop(lambda e: e.affine_select(m01[:], onesb[:], [[1, 128]], ALU.is_gt, 0.0, base=0, channel_multiplier=-1))
    cop(lambda e: e.memset(blk64[:], 0.0))
    cop(lambda e: e.memset(blk64[0:64, 0:64], 1.0))
    cop(lambda e: e.memset(blk64[64:128, 64:128], 1.0))

    gain_i = [0]

    def load_gain(ap_row):
        i = 0
        r = P.R("gain", i)
        P.dma(gain[i][:], ap_row.partition_broadcast(128), w=[r])
        return gain[i], r

    stat_i = [0]

    def stat_col():
        i = stat_i[0] % 64
        stat_i[0] += 1
        return stat[:, i:i + 1], P.R("stat", i)

    def rstd_from_ss(ss, ss_r, n, eps_ap=None):
        rs, rs_r = stat_col()
        ea = eps_col if eps_ap is None else eps_ap
        P.op("act", lambda e: e.activation(rs, ss, AF.Ln, bias=ea[:], scale=1.0 / n), r=[ss_r, CR], w=[rs_r])
        P.op("act", lambda e: e.activation(rs, rs, AF.Exp, scale=-0.5), r=[rs_r], w=[rs_r])
        return rs, rs_r

    def rms_rstd(src_ap, src_res, n):
        ss, ss_r = stat_col()
        P.op("act", lambda e: e.activation(xn_junk[:, 0:n], src_ap, AF.Square, accum_out=ss),
             r=[src_res], w=[P.R("xn_junk"), ss_r])
        return rstd_from_ss(ss, ss_r, n)

    def norm_tile_to_xnT(h_ap, h_res, g_t, g_r, xn_t, xn_r, pbank, xnT_dst, xnT_res, evac_eng):
        rs, rs_r = rms_rstd(h_ap, h_res, D)
        P.op("dve", lambda e: e.scalar_tensor_tensor(xn_t[:], h_ap, rs, g_t[:], ALU.mult, ALU.mult),
             r=[h_res, rs_r, g_r], w=[xn_r])
        for k in range(8):
            P.op("pe", lambda e, k=k: e.transpose(psb(pbank, k * 128, (k + 1) * 128),
                                                  xn_t[:, k * 128:(k + 1) * 128], ident[:]),
                 r=[xn_r, CR], w=[psr[pbank]])
        src = psb(pbank).rearrange("p (k t) -> p k t", k=8)
        if evac_eng == "act":
            P.op("act", lambda e: e.copy(xnT_dst, src), r=[psr[pbank]], w=[xnT_res])
        else:
            P.op("dve", lambda e: e.tensor_copy(xnT_dst, src), r=[psr[pbank]], w=[xnT_res])

    def bmid(ap2, n):
        a = ap2.ap
        return bass.AP(ap2.tensor, ap2.offset, [list(a[0]), [0, n], list(a[1])])

    def mm(out_ap, lhsT, rhs, start, stop, r, w, skip=False):
        if skip:
            P.op("pe", lambda e: e.matmul(out_ap, lhsT, rhs, start=start, stop=stop, skip_group_check=True), r=r, w=w)
        else:
            P.op("pe", lambda e: e.matmul(out_ap, lhsT, rhs, start=start, stop=stop), r=r, w=w)

    def load_w(dst, src2d, name, kc=2):
        nk = src2d.shape[0] // 128
        for k0 in range(0, nk, kc):
            k1 = min(nk, k0 + kc)
            P.dma(dst[:, k0:k1, :], src2d[k0 * 128:k1 * 128, :].rearrange("(k p) n -> p k n", p=128),
                  r=[], w=[P.R(name, k0 // kc)])
        return lambda k: P.R(name, k // kc)

    def ffn_phase(l, which, h_src, h_dst):
        G = 1024
        NCH = G // 512
        TPG = G // 128
        m = ar.mark()
        xnT = ar.alloc("xnT", [128, 8, G], BF16)
        actT = ar.alloc("actT", [128, NFC, G], BF16)
        wout = ar.alloc("wout", [128, NFC, D], BF16)
        htile = [ar.alloc("htile", [128, D], F32) for _ in range(TPG)]
        xn = [ar.alloc("xn", [128, D], BF16) for _ in range(2)]
        NWB = 3
        wg = [ar.alloc("wg", [128, 8, 256], BF16) for _ in range(NWB)]
        wu = [ar.alloc("wu", [128, 8, 256], BF16) for _ in range(NWB)]
        sg = [ar.alloc("sg", [128, 512], BF16) for _ in range(2)]
        hout = [ar.alloc("hout", [128, D], F32) for _ in range(2)]
        w_in_b = wb[which + "_w_in"]
        w_out_b = wb[which + "_w_out"]
        g_t, g_r = load_gain(wd[which + "_norm"][l:l + 1, :])
        wout_r = load_w(wout, w_out_b[l], "wout")
        wbi = 0
        for g in range(S // G):
            for t in range(TPG):
                row0 = g * G + t * 128
                hr = P.R("htile", t)
                P.dma(htile[t][:], h_src[row0:row0 + 128, :], w=[hr], r=[P.R("hdram", row0)])
                i = t % 2
                norm_tile_to_xnT(htile[t][:], hr, g_t, g_r, xn[i], P.R("xn", i), 4 + i,
                                 xnT[:, :, t * 128:(t + 1) * 128], P.R("xnT", t // 4), "act" if t % 2 else "dve")
            for fb in range(NFC // 2):
                i = wbi % NWB
                wbi += 1
                c0 = fb * 256
                P.dma(wg[i][:], w_in_b[l, :, c0:c0 + 256].rearrange("(k p) f -> p k f", p=128),
                      w=[P.R("wg", i)])
                P.dma(wu[i][:], w_in_b[l, :, DFF + c0:DFF + c0 + 256].rearrange("(k p) f -> p k f", p=128),
                      w=[P.R("wu", i)])
                for fj in range(2):
                    f = fb * 2 + fj
                    for c in range(NCH):
                        u = (f * NCH + c) % 2
                        bg, bu = 2 * u, 2 * u + 1
                        for k in range(8):
                            mm(ps(bg), wg[i][:, k, fj * 128:(fj + 1) * 128], xnT[:, k, c * 512:(c + 1) * 512],
                               k == 0, k == 7, [P.R("wg", i), P.R("xnT", c)], [psr[bg]])
                        for k in range(8):
                            mm(ps(bu), wu[i][:, k, fj * 128:(fj + 1) * 128], xnT[:, k, c * 512:(c + 1) * 512],
                               k == 0, k == 7, [P.R("wu", i), P.R("xnT", c)], [psr[bu]])
                        P.op("act", lambda e, u=u, bg=bg: e.activation(sg[u][:], ps(bg), AF.Silu),
                             r=[psr[bg]], w=[P.R("sg", u)])
                        P.op("dve", lambda e, u=u, bu=bu, f=f, c=c: e.tensor_tensor(
                            actT[:, f, c * 512:(c + 1) * 512], ps(bu), sg[u][:], ALU.mult),
                            r=[psr[bu], P.R("sg", u)], w=[P.R("actT", f, c)])
            for t in range(TPG):
                row0 = g * G + t * 128
                c = t // 4
                o = t % 2
                for hf in range(2):
                    b = 4 + (t * 2 + hf) % 4
                    for f in range(NFC):
                        mm(ps(b), actT[:, f, t * 128:(t + 1) * 128], wout[:, f, hf * 512:(hf + 1) * 512],
                           f == 0, f == NFC - 1, [P.R("actT", f, c), wout_r(f)], [psr[b]])
                    P.op("dve", lambda e, t=t, hf=hf, b=b, o=o: e.scalar_tensor_tensor(
                        hout[o][:, hf * 512:(hf + 1) * 512], ps(b), 0.5, htile[t][:, hf * 512:(hf + 1) * 512],
                        ALU.mult, ALU.add),
                        r=[psr[b], P.R("htile", t)], w=[P.R("hout", o, hf)])
                P.dma(h_dst[row0:row0 + 128, :], hout[o][:], r=[P.R("hout", o, 0), P.R("hout", o, 1)],
                      w=[P.R("hdram", row0)])
        P.barrier()
        ar.release(m)

    def xattn_phase(l, h_src, h_dst):
        m = ar.mark()
        wq = ar.alloc("xwq", [128, 8, D], BF16)
        wkv = ar.alloc("xwkv", [128, 8, 2 * D], BF16)
        wo = ar.alloc("xwo", [128, 8, D], BF16)
        wq_r = load_w(wq, wb["xattn_w_q"][l], "xwq")
        wkv_r = load_w(wkv, wb["xattn_w_kv"][l], "xwkv")
        wo_r = load_w(wo, wb["xattn_w_o"][l], "xwo")
        mt = [ar.alloc("memt", [128, D], F32) for _ in range(2)]
        xn = [ar.alloc("xn", [128, D], BF16) for _ in range(2)]
        memnT = ar.alloc("memnT", [128, 8, MEM], BF16)
        kTx = ar.alloc("kTx", [128, 8, MEM], BF16)
        vx = ar.alloc("vx", [128, 2, D], BF16)
        htile = [ar.alloc("htile", [128, D], F32) for _ in range(4)]
        xnT = [ar.alloc("xnT", [128, 8, 512], BF16) for _ in range(2)]
        qT = [ar.alloc("qT", [128, 8, 512], BF16) for _ in range(2)]
        pe_t = [ar.alloc("pe_t", [128, 4, MEM], BF16) for _ in range(2)]
        pn = [ar.alloc("pn", [128, 4, MEM], BF16) for _ in range(2)]
        pT = [ar.alloc("pT", [128, 8, 512], BF16) for _ in range(2)]
        oT = [ar.alloc("oT", [128, 8, 512], BF16) for _ in range(2)]
        hout = [ar.alloc("hout", [128, D], F32) for _ in range(2)]
        sm = ar.alloc("sm", [128, 16, 12], F32)
        g_t, g_r = load_gain(wd["mem_norm"][l:l + 1, :])
        for mi in range(2):
            P.dma(mt[mi][:], mem_d[mi * 128:(mi + 1) * 128, :], w=[P.R("memt", mi)])
            norm_tile_to_xnT(mt[mi][:], P.R("memt", mi), g_t, g_r, xn[mi], P.R("xn", mi), 4 + mi,
                             memnT[:, :, mi * 128:(mi + 1) * 128], P.R("memnT", mi), "dve")
        mres = [P.R("memnT", 0), P.R("memnT", 1)]
        for hj in range(8):
            b = hj % 2
            for k in range(8):
                mm(ps(b, 0, MEM), wkv[:, k, hj * 128:(hj + 1) * 128], memnT[:, k, :], k == 0, k == 7,
                   [wkv_r(k)] + mres, [psr[b]])
            P.op("act", lambda e, hj=hj, b=b: e.mul(kTx[:, hj, :], ps(b, 0, MEM), float(XD) ** -0.5),
                 r=[psr[b]], w=[P.R("kTx")])
        for mi in range(2):
            for hf in range(2):
                b = 2 + hf
                for k in range(8):
                    mm(ps(b), memnT[:, k, mi * 128:(mi + 1) * 128], wkv[:, k, D + hf * 512:D + (hf + 1) * 512],
                       k == 0, k == 7, [wkv_r(k), mres[mi]], [psr[b]])
                P.op("dve", lambda e, mi=mi, hf=hf, b=b: e.tensor_copy(vx[:, mi, hf * 512:(hf + 1) * 512], ps(b)),
                     r=[psr[b]], w=[P.R("vx")])
        g_t, g_r = load_gain(wd["xattn_norm"][l:l + 1, :])
        for g in range(S // 512):
            gi = g % 2
            for t in range(4):
                row0 = g * 512 + t * 128
                hr = P.R("htile", t)
                P.dma(htile[t][:], h_src[row0:row0 + 128, :], w=[hr], r=[P.R("hdram", row0)])
                i = t % 2
                norm_tile_to_xnT(htile[t][:], hr, g_t, g_r, xn[i], P.R("xn", i), 4 + i,
                                 xnT[gi][:, :, t * 128:(t + 1) * 128], P.R("xnTx", gi), "act" if t % 2 else "dve")
            for hj in range(8):
                b = hj % 2
                for k in range(8):
                    mm(ps(b), wq[:, k, hj * 128:(hj + 1) * 128], xnT[gi][:, k, :], k == 0, k == 7,
                       [wq_r(k), P.R("xnTx", gi)], [psr[b]])
                if hj % 2:
                    P.op("act", lambda e, hj=hj, b=b, gi=gi: e.copy(qT[gi][:, hj, :], ps(b)),
                         r=[psr[b]], w=[P.R("qTx", gi)])
                else:
                    P.op("dve", lambda e, hj=hj, b=b, gi=gi: e.tensor_copy(qT[gi][:, hj, :], ps(b)),
                         r=[psr[b]], w=[P.R("qTx", gi)])
            for t in range(4):
                u = t % 2
                si = (g * 4 + t) % 16
                for h in range(4):
                    b = 2 + h // 2
                    c0 = (h % 2) * MEM
                    for j in range(2):
                        mm(ps(b, c0, c0 + MEM), qT[gi][:, h * 2 + j, t * 128:(t + 1) * 128], kTx[:, h * 2 + j, :],
                           j == 0, j == 1, [P.R("qTx", gi), P.R("kTx")], [psr[b]])
                sc = psall[:, 2 * 512:4 * 512].rearrange("p (h m) -> p h m", h=4)
                P.op("dve", lambda e, si=si, sc=sc: e.tensor_reduce(sm[:, si, 0:4], sc, AX.X, ALU.max, negate=True),
                     r=[psr[2], psr[3]], w=[P.R("sm", si)])
                for h in range(4):
                    b = 2 + h // 2
                    c0 = (h % 2) * MEM
                    P.op("act", lambda e, h=h, b=b, c0=c0, u=u, si=si: e.activation(
                        pe_t[u][:, h, :], ps(b, c0, c0 + MEM), AF.Exp, bias=sm[:, si, h:h + 1], scale=1.0,
                        accum_out=sm[:, si, 4 + h:5 + h]),
                        r=[psr[b], P.R("sm", si)], w=[P.R("pe_t", u, h), P.R("smz", si, h)])
                P.op("dve", lambda e, si=si: e.reciprocal(sm[:, si, 8:12], sm[:, si, 4:8]),
                     r=[P.R("smz", si, h) for h in range(4)], w=[P.R("smr", si)])
                for h in range(4):
                    P.op("dve", lambda e, h=h, u=u, si=si: e.tensor_scalar(
                        pn[u][:, h, :], pe_t[u][:, h, :], sm[:, si, 8 + h:9 + h], None, ALU.mult),
                        r=[P.R("pe_t", u, h), P.R("smr", si)], w=[P.R("pn", u)])
                tb = 6 + u
                for h in range(4):
                    for mc in range(2):
                        P.op("pe", lambda e, h=h, mc=mc, u=u, tb=tb: e.transpose(
                            psb(tb, (h * 2 + mc) * 128, (h * 2 + mc + 1) * 128),
                            pn[u][:, h, mc * 128:(mc + 1) * 128], ident[:]),
                            r=[P.R("pn", u), CR], w=[psr[tb]])
                srcT = psb(tb).rearrange("p (k t) -> p k t", k=8)
                if t % 2:
                    P.op("act", lambda e, t=t, gi=gi, srcT=srcT: e.copy(pT[gi][:, :, t * 128:(t + 1) * 128], srcT),
                         r=[psr[tb]], w=[P.R("pT", gi)])
                else:
                    P.op("dve", lambda e, t=t, gi=gi, srcT=srcT: e.tensor_copy(pT[gi][:, :, t * 128:(t + 1) * 128],
                                                                              srcT),
                         r=[psr[tb]], w=[P.R("pT", gi)])
            for hc in range(8):
                b = hc % 2
                h = hc // 2
                for mc in range(2):
                    mm(ps(b), vx[:, mc, hc * 128:(hc + 1) * 128], pT[gi][:, h * 2 + mc, :], mc == 0, mc == 1,
                       [P.R("vx"), P.R("pT", gi)], [psr[b]])
                if hc % 2:
                    P.op("act", lambda e, hc=hc, b=b, gi=gi: e.copy(oT[gi][:, hc, :], ps(b)),
                         r=[psr[b]], w=[P.R("oT", gi)])
                else:
                    P.op("dve", lambda e, hc=hc, b=b, gi=gi: e.tensor_copy(oT[gi][:, hc, :], ps(b)),
                         r=[psr[b]], w=[P.R("oT", gi)])
            for t in range(4):
                row0 = g * 512 + t * 128
                o = t % 2
                for hf in range(2):
                    b = 4 + (t * 2 + hf) % 2
                    for c in range(8):
                        mm(ps(b), oT[gi][:, c, t * 128:(t + 1) * 128], wo[:, c, hf * 512:(hf + 1) * 512],
                           c == 0, c == 7, [P.R("oT", gi), wo_r(c)], [psr[b]])
                    P.op("dve", lambda e, t=t, hf=hf, b=b, o=o: e.tensor_tensor(
                        hout[o][:, hf * 512:(hf + 1) * 512], ps(b), htile[t][:, hf * 512:(hf + 1) * 512], ALU.add),
                        r=[psr[b], P.R("htile", t)], w=[P.R("hout", o, hf)])
                P.dma(h_dst[row0:row0 + 128, :], hout[o][:], r=[P.R("hout", o, 0), P.R("hout", o, 1)],
                      w=[P.R("hdram", row0)])
        P.barrier()
        ar.release(m)

    def mixer_phase(l, h_src, h_dst):
        m0 = ar.mark()
        xnT = ar.alloc_hi("mxnT", [128, 8, S], BF16)
        w_in_b = wb["w_in"][l]
        mA = ar.mark()
        ht = [ar.alloc("mht", [128, D], F32) for _ in range(3)]
        xn = [ar.alloc("xn", [128, D], BF16) for _ in range(2)]
        g_t, g_r = load_gain(wd["mix_norm"][l:l + 1, :])
        for t in range(NT):
            i3 = t % 3
            hr = P.R("mht", i3)
            P.dma(ht[i3][:], h_src[t * 128:(t + 1) * 128, :], w=[hr], r=[P.R("hdram", t * 128)])
            i = t % 2
            norm_tile_to_xnT(ht[i3][:], hr, g_t, g_r, xn[i], P.R("xn", i), 4 + i,
                             xnT[:, :, t * 128:(t + 1) * 128], P.R("mxnT", t // 4), "act" if t % 2 else "dve")
        xr = lambda c: P.R("mxnT", c)
        allx = [xr(c) for c in range(8)]
        P.barrier()
        ar.release(mA)
        if do_sb:
            sb_part(l, xnT, xr, w_in_b)
        else:
            zero_mix(0)
        if do_gdn:
            gdn_part(l, xnT, xr, allx, w_in_b)
        else:
            zero_mix(4)
        ar.release(m0)
        ar.release_hi()
        outproj_part(l, h_src, h_dst)

    def zero_mix(c0):
        m = ar.mark()
        z = ar.alloc("zmix", [128, S], BF16)
        P.op("pool", lambda e: e.memset(z[:], 0.0), w=[P.R("zmix")])
        for c in range(c0, c0 + 4):
            P.dma(mixT_d[c], z[:], r=[P.R("zmix")], w=[P.R("mixT_d", c, g) for g in range(8)])
        P.barrier()
        ar.release(m)

    def sb_part(l, xnT, xr, w_in_b):
        m = ar.mark()
        wqk = ar.alloc("wqk", [128, 8, 1024], BF16)
        wv = ar.alloc("wv", [128, 8, 512], BF16)
        wqk_r = load_w(wqk, w_in_b[:, 0:1024], "wqk")
        wv_r = load_w(wv, w_in_b[:, 1024:1536], "wv")
        v_all = ar.alloc("v_all", [128, NT, 512], BF16)
        qT2 = ar.alloc("qT2", [128, S], BF16)
        kT2 = ar.alloc("kT2", [128, S], BF16)
        Eb = [ar.alloc("Eb", [128, 2, 512], F32) for _ in range(3)]
        SP = [ar.alloc("SP", [128, 2, 512], BF16) for _ in range(3)]
        AT = [ar.alloc("AT", [128, 2, 512], BF16) for _ in range(2)]
        osb = ar.alloc("osb", [128, 512], F32)
        osq = ar.alloc("osq", [128, 512], BF16)
        rsd = ar.alloc("rsd", [128, 512], F32)
        sbo = [ar.alloc("sbo", [128, 512], BF16) for _ in range(2)]
        sbg = ar.alloc("sbg", [128, 4], F32)
        eps64 = ar.alloc("eps64", [128, 1], F32)
        P.op("pool", lambda e: e.memset(eps64[:], RMS_EPS), w=[P.R("eps64")])
        P.dma(sbg[:], wd["sb_out_norm"][l].rearrange("(j p) -> p j", p=128), w=[P.R("sbg")],
              allow_slow_non_contiguous=True)
        for t in range(NT):
            b = t % 2
            for k in range(8):
                mm(ps(b), xnT[:, k, t * 128:(t + 1) * 128], wv[:, k, :], k == 0, k == 7, [xr(t // 4), wv_r(k)], [psr[b]])
            if t % 2:
                P.op("act", lambda e, t=t, b=b: e.copy(v_all[:, t, :], ps(b)), r=[psr[b]], w=[P.R("v_all", t)])
            else:
                P.op("dve", lambda e, t=t, b=b: e.tensor_copy(v_all[:, t, :], ps(b)), r=[psr[b]], w=[P.R("v_all", t)])
        for j in range(cfg.get("sb_pairs", 4)):
            for c in range(8):
                cs = slice(c * 512, (c + 1) * 512)
                b = 0
                for k in range(8):
                    mm(ps(b), wqk[:, k, j * 128:(j + 1) * 128], xnT[:, k, cs], k == 0, k == 7, [wqk_r(k), xr(c)],
                       [psr[b]])
                if cfg.get("sb_proj", 7) & 1:
                    P.op("dve", lambda e, cs=cs, b=b: e.tensor_copy(qT2[:, cs], ps(b)), r=[psr[b]], w=[P.R("qT2", c)])
                b = 1
                for k in range(8):
                    mm(ps(b), wqk[:, k, 512 + j * 128:512 + (j + 1) * 128], xnT[:, k, cs], k == 0, k == 7,
                       [wqk_r(k), xr(c)], [psr[b]])
                if cfg.get("sb_proj", 7) & 2:
                    P.op("act", lambda e, cs=cs, b=b: e.mul(kT2[:, cs], ps(b), float(SBD) ** -0.5),
                         r=[psr[b]], w=[P.R("kT2", c)])
            units = []
            for c in range(cfg.get("sb_chunks", 8)):
                for kb in range(4 * c + 3, -1, -1):
                    units.append((c, kb))
            NU = len(units)

            def geom(u):
                c, kb = units[u]
                i = kb - 4 * c
                col0 = 128 * i if i > 0 else 0
                return c, kb, col0, (i >= 0)

            def rq(c):
                return P.R("qT2", c)

            def rk(kb):
                return P.R("kT2", kb // 4)

            def rnk(kb):
                return P.R("nkT2", kb // 4)

            def do_Z(u):
                c, kb, col0, diag = geom(u)
                zb = 2 * (u % 2)
                for hd in range(2):
                    pp = slice(hd * 64, (hd + 1) * 64)
                    mm(ps(zb + hd, col0, 512), kT2[pp, kb * 128:(kb + 1) * 128],
                       qT2[pp, c * 512 + col0:(c + 1) * 512], True, True, [rk(kb), rq(c)], [psr[zb + hd]])

            def do_ESP(u):
                c, kb, col0, diag = geom(u)
                zb = 2 * (u % 2)
                eb = u % 3
                sp = u % 3
                zin = psall[:, zb * 512:(zb + 2) * 512].rearrange("p (h t) -> p h t", h=2)[:, :, col0:512]
                P.op("act", lambda e: e.activation(Eb[eb][:, :, col0:512], zin, AF.Exp),
                     r=[psr[zb], psr[zb + 1]], w=[P.R("Eb", eb)])
                P.op("act", lambda e: e.activation(SP[sp][:, :, col0:512], Eb[eb][:, :, col0:512], AF.Ln, bias=1.0),
                     r=[P.R("Eb", eb)], w=[P.R("SP", sp)])
                if diag:
                    for hd in range(2):
                        P.op("pool", lambda e, hd=hd: e.tensor_tensor(SP[sp][:, hd, col0:col0 + 128],
                                                                      SP[sp][:, hd, col0:col0 + 128], m01[:], ALU.mult),
                             r=[P.R("SP", sp), CR], w=[P.R("SP", sp)])

            def do_G1(u):
                c, kb, col0, diag = geom(u)
                sp = u % 3
                for hd in range(2):
                    pp = slice(hd * 64, (hd + 1) * 64)
                    mm(ps(4 + hd, col0, 512), ltri[:], SP[sp][:, hd, col0:512], False, True, [CR, P.R("SP", sp)],
                       [psr[4 + hd]], skip=True)

            def do_AT(u):
                c, kb, col0, diag = geom(u)
                a = u % 2
                gin = psall[:, 4 * 512:6 * 512].rearrange("p (h t) -> p h t", h=2)[:, :, col0:512]
                P.op("act", lambda e: e.activation(AT[a][:, :, col0:512], gin, AF.Exp, scale=-1.0),
                     r=[psr[4], psr[5]], w=[P.R("AT", a)])
                eb = u % 3
                P.op("dve", lambda e: e.tensor_tensor(AT[a][:, :, col0:512], Eb[eb][:, :, col0:512],
                                                      AT[a][:, :, col0:512], ALU.mult),
                     r=[P.R("Eb", eb), P.R("AT", a)], w=[P.R("AT", a)])
                if diag:
                    for hd in range(2):
                        P.op("pool", lambda e, hd=hd: e.tensor_tensor(AT[a][:, hd, col0:col0 + 128],
                                                                      AT[a][:, hd, col0:col0 + 128], m01[:], ALU.mult),
                             r=[P.R("AT", a), CR], w=[P.R("AT", a)])

            def do_G2(u):
                c, kb, col0, diag = geom(u)
                sp = u % 3
                for hd in range(2):
                    mm(ps(4 + hd, col0, 512), utri[:], SP[sp][:, hd, col0:512], False, True, [CR, P.R("SP", sp)],
                       [psr[4 + hd]], skip=True)

            def do_AV(u):
                c, kb, col0, diag = geom(u)
                a = u % 2
                ob = 6 + (c % 2)
                for hd in range(2):
                    hcol = (2 * j + hd) * 64
                    mm(psall[hd * 64:(hd + 1) * 64, ob * 512 + col0:ob * 512 + 512],
                       v_all[:, kb, hcol:hcol + 64], AT[a][:, hd, col0:512], False, True,
                       [P.R("v_all", kb), P.R("AT", a)], [psr[ob]], skip=True)

            def chain_init(c):
                ob = 6 + (c % 2)
                for b in (4, 5, ob):
                    mm(ps(b), zerob[:], qT2[:, c * 512:(c + 1) * 512], True, True, [CR, rq(c)], [psr[b]], skip=True)

            def chain_fin(c):
                ob = 6 + (c % 2)
                o = c % 2
                P.op("dve", lambda e: e.tensor_copy(osb[:], ps(ob)), r=[psr[ob]], w=[P.R("osb")])
                P.op("act", lambda e: e.activation(osq[:], ps(ob), AF.Square), r=[psr[ob]], w=[P.R("osq")])
                mm(ps(ob), blk64[:], osq[:], True, True, [CR, P.R("osq")], [psr[ob]])
                P.op("act", lambda e: e.activation(rsd[:], ps(ob), AF.Ln, bias=eps64[:], scale=1.0 / SBD),
                     r=[psr[ob], P.R("eps64")], w=[P.R("rsd")])
                P.op("act", lambda e: e.activation(rsd[:], rsd[:], AF.Exp, scale=-0.5), r=[P.R("rsd")], w=[P.R("rsd")])
                P.op("dve", lambda e, j=j: e.scalar_tensor_tensor(sbo[o][:], osb[:], sbg[:, j:j + 1], rsd[:], ALU.mult,
                                                                  ALU.mult),
                     r=[P.R("osb"), P.R("sbg"), P.R("rsd")], w=[P.R("sbo", o)])
                P.dma(mixT_d[j, :, c * 512:(c + 1) * 512], sbo[o][:], r=[P.R("sbo", o)], w=[P.R("mixT_d", j, c)])

            stg = cfg.get("sb_stage", 9)
            if stg >= 1:
                do_Z(0)
                do_ESP(0)
                if NU > 1:
                    do_Z(1)
                    do_ESP(1)
            for u in range(NU):
                c, kb, col0, diag = geom(u)
                if kb == 4 * c + 3 and stg >= 2:
                    chain_init(c)
                if stg >= 2:
                    do_G1(u)
                    do_AT(u)
                if u + 2 < NU and stg >= 1:
                    do_Z(u + 2)
                    do_ESP(u + 2)
                if stg >= 3:
                    do_G2(u)
                    if kb != 4 * c + 3:
                        do_AV(u - 1)
                    if kb == 0:
                        do_AV(u)
                if kb == 0 and stg >= 4:
                    chain_fin(c)
        P.barrier()
        ar.release(m)

    def gdn_part(l, xnT, xr, allx, w_in_b):
        m = ar.mark()
        qT = ar.alloc("gqT", [128, 4, S], BF16)
        kT = ar.alloc("gkT", [128, 4, S], BF16)
        vT = ar.alloc("gvT", [128, 4, S], BF16)
        beta = ar.alloc("g_beta", [128, 128], F32)
        gcs = ar.alloc("g_gc", [128, 128], F32)
        ebg = ar.alloc("g_ebg", [128, 128], F32)
        ekend = ar.alloc("g_ekend", [128, 128], F32)
        bgs = ar.alloc("g_bg", [128, 128], F32)
        egl = ar.alloc("g_egl", [128, 128], F32)
        gnrep = ar.alloc("g_gn", [128, 128], F32)
        eps128 = ar.alloc("eps128", [128, 1], F32)
        P.op("pool", lambda e: e.memset(eps128[:], L2_EPS), w=[P.R("eps128")])
        P.dma(gnrep[:], wd["gdn_out_norm"][l:l + 1, :].partition_broadcast(128), w=[P.R("g_gn")])
        m1 = ar.mark()
        cw = ar.alloc("cw", [128, 4, 12], F32)
        for jt in range(4):
            P.dma(cw[:, jt, :], wd["conv_w"][l, jt, :].rearrange("(g p) -> p g", p=128), w=[P.R("cw", jt)],
                  allow_slow_non_contiguous=True)
        cwr = [P.R("cw", jt) for jt in range(4)]
        wsl = [ar.alloc("wsl", [128, 8, 128], BF16) for _ in range(2)]
        wz = ar.alloc("wz", [128, 8, 512], BF16)
        wab = ar.alloc("wab", [128, 8, 8], BF16)
        wz_r = load_w(wz, w_in_b[:, OFF_GZ:OFF_GZ + 512], "wz")
        wab_r = load_w(wab, w_in_b[:, OFF_GA:OFF_GA + 8], "wab", kc=8)
        raw = [ar.alloc("raw", [128, 515], F32) for _ in range(2)]
        acc = ar.alloc("acc", [128, 512], F32)
        sil = ar.alloc("sil", [128, 512], F32)
        sqb = ar.alloc("sqb", [128, 512], BF16)
        rinv = ar.alloc("rinv", [128, 512], F32)
        zsb = [ar.alloc("zsb", [128, 512], F32) for _ in range(2)]
        dtb = ar.alloc("dtb", [128, 4], F32)
        nA = ar.alloc("nA", [128, 4], F32)
        t1 = ar.alloc("g_t1", [128, 128], F32)
        t2 = ar.alloc("g_t2", [128, 128], F32)
        t3 = ar.alloc("g_t3", [128, 128], F32)
        P.dma(dtb[:], wd["dt_bias"][l:l + 1, :].partition_broadcast(128), w=[P.R("dtb")])
        P.dma(nA[:], wd["a_log"][l:l + 1, :].partition_broadcast(128), w=[P.R("nA")])
        P.op("act", lambda e: e.activation(nA[:], nA[:], AF.Exp), r=[P.R("nA")], w=[P.R("nA")])
        P.op("dve", lambda e: e.tensor_scalar(nA[:], nA[:], -1.0, None, ALU.mult), r=[P.R("nA")], w=[P.R("nA")])
        wi = 0
        for hg in range(4):
            for X, (off, dstT) in enumerate(((OFF_GQ, qT), (OFF_GK, kT), (OFF_GV, vT))):
                ws = wsl[wi % 2]
                wsr = P.R("wsl", wi % 2)
                wi += 1
                P.dma(ws[:], w_in_b[:, off + hg * 128:off + (hg + 1) * 128].rearrange("(k p) n -> p k n", p=128),
                      w=[wsr])
                gidx = X * 4 + hg
                for c in range(8):
                    b = c % 2
                    rw = raw[c % 2]
                    rwr = P.R("raw", c % 2)
                    for k in range(8):
                        mm(ps(b), ws[:, k, :], xnT[:, k, c * 512:(c + 1) * 512], k == 0, k == 7, [wsr, xr(c)],
                           [psr[b]])
                    if c == 0:
                        P.op("pool", lambda e, rw=rw: e.memset(rw[:, 0:3], 0.0), w=[rwr])
                    else:
                        P.op("pool", lambda e, rw=rw, c=c: e.tensor_copy(rw[:, 0:3], raw[(c - 1) % 2][:, 512:515]),
                             r=[P.R("raw", (c - 1) % 2)], w=[rwr])
                    P.op("act", lambda e, rw=rw, b=b: e.copy(rw[:, 3:515], ps(b)), r=[psr[b]], w=[rwr])
                    P.op("dve", lambda e, rw=rw, gidx=gidx: e.tensor_scalar(acc[:], rw[:, 3:515], cw[:, 3, gidx:gidx + 1], None,
                                                                           ALU.mult),
                         r=[rwr] + cwr, w=[P.R("acc")])
                    for jj in (2, 1, 0):
                        P.op("dve", lambda e, rw=rw, gidx=gidx, jj=jj: e.scalar_tensor_tensor(
                            acc[:], rw[:, jj:jj + 512], cw[:, jj, gidx:gidx + 1], acc[:], ALU.mult, ALU.add),
                            r=[rwr, P.R("acc")] + cwr, w=[P.R("acc")])
                    dst = dstT[:, hg, c * 512:(c + 1) * 512]
                    dres = P.R("gqkv", X, hg, c)
                    if X == 2:
                        P.op("act", lambda e, dst=dst: e.activation(dst, acc[:], AF.Silu), r=[P.R("acc")], w=[dres])
                    else:
                        P.op("act", lambda e: e.activation(sil[:], acc[:], AF.Silu), r=[P.R("acc")], w=[P.R("sil")])
                        P.op("act", lambda e: e.activation(sqb[:], sil[:], AF.Square), r=[P.R("sil")], w=[P.R("sqb")])
                        b2 = 2 + c % 2
                        mm(ps(b2), onesb[:], sqb[:], True, True, [CR, P.R("sqb")], [psr[b2]])
                        P.op("act", lambda e, b2=b2: e.activation(rinv[:], ps(b2), AF.Ln, bias=eps128[:]),
                             r=[psr[b2], P.R("eps128")], w=[P.R("rinv")])
                        P.op("act", lambda e: e.activation(rinv[:], rinv[:], AF.Exp, scale=-0.5),
                             r=[P.R("rinv")], w=[P.R("rinv")])
                        sc = float(GD) ** -0.5 if X == 0 else 1.0
                        P.op("dve", lambda e, dst=dst, sc=sc: e.scalar_tensor_tensor(dst, sil[:], sc, rinv[:], ALU.mult,
                                                                                    ALU.mult),
                             r=[P.R("sil"), P.R("rinv")], w=[dres])
        for t in range(NT):
            b = 4 + t % 2
            for k in range(8):
                mm(ps(b), xnT[:, k, t * 128:(t + 1) * 128], wz[:, k, :], k == 0, k == 7, [xr(t // 4), wz_r(k)], [psr[b]])
            o = t % 2
            P.op("act", lambda e, b=b, o=o: e.activation(zsb[o][:], ps(b), AF.Silu), r=[psr[b]], w=[P.R("zsb", o)])
            P.dma(zs_d[t * 128:(t + 1) * 128, :], zsb[o][:], r=[P.R("zsb", o)], w=[P.R("zs_d", t)])
        for t in range(NT):
            for k in range(8):
                mm(ps(6, t * 8, t * 8 + 8), xnT[:, k, t * 128:(t + 1) * 128], wab[:, k, :], k == 0, k == 7,
                   [xr(t // 4), wab_r(k)], [psr[6]])
        ab = ps(6, 0, 256).rearrange("p (t j) -> p t j", j=8)
        t1v = t1[:].rearrange("p (t h) -> p t h", h=4)
        t2v = t2[:].rearrange("p (t h) -> p t h", h=4)
        P.op("dve", lambda e: e.tensor_tensor(t1v, ab[:, :, 0:4], bmid(dtb[:], NT),
                                              ALU.add), r=[psr[6], P.R("dtb")], w=[P.R("g_t1")])
        P.op("act", lambda e: e.activation(t1[:], t1[:], AF.Exp), r=[P.R("g_t1")], w=[P.R("g_t1")])
        P.op("act", lambda e: e.activation(t1[:], t1[:], AF.Ln, bias=1.0), r=[P.R("g_t1")], w=[P.R("g_t1")])
        P.op("dve", lambda e: e.tensor_tensor(t1v, t1v, bmid(nA[:], NT), ALU.mult),
             r=[P.R("g_t1"), P.R("nA")], w=[P.R("g_t1")])
        P.op("act", lambda e: e.activation(t2v, ab[:, :, 4:8], AF.Exp, scale=-1.0), r=[psr[6]], w=[P.R("g_t2")])
        P.op("act", lambda e: e.activation(t2[:], t2[:], AF.Ln, bias=1.0), r=[P.R("g_t2")], w=[P.R("g_t2")])
        P.op("act", lambda e: e.activation(beta[:], t2[:], AF.Exp, scale=-1.0), r=[P.R("g_t2")], w=[P.R("g_beta")])
        mm(ps(7, 0, 128), triuf[:], t1[:], True, True, [CR, P.R("g_t1")], [psr[7]])
        mm(ps(7, 128, 256), onesf[:], t1[:], True, True, [CR, P.R("g_t1")], [psr[7]])
        P.op("dve", lambda e: e.tensor_copy(gcs[:], ps(7, 0, 128)), r=[psr[7]], w=[P.R("g_gc")])
        P.op("dve", lambda e: e.tensor_tensor(bgs[:], ps(7, 0, 128), t2[:], ALU.subtract),
             r=[psr[7], P.R("g_t2")], w=[P.R("g_bg")])
        P.op("act", lambda e: e.activation(ebg[:], bgs[:], AF.Exp), r=[P.R("g_bg")], w=[P.R("g_ebg")])
        P.op("dve", lambda e: e.tensor_tensor(t3[:], ps(7, 128, 256), gcs[:], ALU.subtract),
             r=[psr[7], P.R("g_gc")], w=[P.R("g_t3")])
        P.op("act", lambda e: e.activation(ekend[:], t3[:], AF.Exp), r=[P.R("g_t3")], w=[P.R("g_ekend")])
        P.op("act", lambda e: e.activation(egl[:], ps(7, 128, 256), AF.Exp), r=[psr[7]], w=[P.R("g_egl")])
        P.barrier()
        ar.release(m1)
        ar.release_hi()
        H4 = range(4)
        Xg = [ar.alloc("Xg", [128, 256], F32) for _ in H4]
        arg = [ar.alloc("arg", [128, 256], F32) for _ in H4]
        Wm = [ar.alloc("Wm", [128, 256], F32) for _ in H4]
        egr = [ar.alloc("egr", [128, 128], F32) for _ in H4]
        QM = [ar.alloc("QM", [128, 256], BF16) for _ in H4]
        qdec = [ar.alloc("qdec", [128, 128], BF16) for _ in H4]
        kvt = [ar.alloc("kvt", [128, 128], BF16) for _ in H4]
        kvf = [ar.alloc("kvf", [128, 256], F32) for _ in H4]
        Mf = [ar.alloc("Mf", [128, 128], F32) for _ in H4]
        PQ = [[ar.alloc("PQ", [128, 256], F32) for _ in range(2)] for _ in H4]
        Y = [[ar.alloc("Y", [128, 128], F32) for _ in range(2)] for _ in H4]
        nwT = [ar.alloc("nwT", [128, 128], BF16) for _ in H4]
        vnew = [ar.alloc("vnew", [128, 128], BF16) for _ in H4]
        st = [ar.alloc("st", [128, 128], F32) for _ in H4]
        stb = [ar.alloc("stb", [128, 128], BF16) for _ in H4]
        got = [ar.alloc("got", [128, 128], F32) for _ in H4]
        gob = [ar.alloc("gob", [128, 128], BF16) for _ in H4]
        goT = [ar.alloc("goT", [128, 4, 512], BF16) for _ in range(2)]
        zt = [ar.alloc("zt", [128, 512], F32) for _ in range(2)]
        gsm = ar.alloc("gsm", [128, 8, 8], F32)
        for h in H4:
            P.op("pool", lambda e, h=h: e.memset(st[h][:], 0.0), w=[P.R("st", h)])
            P.op("pool", lambda e, h=h: e.memset(stb[h][:], 0.0), w=[P.R("stb", h)])

        def R_(n, h):
            return P.R(n, h)

        for T in range(NT):
            cs = slice(T * 128, (T + 1) * 128)
            cc = T // 4
            zi = T % 2
            P.dma(zt[zi][:], zs_d[cs, :], r=[P.R("zs_d", T)], w=[P.R("zt", zi)])
            bank = lambda h: h
            for h in H4:
                col = T * 4 + h
                P.op("dve", lambda e, h=h, col=col: e.tensor_scalar(Xg[h][:, 0:128], onesf[:], gcs[:, col:col + 1], None,
                                                                    ALU.mult),
                     r=[CR, P.R("g_gc")], w=[R_("Xg", h)])
                P.op("dve", lambda e, h=h, col=col: e.tensor_scalar(Xg[h][:, 128:256], onesf[:], bgs[:, col:col + 1],
                                                                    None, ALU.mult),
                     r=[CR, P.R("g_bg")], w=[R_("Xg", h)])
            for h in H4:
                mm(ps(h, 0, 128), Xg[h][:, 0:128], identf[:], True, True, [R_("Xg", h), CR], [psr[h]])
                mm(ps(h, 128, 256), Xg[h][:, 128:256], identf[:], True, True, [R_("Xg", h), CR], [psr[h]])
            for h in H4:
                col = T * 4 + h
                P.op("dve", lambda e, h=h, col=col: e.tensor_scalar(arg[h][:], ps(h, 0, 256), gcs[:, col:col + 1], 0.0,
                                                                    ALU.subtract, ALU.min),
                     r=[psr[h], P.R("g_gc")], w=[R_("arg", h)])
                P.op("act", lambda e, h=h: e.activation(egr[h][:], ps(h, 0, 128), AF.Exp), r=[psr[h]], w=[R_("egr", h)])
            for h in H4:
                P.op("act", lambda e, h=h: e.activation(Wm[h][:], arg[h][:], AF.Exp), r=[R_("arg", h)], w=[R_("Wm", h)])
                P.op("pool", lambda e, h=h: e.affine_select(Wm[h][:, 0:128], Wm[h][:, 0:128], [[1, 128]], ALU.is_ge, 0.0,
                                                            base=0, channel_multiplier=-1),
                     r=[R_("Wm", h)], w=[R_("Wm", h)])
                P.op("pool", lambda e, h=h: e.affine_select(Wm[h][:, 128:256], Wm[h][:, 128:256], [[1, 128]], ALU.is_gt,
                                                            0.0, base=0, channel_multiplier=-1),
                     r=[R_("Wm", h)], w=[R_("Wm", h)])
            for h in H4:
                rr = [P.R("gqkv", 0, h, cc), P.R("gqkv", 1, h, cc), P.R("gqkv", 2, h, cc)]
                mm(ps(4 + h, 0, 128), kT[:, h, cs], qT[:, h, cs], True, True, rr, [psr[4 + h]])
                mm(ps(4 + h, 128, 256), kT[:, h, cs], kT[:, h, cs], True, True, rr, [psr[4 + h]])
                P.op("pe", lambda e, h=h, cs=cs: e.transpose(psb(4 + h, 512, 640), kT[:, h, cs], ident[:]),
                     r=rr + [CR], w=[psr[4 + h]])
                P.op("pe", lambda e, h=h, cs=cs: e.transpose(psb(4 + h, 640, 768), vT[:, h, cs], ident[:]),
                     r=rr + [CR], w=[psr[4 + h]])
            for h in H4:
                col = T * 4 + h
                P.op("dve", lambda e, h=h: e.tensor_tensor(QM[h][:, 0:128], ps(4 + h, 0, 128), Wm[h][:, 0:128], ALU.mult),
                     r=[psr[4 + h], R_("Wm", h)], w=[R_("QM", h)])
                P.op("dve", lambda e, h=h: e.tensor_tensor(Mf[h][:], ps(4 + h, 128, 256), Wm[h][:, 128:256], ALU.mult),
                     r=[psr[4 + h], R_("Wm", h)], w=[R_("Mf", h)])
                P.op("dve", lambda e, h=h, cs=cs: e.tensor_tensor(qdec[h][:], qT[:, h, cs], egr[h][:], ALU.mult),
                     r=[P.R("gqkv", 0, h, cc), R_("egr", h)], w=[R_("qdec", h)])
                P.op("dve", lambda e, h=h, col=col: e.tensor_scalar(kvf[h][:, 0:128], psb(4 + h, 512, 640),
                                                                    ebg[:, col:col + 1], None, ALU.mult),
                     r=[psr[4 + h], P.R("g_ebg")], w=[R_("kvf", h)])
                P.op("dve", lambda e, h=h, col=col: e.tensor_scalar(kvt[h][:], psb(4 + h, 512, 640),
                                                                    ekend[:, col:col + 1], None, ALU.mult),
                     r=[psr[4 + h], P.R("g_ekend")], w=[R_("kvt", h)])
                P.op("dve", lambda e, h=h, col=col: e.tensor_scalar(kvf[h][:, 128:256], psb(4 + h, 640, 768),
                                                                    beta[:, col:col + 1], None, ALU.mult),
                     r=[psr[4 + h], P.R("g_beta")], w=[R_("kvf", h)])
            for h in H4:
                P.op("pe", lambda e, h=h: e.transpose(ps(h, 0, 128), Mf[h][:], identf[:]),
                     r=[R_("Mf", h), CR], w=[psr[h]])
            for h in H4:
                P.op("act", lambda e, h=h: e.copy(PQ[h][0][:, 128:256], ps(h, 0, 128)), r=[psr[h]], w=[R_("PQ0", h)])
                P.op("pool", lambda e, h=h: e.tensor_copy(PQ[h][0][:, 0:128], Mf[h][:]),
                     r=[R_("Mf", h)], w=[R_("PQ0", h)])
                P.op("dve", lambda e, h=h: e.tensor_tensor(Y[h][0][:], identf[:], Mf[h][:], ALU.subtract),
                     r=[CR, R_("Mf", h)], w=[R_("Y0", h)])
            for lv in range(6):
                a, bnx = lv % 2, (lv + 1) % 2
                pa, pb_ = "PQ%d" % a, "PQ%d" % bnx
                ya, yb = "Y%d" % a, "Y%d" % bnx
                for h in H4:
                    mm(ps(h, 0, 128), PQ[h][a][:, 128:256], PQ[h][a][:, 0:128], True, True, [R_(pa, h)], [psr[h]])
                    mm(ps(h, 128, 256), PQ[h][a][:, 0:128], PQ[h][a][:, 128:256], True, True, [R_(pa, h)], [psr[h]])
                for h in H4:
                    if h % 2:
                        P.op("act", lambda e, h=h, bnx=bnx: e.copy(PQ[h][bnx][:], ps(h, 0, 256)),
                             r=[psr[h]], w=[R_(pb_, h)])
                    else:
                        P.op("dve", lambda e, h=h, bnx=bnx: e.tensor_copy(PQ[h][bnx][:], ps(h, 0, 256)),
                             r=[psr[h]], w=[R_(pb_, h)])
                for h in H4:
                    mm(ps(h, 256, 384), PQ[h][bnx][:, 128:256], Y[h][a][:], True, True, [R_(pb_, h), R_(ya, h)],
                       [psr[h]])
                for h in H4:
                    P.op("dve", lambda e, h=h, a=a, bnx=bnx: e.tensor_tensor(Y[h][bnx][:], ps(h, 256, 384), Y[h][a][:],
                                                                             ALU.add),
                         r=[psr[h], R_(ya, h)], w=[R_(yb, h)])
            YF = [Y[h][0] for h in H4]
            yf = "Y0"
            for h in H4:
                mm(ps(h, 0, 128), kvf[h][:, 0:128], YF[h][:], True, True, [R_("kvf", h), R_(yf, h)], [psr[h]])
            for h in H4:
                P.op("act", lambda e, h=h: e.mul(nwT[h][:], ps(h, 0, 128), -1.0), r=[psr[h]], w=[R_("nwT", h)])
            for h in H4:
                mm(ps(h, 128, 256), YF[h][:], kvf[h][:, 128:256], True, False, [R_(yf, h), R_("kvf", h)], [psr[h]])
                mm(ps(h, 128, 256), nwT[h][:], stb[h][:], False, True, [R_("nwT", h), R_("stb", h)], [psr[h]])
            for h in H4:
                if h % 2:
                    P.op("act", lambda e, h=h: e.copy(vnew[h][:], ps(h, 128, 256)), r=[psr[h]], w=[R_("vnew", h)])
                else:
                    P.op("dve", lambda e, h=h: e.tensor_copy(vnew[h][:], ps(h, 128, 256)), r=[psr[h]], w=[R_("vnew", h)])
            for h in H4:
                mm(ps(4 + h, 256, 384), qdec[h][:], stb[h][:], True, False, [R_("qdec", h), R_("stb", h)], [psr[4 + h]])
                mm(ps(4 + h, 256, 384), QM[h][:, 0:128], vnew[h][:], False, True, [R_("QM", h), R_("vnew", h)],
                   [psr[4 + h]])
                mm(ps(4 + h, 384, 512), kvt[h][:], vnew[h][:], True, True, [R_("kvt", h), R_("vnew", h)],
                   [psr[4 + h]])
            for h in H4:
                col = T * 4 + h
                P.op("dve", lambda e, h=h, col=col: e.scalar_tensor_tensor(st[h][:], st[h][:], egl[:, col:col + 1],
                                                                           ps(4 + h, 384, 512), ALU.mult, ALU.add),
                     r=[R_("st", h), P.R("g_egl"), psr[4 + h]], w=[R_("st", h)])
                P.op("act", lambda e, h=h: e.copy(stb[h][:], st[h][:]), r=[R_("st", h)], w=[R_("stb", h)])
            si = T % 8
            for h in H4:
                P.op("act", lambda e, h=h, si=si: e.activation(xn_junk[:, 0:128], ps(4 + h, 256, 384), AF.Square,
                                                               accum_out=gsm[:, si, h:h + 1]),
                     r=[psr[4 + h]], w=[P.R("xn_junk"), P.R("gsm", si, h)])
            P.op("act", lambda e, si=si: e.activation(gsm[:, si, 4:8], gsm[:, si, 0:4], AF.Ln, bias=eps_col[:],
                                                      scale=1.0 / GD),
                 r=[P.R("gsm", si, h) for h in H4] + [CR], w=[P.R("gsmr", si)])
            P.op("act", lambda e, si=si: e.activation(gsm[:, si, 4:8], gsm[:, si, 4:8], AF.Exp, scale=-0.5),
                 r=[P.R("gsmr", si)], w=[P.R("gsmr", si)])
            for h in H4:
                P.op("dve", lambda e, h=h, si=si: e.scalar_tensor_tensor(got[h][:], ps(4 + h, 256, 384),
                                                                         gsm[:, si, 4 + h:5 + h], gnrep[:], ALU.mult,
                                                                         ALU.mult),
                     r=[psr[4 + h], P.R("gsmr", si), P.R("g_gn")], w=[R_("got", h)])
                P.op("pool", lambda e, h=h, zi=zi: e.tensor_tensor(gob[h][:], got[h][:], zt[zi][:, h * 128:(h + 1) * 128],
                                                                   ALU.mult),
                     r=[R_("got", h), P.R("zt", zi)], w=[R_("gob", h)])
            gi = (T // 4) % 2
            tq = T % 4
            for h in H4:
                P.op("pe", lambda e, h=h: e.transpose(psb(h, 768, 896), gob[h][:], ident[:]),
                     r=[R_("gob", h), CR], w=[psr[h]])
            for h in H4:
                if h % 2:
                    P.op("act", lambda e, h=h, gi=gi, tq=tq: e.copy(goT[gi][:, h, tq * 128:(tq + 1) * 128],
                                                                    psb(h, 768, 896)),
                         r=[psr[h]], w=[P.R("goT", gi, h)])
                else:
                    P.op("dve", lambda e, h=h, gi=gi, tq=tq: e.tensor_copy(goT[gi][:, h, tq * 128:(tq + 1) * 128],
                                                                           psb(h, 768, 896)),
                         r=[psr[h]], w=[P.R("goT", gi, h)])
            if tq == 3:
                c = T // 4
                for h in H4:
                    P.dma(mixT_d[4 + h, :, c * 512:(c + 1) * 512], goT[gi][:, h, :], r=[P.R("goT", gi, h)],
                          w=[P.R("mixT_d", 4 + h, c)])
        P.barrier()
        ar.release(m)

    def outproj_part(l, h_src, h_dst):
        m = ar.mark()
        wo = ar.alloc("mwo", [128, 8, D], BF16)
        wo_r = load_w(wo, wb["w_out"][l], "mwo")
        mx = [ar.alloc("mx", [128, 8, 512], BF16) for _ in range(2)]
        htile = [ar.alloc("htile", [128, D], F32) for _ in range(4)]
        hout = [ar.alloc("hout", [128, D], F32) for _ in range(2)]
        for g in range(8):
            gi = g % 2
            for c in range(8):
                P.dma(mx[gi][:, c, :], mixT_d[c, :, g * 512:(g + 1) * 512], r=[P.R("mixT_d", c, g)],
                      w=[P.R("mx", gi, c)])
            for t in range(4):
                T = g * 4 + t
                row0 = T * 128
                ti = T % 4
                P.dma(htile[ti][:], h_src[row0:row0 + 128, :], w=[P.R("htile", ti)], r=[P.R("hdram", row0)])
                o = T % 2
                for hf in range(2):
                    b = (T * 2 + hf) % 4
                    for c in range(8):
                        mm(ps(b), mx[gi][:, c, t * 128:(t + 1) * 128], wo[:, c, hf * 512:(hf + 1) * 512], c == 0, c == 7,
                           [P.R("mx", gi, c), wo_r(c)], [psr[b]])
                    P.op("dve", lambda e, ti=ti, hf=hf, b=b, o=o: e.tensor_tensor(
                        hout[o][:, hf * 512:(hf + 1) * 512], ps(b), htile[ti][:, hf * 512:(hf + 1) * 512], ALU.add),
                        r=[psr[b], P.R("htile", ti)], w=[P.R("hout", o, hf)])
                P.dma(h_dst[row0:row0 + 128, :], hout[o][:], r=[P.R("hout", o, 0), P.R("hout", o, 1)],
                      w=[P.R("hdram", row0)])
        P.barrier()
        ar.release(m)

    def final_phase(h_src):
        m = ar.mark()
        ht = [ar.alloc("fht", [128, D], F32) for _ in range(3)]
        ot = [ar.alloc("fot", [128, D], F32) for _ in range(3)]
        g_t, g_r = load_gain(wd["final_norm"][0:1, :])
        for t in range(NT):
            i = t % 3
            hr = P.R("fht", i)
            P.dma(ht[i][:], h_src[t * 128:(t + 1) * 128, :], w=[hr], r=[P.R("hdram", t * 128)])
            rs, rs_r = rms_rstd(ht[i][:], hr, D)
            P.op("dve", lambda e, i=i, rs=rs: e.scalar_tensor_tensor(ot[i][:], ht[i][:], rs, g_t[:], ALU.mult,
                                                                    ALU.mult),
                 r=[hr, rs_r, g_r], w=[P.R("fot", i)])
            P.dma(out_d[t * 128:(t + 1) * 128, :], ot[i][:], r=[P.R("fot", i)], w=[P.R("out", t)])
        P.barrier()
        ar.release(m)

    P.barrier()
    for l in range(1, n_layers):
        cast_layer(l)
    cur = x_d
    for l in range(n_layers):
        if "ffn1" in phases:
            ffn_phase(l, "ffn1", cur, hbuf)
            cur = hbuf
        if "mix" in phases:
            mixer_phase(l, cur, hbuf)
            cur = hbuf
        if "xattn" in phases:
            xattn_phase(l, cur, hbuf)
            cur = hbuf
        if "ffn2" in phases:
            ffn_phase(l, "ffn2", cur, hbuf)
            cur = hbuf
    final_phase(cur)
    P.finish()
    P.emit(stack)
    stack.close()
    global _LAST_PROG
    _LAST_PROG = P
    return nc


_NC_CACHE = {}
_LAST_PROG = None


def kernel(**inputs):
    cfg = inputs.pop("_cfg", None)
    key = repr(cfg)
    if key not in _NC_CACHE:
        _NC_CACHE[key] = build_nc(cfg)
    nc = _NC_CACHE[key]
    nl = (cfg or {}).get("layers", DEPTH)
    x = np.ascontiguousarray(inputs["x"], dtype=np.float32)
    mem = np.ascontiguousarray(inputs["mem"], dtype=np.float32)
    shared = {}
    for k, v in inputs.items():
        if k in ("x", "mem"):
            continue
        a = np.asarray(v, dtype=np.float32)
        if k == "final_norm":
            a = a.reshape(1, D)
        else:
            a = a[:nl]
        shared[k] = np.ascontiguousarray(a)
    in_maps = []
    for c in range(8):
        mp = dict(shared)
        mp["x"] = x[c]
        mp["mem"] = mem[c]
        in_maps.append(mp)
    res = run_bass_kernel_spmd(nc, in_maps, core_ids=list(range(8)))
    return np.stack([np.asarray(r["out"], dtype=np.float32) for r in res.results], axis=0)
```

```python
import numpy as np
import concourse.bass as bass
import concourse.mybir as mybir
from concourse.bass_utils import run_bass_kernel_spmd
from contextlib import ExitStack

F32 = mybir.dt.float32
BF16 = mybir.dt.bfloat16
ALU = mybir.AluOpType
AF = mybir.ActivationFunctionType
AX = mybir.AxisListType

D = 1024
S = 4096
DEPTH = 4
MEM = 256
DFF = 2816
NFC = DFF // 128
N_IN = 3592
SBH, SBD = 8, 64
GH, GD = 4, 128
XH, XD = 4, 256
OFF_SBQ, OFF_SBK, OFF_SBV = 0, 512, 1024
OFF_GQ, OFF_GK, OFF_GV = 1536, 2048, 2560
OFF_GZ, OFF_GA, OFF_GB = 3072, 3584, 3588
RMS_EPS = 1e-6
L2_EPS = 1e-6
NT = S // 128

ENGS = ("pe", "act", "dve", "pool", "sp")
NDMA = 16
NDMA_SP = 12


class Res:
    __slots__ = ("name", "lastw", "readers", "excl")

    def __init__(self, name):
        self.name = name
        self.lastw = None
        self.readers = {}
        self.excl = (name[0] in ("ps", "ps0", "ps1", "ps3", "ps4", "ps5"))


class Prog:
    def __init__(self, nc):
        self.nc = nc
        self.ops = {e: [] for e in ENGS}
        self.known = {e: {} for e in ENGS}
        self.dma_rr = 0
        self.dma_rr2 = 0
        self.dma_cnt = [0] * NDMA
        self.dma_last = [None] * NDMA
        self.res = {}

    def R(self, *key):
        r = self.res.get(key)
        if r is None:
            r = self.res[key] = Res(key)
        return r

    def _waits(self, eng, raw, other):
        out = []
        kn = self.known[eng]
        for tok, is_raw in [(t, True) for t in raw] + [(t, False) for t in other]:
            if tok is None:
                continue
            kind, key, val = tok
            if kind == "e" and key == eng:
                if eng in ("pe", "sp"):
                    continue
            k = (kind, key)
            if kn.get(k, -1) >= val:
                continue
            kn[k] = val
            out.append(tok)
            if kind == "e":
                self.ops[key][val][2] = True
        return out

    def _deps(self, r, w, eng=None):
        raw = [x.lastw for x in r]
        other = []
        for x in r:
            if x.excl:
                other.extend(t for k, t in x.readers.items() if k != eng)
        for x in w:
            other.append(x.lastw)
            other.extend(x.readers.values())
        return raw, other

    def _commit(self, tok, eng, r, w):
        for x in r:
            x.readers[eng] = tok
        for x in w:
            x.lastw = tok
            x.readers = {}

    def op(self, eng, fn, r=(), w=()):
        raw, other = self._deps(r, w, eng)
        waits = self._waits(eng, raw, other)
        idx = len(self.ops[eng])
        self.ops[eng].append([fn, waits, False, None])
        tok = ("e", eng, idx)
        self._commit(tok, eng, r, w)
        return tok

    def dma(self, out_ap, in_ap, r=(), w=(), q="sp", **kw):
        if q == "sp":
            slot = self.dma_rr
            self.dma_rr = (slot + 1) % NDMA_SP
        else:
            slot = NDMA_SP + self.dma_rr2
            self.dma_rr2 = (self.dma_rr2 + 1) % (NDMA - NDMA_SP)
        raw, other = self._deps(r, w)
        other = list(other) + [self.dma_last[slot]]
        waits = self._waits(q, raw, other)
        self.dma_cnt[slot] += 16
        tok = ("d", slot, self.dma_cnt[slot])
        self.dma_last[slot] = tok

        def fn(e, out_ap=out_ap, in_ap=in_ap, kw=kw):
            return e.dma_start(out=out_ap, in_=in_ap, **kw)

        self.ops[q].append([fn, waits, False, slot])
        for x in r:
            x.readers[("dma", slot)] = tok
        for x in w:
            x.lastw = tok
            x.readers = {}
        return tok

    def _last_toks(self, casts=True):
        toks = []
        for e in ENGS:
            if e == "sp":
                continue
            for i in range(len(self.ops[e]) - 1, -1, -1):
                o = self.ops[e][i]
                if o[0] is not None and o[3] is None:
                    toks.append(("e", e, i))
                    break
        toks += [t for i, t in enumerate(self.dma_last) if t is not None and (casts or i < NDMA_SP)]
        return toks

    def barrier(self):
        toks = self._last_toks(casts=False)
        for e in ENGS:
            waits = self._waits(e, toks, [])
            if waits:
                self.ops[e].append([None, waits, False, None])

    def finish(self):
        toks = self._last_toks()
        waits = self._waits("sp", toks, [])
        self.ops["sp"].append([None, waits, False, None])

    def emit(self, stack):
        nc = self.nc
        esem = {e: stack.enter_context(nc.semaphore("s_" + e)) for e in ENGS if e != "sp"}
        dsem = [stack.enter_context(nc.semaphore("d%d" % i)) for i in range(NDMA)]
        val = {}
        for e in ENGS:
            if e == "sp":
                continue
            c = 0
            for i, o in enumerate(self.ops[e]):
                if o[2] and o[0] is not None and o[3] is None:
                    c += 1
                    val[(e, i)] = c
                elif o[2]:
                    raise RuntimeError("flagged a non-instruction op")

        def replay(name, eng):
            for i, (fn, waits, flag, slot) in enumerate(self.ops[name]):
                for kind, key, v in waits:
                    if kind == "e":
                        eng.wait_ge(esem[key], val[(key, v)])
                    else:
                        eng.wait_ge(dsem[key], v)
                if fn is None:
                    continue
                ins = fn(eng)
                if slot is not None:
                    ins.then_inc(dsem[slot], 16)
                elif flag:
                    ins.then_inc(esem[name], 1)

        block = stack.enter_context(nc.Block())

        @block.tensor
        def _(eng):
            replay("pe", eng)

        @block.scalar
        def _(eng):
            replay("act", eng)

        @block.vector
        def _(eng):
            replay("dve", eng)

        @block.gpsimd
        def _(eng):
            replay("pool", eng)

        @block.sync
        def _(eng):
            replay("sp", eng)


class Arena:
    def __init__(self, nc, base, limit):
        self.nc = nc
        self.top = base
        self.limit = limit
        self.hi = limit
        self.n = 0

    def alloc_hi(self, name, shape, dtype):
        esz = 4 if dtype == F32 else 2
        per = esz
        for s in shape[1:]:
            per *= s
        off = (self.hi - per) // 64 * 64
        if off < self.top:
            raise RuntimeError("SBUF arena overflow (hi) at %s" % name)
        self.hi = off
        self.n += 1
        return self.nc.alloc_sbuf_tensor_at("%s_%d" % (name, self.n), list(shape), dtype, offset=off)

    def release_hi(self):
        self.hi = self.limit

    def mark(self):
        return self.top

    def release(self, m):
        self.top = m

    def alloc(self, name, shape, dtype):
        esz = 4 if dtype == F32 else 2
        per = esz
        for s in shape[1:]:
            per *= s
        off = (self.top + 63) // 64 * 64
        if off + per > self.hi:
            raise RuntimeError("SBUF arena overflow at %s: %d + %d > %d" % (name, off, per, self.hi))
        self.top = off + per
        self.n += 1
        return self.nc.alloc_sbuf_tensor_at("%s_%d" % (name, self.n), list(shape), dtype, offset=off)


def build_nc(cfg=None):
    cfg = cfg or {}
    n_layers = cfg.get("layers", DEPTH)
    phases = cfg.get("phases", ("ffn1", "mix", "xattn", "ffn2"))
    do_sb = cfg.get("sb", True)
    do_gdn = cfg.get("gdn", True)
    nc = bass.Bass("TRN2", target_bir_lowering=False)
    stack = ExitStack()
    P = Prog(nc)
    L = n_layers

    def din(name, shape):
        return nc.dram_tensor(name, list(shape), F32, kind="ExternalInput").ap()

    x_d = din("x", [S, D])
    mem_d = din("mem", [MEM, D])
    wd = {}
    for name, shape in [("ffn1_norm", [L, D]), ("ffn1_w_in", [L, D, 2 * DFF]), ("ffn1_w_out", [L, DFF, D]),
                        ("mix_norm", [L, D]), ("w_in", [L, D, N_IN]), ("conv_w", [L, 4, 1536]),
                        ("a_log", [L, 4]), ("dt_bias", [L, 4]), ("sb_out_norm", [L, 512]),
                        ("gdn_out_norm", [L, 128]), ("w_out", [L, D, D]), ("xattn_norm", [L, D]),
                        ("mem_norm", [L, D]), ("xattn_w_q", [L, D, D]), ("xattn_w_kv", [L, D, 2 * D]),
                        ("xattn_w_o", [L, D, D]), ("ffn2_norm", [L, D]), ("ffn2_w_in", [L, D, 2 * DFF]),
                        ("ffn2_w_out", [L, DFF, D]), ("final_norm", [1, D])]:
        wd[name] = din(name, shape)
    out_d = nc.dram_tensor("out", [S, D], F32, kind="ExternalOutput").ap()
    hbuf = nc.dram_tensor("hbuf", [S, D], F32, kind="Internal").ap()
    mixT_d = nc.dram_tensor("mixT_d", [8, 128, S], BF16, kind="Internal").ap()
    zs_d = nc.dram_tensor("zs_d", [S, 512], F32, kind="Internal").ap()
    wb = {}
    for name in ("ffn1_w_in", "ffn1_w_out", "w_in", "w_out", "xattn_w_q", "xattn_w_kv", "xattn_w_o",
                 "ffn2_w_in", "ffn2_w_out"):
        wb[name] = nc.dram_tensor(name + "_bf", list(wd[name].shape), BF16, kind="Internal").ap()

    need = set()
    for ph in phases:
        need |= {"ffn1": {"ffn1_w_in", "ffn1_w_out"}, "ffn2": {"ffn2_w_in", "ffn2_w_out"},
                 "mix": {"w_in", "w_out"}, "xattn": {"xattn_w_q", "xattn_w_kv", "xattn_w_o"}}[ph]
    cast_tok = {}

    def cast_layer(l):
        for name in wb:
            cast_tok[(name, l)] = None
            if name not in need:
                continue
            rows = wd[name].shape[1]
            for r0 in range(0, rows, 256):
                r1 = min(rows, r0 + 256)
                P.dma(wb[name][l, r0:r1, :], wd[name][l, r0:r1, :], w=[P.R("wb", name, l)], q="pool",
                      max_dma_last_dim=8192)
            cast_tok[(name, l)] = [t for t in P.dma_last[NDMA_SP:] if t is not None]

    def need_w(l, *names):
        toks = []
        for name in names:
            toks += cast_tok[(name, l)] or []
        waits = P._waits("sp", toks, [])
        if waits:
            P.ops["sp"].append([None, waits, False, None])

    ar = Arena(nc, 16640, 229376 - 128)
    ident = ar.alloc("ident", [128, 128], BF16)
    identf = ar.alloc("identf", [128, 128], F32)
    onesf = ar.alloc("onesf", [128, 128], F32)
    onesb = ar.alloc("onesb", [128, 128], BF16)
    zerob = ar.alloc("zerob", [128, 128], BF16)
    triuf = ar.alloc("triuf", [128, 128], F32)
    ltri = ar.alloc("ltri", [128, 128], BF16)
    utri = ar.alloc("utri", [128, 128], BF16)
    m01 = ar.alloc("m01", [128, 128], BF16)
    blk64 = ar.alloc("blk64", [128, 128], BF16)
    gain = [ar.alloc("gain", [128, D], F32) for _ in range(1)]
    stat = ar.alloc("stat", [128, 64], F32)
    eps_col = ar.alloc("eps", [128, 1], F32)
    xn_junk = ar.alloc("xn_junk", [128, D], BF16)
    psall = stack.enter_context(nc.psum_tensor("psall", [128, 4096], F32))
    psbf_all = psall.bitcast(BF16)

    def ps(i, c0=0, c1=512):
        return psall[:, i * 512 + c0:i * 512 + c1]

    def psb(i, c0=0, c1=1024):
        return psbf_all[:, i * 1024 + c0:i * 1024 + c1]

    psr = [P.R("ps", i) for i in range(8)]
    CR = P.R("consts")

    def cop(fn):
        P.op("pool", fn, r=[CR], w=[CR])

    cop(lambda e: e.memset(identf[:], 0.0))
    cop(lambda e: e.affine_select(identf[:], identf[:], [[-1, 128]], ALU.not_equal, 1.0, base=0,
                                  channel_multiplier=1))
    cop(lambda e: e.tensor_copy(ident[:], identf[:]))
    cop(lambda e: e.memset(onesf[:], 1.0))
    cop(lambda e: e.memset(onesb[:], 1.0))
    cop(lambda e: e.memset(zerob[:], 0.0))
    cop(lambda e: e.memset(eps_col[:], RMS_EPS))
    cop(lambda e: e.affine_select(triuf[:], onesf[:], [[1, 128]], ALU.is_ge, 0.0, base=0, channel_multiplier=-1))
    cop(lambda e: e.affine_select(ltri[:], onesb[:], [[-1, 128]], ALU.is_ge, 0.0, base=0, channel_multiplier=1))
    cop(lambda e: e.affine_select(utri[:], onesb[:], [[1, 128]], ALU.is_gt, 0.0, base=0, channel_multiplier=-1))
    cop(lambda e: e.affine_select(m01[:], onesb[:], [[1, 128]], ALU.is_gt, 0.0, base=0, channel_multiplier=-1))
    cop(lambda e: e.memset(blk64[:], 0.0))
    cop(lambda e: e.memset(blk64[0:64, 0:64], 1.0))
    cop(lambda e: e.memset(blk64[64:128, 64:128], 1.0))
    cast_layer(0)

    gain_i = [0]

    def load_gain(ap_row):
        i = 0
        r = P.R("gain", i)
        P.dma(gain[i][:], ap_row.partition_broadcast(128), w=[r])
        return gain[i], r

    stat_i = [0]

    def stat_col():
        i = stat_i[0] % 64
        stat_i[0] += 1
        return stat[:, i:i + 1], P.R("stat", i)

    def rstd_from_ss(ss, ss_r, n, eps_ap=None):
        rs, rs_r = stat_col()
        ea = eps_col if eps_ap is None else eps_ap
        P.op("act", lambda e: e.activation(rs, ss, AF.Ln, bias=ea[:], scale=1.0 / n), r=[ss_r, CR], w=[rs_r])
        P.op("act", lambda e: e.activation(rs, rs, AF.Exp, scale=-0.5), r=[rs_r], w=[rs_r])
        return rs, rs_r

    def rms_rstd(src_ap, src_res, n):
        ss, ss_r = stat_col()
        P.op("act", lambda e: e.activation(xn_junk[:, 0:n], src_ap, AF.Square, accum_out=ss),
             r=[src_res], w=[P.R("xn_junk"), ss_r])
        return rstd_from_ss(ss, ss_r, n)

    def norm_tile_to_xnT(h_ap, h_res, g_t, g_r, xn_t, xn_r, pbank, xnT_dst, xnT_res, evac_eng):
        rs, rs_r = rms_rstd(h_ap, h_res, D)
        P.op("dve", lambda e: e.scalar_tensor_tensor(xn_t[:], h_ap, rs, g_t[:], ALU.mult, ALU.mult),
             r=[h_res, rs_r, g_r], w=[xn_r])
        for k in range(8):
            P.op("pe", lambda e, k=k: e.transpose(psb(pbank, k * 128, (k + 1) * 128),
                                                  xn_t[:, k * 128:(k + 1) * 128], ident[:]),
                 r=[xn_r, CR], w=[psr[pbank]])
        src = psb(pbank).rearrange("p (k t) -> p k t", k=8)
        if evac_eng == "act":
            P.op("act", lambda e: e.copy(xnT_dst, src), r=[psr[pbank]], w=[xnT_res])
        else:
            P.op("dve", lambda e: e.tensor_copy(xnT_dst, src), r=[psr[pbank]], w=[xnT_res])

    def bmid(ap2, n):
        a = ap2.ap
        return bass.AP(ap2.tensor, ap2.offset, [list(a[0]), [0, n], list(a[1])])

    def mm(out_ap, lhsT, rhs, start, stop, r, w, skip=False):
        if skip:
            P.op("pe", lambda e: e.matmul(out_ap, lhsT, rhs, start=start, stop=stop, skip_group_check=True), r=r, w=w)
        else:
            P.op("pe", lambda e: e.matmul(out_ap, lhsT, rhs, start=start, stop=stop), r=r, w=w)

    def load_w(dst, src2d, name, kc=2):
        nk = src2d.shape[0] // 128
        for k0 in range(0, nk, kc):
            k1 = min(nk, k0 + kc)
            P.dma(dst[:, k0:k1, :], src2d[k0 * 128:k1 * 128, :].rearrange("(k p) n -> p k n", p=128),
                  r=[], w=[P.R(name, k0 // kc)])
        return lambda k: P.R(name, k // kc)

    def ffn_phase(l, which, h_src, h_dst):
        G = 1024
        NCH = G // 512
        TPG = G // 128
        m = ar.mark()
        xnT = ar.alloc("xnT", [128, 8, G], BF16)
        actT = ar.alloc("actT", [128, NFC, G], BF16)
        wout = ar.alloc("wout", [128, NFC, D], BF16)
        htile = [ar.alloc("htile", [128, D], F32) for _ in range(TPG)]
        xn = [ar.alloc("xn", [128, D], BF16) for _ in range(2)]
        NWB = 3
        wg = [ar.alloc("wg", [128, 8, 256], BF16) for _ in range(NWB)]
        wu = [ar.alloc("wu", [128, 8, 256], BF16) for _ in range(NWB)]
        sg = [ar.alloc("sg", [128, 512], BF16) for _ in range(2)]
        hout = [ar.alloc("hout", [128, D], F32) for _ in range(2)]
        w_in_b = wb[which + "_w_in"]
        w_out_b = wb[which + "_w_out"]
        need_w(l, which + "_w_in", which + "_w_out")
        g_t, g_r = load_gain(wd[which + "_norm"][l:l + 1, :])
        wout_r = load_w(wout, w_out_b[l], "wout")
        wbi = 0
        for g in range(S // G):
            for t in range(TPG):
                row0 = g * G + t * 128
                hr = P.R("htile", t)
                P.dma(htile[t][:], h_src[row0:row0 + 128, :], w=[hr], r=[P.R("hdram", row0)])
                i = t % 2
                norm_tile_to_xnT(htile[t][:], hr, g_t, g_r, xn[i], P.R("xn", i), 4 + i,
                                 xnT[:, :, t * 128:(t + 1) * 128], P.R("xnT", t // 4), "act" if t % 2 else "dve")
            for fb in range(NFC // 2):
                i = wbi % NWB
                wbi += 1
                c0 = fb * 256
                P.dma(wg[i][:], w_in_b[l, :, c0:c0 + 256].rearrange("(k p) f -> p k f", p=128),
                      w=[P.R("wg", i)])
                P.dma(wu[i][:], w_in_b[l, :, DFF + c0:DFF + c0 + 256].rearrange("(k p) f -> p k f", p=128),
                      w=[P.R("wu", i)])
                for fj in range(2):
                    f = fb * 2 + fj
                    for c in range(NCH):
                        u = (f * NCH + c) % 2
                        bg, bu = 2 * u, 2 * u + 1
                        for k in range(8):
                            mm(ps(bg), wg[i][:, k, fj * 128:(fj + 1) * 128], xnT[:, k, c * 512:(c + 1) * 512],
                               k == 0, k == 7, [P.R("wg", i), P.R("xnT", c)], [psr[bg]])
                        for k in range(8):
                            mm(ps(bu), wu[i][:, k, fj * 128:(fj + 1) * 128], xnT[:, k, c * 512:(c + 1) * 512],
                               k == 0, k == 7, [P.R("wu", i), P.R("xnT", c)], [psr[bu]])
                        P.op("act", lambda e, u=u, bg=bg: e.activation(sg[u][:], ps(bg), AF.Silu),
                             r=[psr[bg]], w=[P.R("sg", u)])
                        P.op("dve", lambda e, u=u, bu=bu, f=f, c=c: e.tensor_tensor(
                            actT[:, f, c * 512:(c + 1) * 512], ps(bu), sg[u][:], ALU.mult),
                            r=[psr[bu], P.R("sg", u)], w=[P.R("actT", f, c)])
            for t in range(TPG):
                row0 = g * G + t * 128
                c = t // 4
                o = t % 2
                for hf in range(2):
                    b = 4 + (t * 2 + hf) % 4
                    for f in range(NFC):
                        mm(ps(b), actT[:, f, t * 128:(t + 1) * 128], wout[:, f, hf * 512:(hf + 1) * 512],
                           f == 0, f == NFC - 1, [P.R("actT", f, c), wout_r(f)], [psr[b]])
                    P.op("dve", lambda e, t=t, hf=hf, b=b, o=o: e.scalar_tensor_tensor(
                        hout[o][:, hf * 512:(hf + 1) * 512], ps(b), 0.5, htile[t][:, hf * 512:(hf + 1) * 512],
                        ALU.mult, ALU.add),
                        r=[psr[b], P.R("htile", t)], w=[P.R("hout", o, hf)])
                P.dma(h_dst[row0:row0 + 128, :], hout[o][:], r=[P.R("hout", o, 0), P.R("hout", o, 1)],
                      w=[P.R("hdram", row0)])
        P.barrier()
        ar.release(m)

    def xattn_phase(l, h_src, h_dst):
        m = ar.mark()
        need_w(l, "xattn_w_q", "xattn_w_kv", "xattn_w_o")
        wq = ar.alloc("xwq", [128, 8, D], BF16)
        wkv = ar.alloc("xwkv", [128, 8, 2 * D], BF16)
        wo = ar.alloc("xwo", [128, 8, D], BF16)
        wq_r = load_w(wq, wb["xattn_w_q"][l], "xwq")
        wkv_r = load_w(wkv, wb["xattn_w_kv"][l], "xwkv")
        wo_r = load_w(wo, wb["xattn_w_o"][l], "xwo")
        mt = [ar.alloc("memt", [128, D], F32) for _ in range(2)]
        xn = [ar.alloc("xn", [128, D], BF16) for _ in range(2)]
        memnT = ar.alloc("memnT", [128, 8, MEM], BF16)
        kTx = ar.alloc("kTx", [128, 8, MEM], BF16)
        vx = ar.alloc("vx", [128, 2, D], BF16)
        htile = [ar.alloc("htile", [128, D], F32) for _ in range(4)]
        xnT = [ar.alloc("xnT", [128, 8, 512], BF16) for _ in range(2)]
        qT = [ar.alloc("qT", [128, 8, 512], BF16) for _ in range(2)]
        pe_t = [ar.alloc("pe_t", [128, 4, MEM], BF16) for _ in range(2)]
        pn = [ar.alloc("pn", [128, 4, MEM], BF16) for _ in range(2)]
        pT = [ar.alloc("pT", [128, 8, 512], BF16) for _ in range(2)]
        oT = [ar.alloc("oT", [128, 8, 512], BF16) for _ in range(2)]
        hout = [ar.alloc("hout", [128, D], F32) for _ in range(2)]
        sm = ar.alloc("sm", [128, 16, 12], F32)
        g_t, g_r = load_gain(wd["mem_norm"][l:l + 1, :])
        for mi in range(2):
            P.dma(mt[mi][:], mem_d[mi * 128:(mi + 1) * 128, :], w=[P.R("memt", mi)])
            norm_tile_to_xnT(mt[mi][:], P.R("memt", mi), g_t, g_r, xn[mi], P.R("xn", mi), 4 + mi,
                             memnT[:, :, mi * 128:(mi + 1) * 128], P.R("memnT", mi), "dve")
        mres = [P.R("memnT", 0), P.R("memnT", 1)]
        for hj in range(8):
            b = hj % 2
            for k in range(8):
                mm(ps(b, 0, MEM), wkv[:, k, hj * 128:(hj + 1) * 128], memnT[:, k, :], k == 0, k == 7,
                   [wkv_r(k)] + mres, [psr[b]])
            P.op("act", lambda e, hj=hj, b=b: e.mul(kTx[:, hj, :], ps(b, 0, MEM), float(XD) ** -0.5),
                 r=[psr[b]], w=[P.R("kTx")])
        for mi in range(2):
            for hf in range(2):
                b = 2 + hf
                for k in range(8):
                    mm(ps(b), memnT[:, k, mi * 128:(mi + 1) * 128], wkv[:, k, D + hf * 512:D + (hf + 1) * 512],
                       k == 0, k == 7, [wkv_r(k), mres[mi]], [psr[b]])
                P.op("dve", lambda e, mi=mi, hf=hf, b=b: e.tensor_copy(vx[:, mi, hf * 512:(hf + 1) * 512], ps(b)),
                     r=[psr[b]], w=[P.R("vx")])
        g_t, g_r = load_gain(wd["xattn_norm"][l:l + 1, :])
        for g in range(S // 512):
            gi = g % 2
            for t in range(4):
                row0 = g * 512 + t * 128
                hr = P.R("htile", t)
                P.dma(htile[t][:], h_src[row0:row0 + 128, :], w=[hr], r=[P.R("hdram", row0)])
                i = t % 2
                norm_tile_to_xnT(htile[t][:], hr, g_t, g_r, xn[i], P.R("xn", i), 4 + i,
                                 xnT[gi][:, :, t * 128:(t + 1) * 128], P.R("xnTx", gi), "act" if t % 2 else "dve")
            for hj in range(8):
                b = hj % 2
                for k in range(8):
                    mm(ps(b), wq[:, k, hj * 128:(hj + 1) * 128], xnT[gi][:, k, :], k == 0, k == 7,
                       [wq_r(k), P.R("xnTx", gi)], [psr[b]])
                if hj % 2:
                    P.op("act", lambda e, hj=hj, b=b, gi=gi: e.copy(qT[gi][:, hj, :], ps(b)),
                         r=[psr[b]], w=[P.R("qTx", gi)])
                else:
                    P.op("dve", lambda e, hj=hj, b=b, gi=gi: e.tensor_copy(qT[gi][:, hj, :], ps(b)),
                         r=[psr[b]], w=[P.R("qTx", gi)])
            for t in range(4):
                u = t % 2
                si = (g * 4 + t) % 16
                for h in range(4):
                    b = 2 + h // 2
                    c0 = (h % 2) * MEM
                    for j in range(2):
                        mm(ps(b, c0, c0 + MEM), qT[gi][:, h * 2 + j, t * 128:(t + 1) * 128], kTx[:, h * 2 + j, :],
                           j == 0, j == 1, [P.R("qTx", gi), P.R("kTx")], [psr[b]])
                sc = psall[:, 2 * 512:4 * 512].rearrange("p (h m) -> p h m", h=4)
                P.op("dve", lambda e, si=si, sc=sc: e.tensor_reduce(sm[:, si, 0:4], sc, AX.X, ALU.max, negate=True),
                     r=[psr[2], psr[3]], w=[P.R("sm", si)])
                for h in range(4):
                    b = 2 + h // 2
                    c0 = (h % 2) * MEM
                    P.op("act", lambda e, h=h, b=b, c0=c0, u=u, si=si: e.activation(
                        pe_t[u][:, h, :], ps(b, c0, c0 + MEM), AF.Exp, bias=sm[:, si, h:h + 1], scale=1.0,
                        accum_out=sm[:, si, 4 + h:5 + h]),
                        r=[psr[b], P.R("sm", si)], w=[P.R("pe_t", u, h), P.R("smz", si, h)])
                P.op("dve", lambda e, si=si: e.reciprocal(sm[:, si, 8:12], sm[:, si, 4:8]),
                     r=[P.R("smz", si, h) for h in range(4)], w=[P.R("smr", si)])
                for h in range(4):
                    P.op("dve", lambda e, h=h, u=u, si=si: e.tensor_scalar(
                        pn[u][:, h, :], pe_t[u][:, h, :], sm[:, si, 8 + h:9 + h], None, ALU.mult),
                        r=[P.R("pe_t", u, h), P.R("smr", si)], w=[P.R("pn", u)])
                tb = 6 + u
                for h in range(4):
                    for mc in range(2):
                        P.op("pe", lambda e, h=h, mc=mc, u=u, tb=tb: e.transpose(
                            psb(tb, (h * 2 + mc) * 128, (h * 2 + mc + 1) * 128),
                            pn[u][:, h, mc * 128:(mc + 1) * 128], ident[:]),
                            r=[P.R("pn", u), CR], w=[psr[tb]])
                srcT = psb(tb).rearrange("p (k t) -> p k t", k=8)
                if t % 2:
                    P.op("act", lambda e, t=t, gi=gi, srcT=srcT: e.copy(pT[gi][:, :, t * 128:(t + 1) * 128], srcT),
                         r=[psr[tb]], w=[P.R("pT", gi)])
                else:
                    P.op("dve", lambda e, t=t, gi=gi, srcT=srcT: e.tensor_copy(pT[gi][:, :, t * 128:(t + 1) * 128],
                                                                              srcT),
                         r=[psr[tb]], w=[P.R("pT", gi)])
            for hc in range(8):
                b = hc % 2
                h = hc // 2
                for mc in range(2):
                    mm(ps(b), vx[:, mc, hc * 128:(hc + 1) * 128], pT[gi][:, h * 2 + mc, :], mc == 0, mc == 1,
                       [P.R("vx"), P.R("pT", gi)], [psr[b]])
                if hc % 2:
                    P.op("act", lambda e, hc=hc, b=b, gi=gi: e.copy(oT[gi][:, hc, :], ps(b)),
                         r=[psr[b]], w=[P.R("oT", gi)])
                else:
                    P.op("dve", lambda e, hc=hc, b=b, gi=gi: e.tensor_copy(oT[gi][:, hc, :], ps(b)),
                         r=[psr[b]], w=[P.R("oT", gi)])
            for t in range(4):
                row0 = g * 512 + t * 128
                o = t % 2
                for hf in range(2):
                    b = 4 + (t * 2 + hf) % 2
                    for c in range(8):
                        mm(ps(b), oT[gi][:, c, t * 128:(t + 1) * 128], wo[:, c, hf * 512:(hf + 1) * 512],
                           c == 0, c == 7, [P.R("oT", gi), wo_r(c)], [psr[b]])
                    P.op("dve", lambda e, t=t, hf=hf, b=b, o=o: e.tensor_tensor(
                        hout[o][:, hf * 512:(hf + 1) * 512], ps(b), htile[t][:, hf * 512:(hf + 1) * 512], ALU.add),
                        r=[psr[b], P.R("htile", t)], w=[P.R("hout", o, hf)])
                P.dma(h_dst[row0:row0 + 128, :], hout[o][:], r=[P.R("hout", o, 0), P.R("hout", o, 1)],
                      w=[P.R("hdram", row0)])
        P.barrier()
        ar.release(m)

    def mixer_phase(l, h_src, h_dst):
        m0 = ar.mark()
        need_w(l, "w_in", "w_out")
        xnT = ar.alloc_hi("mxnT", [128, 8, S], BF16)
        w_in_b = wb["w_in"][l]
        mA = ar.mark()
        ht = [ar.alloc("mht", [128, D], F32) for _ in range(3)]
        xn = [ar.alloc("xn", [128, D], BF16) for _ in range(2)]
        g_t, g_r = load_gain(wd["mix_norm"][l:l + 1, :])
        for t in range(NT):
            i3 = t % 3
            hr = P.R("mht", i3)
            P.dma(ht[i3][:], h_src[t * 128:(t + 1) * 128, :], w=[hr], r=[P.R("hdram", t * 128)])
            i = t % 2
            norm_tile_to_xnT(ht[i3][:], hr, g_t, g_r, xn[i], P.R("xn", i), 4 + i,
                             xnT[:, :, t * 128:(t + 1) * 128], P.R("mxnT", t // 4), "act" if t % 2 else "dve")
        xr = lambda c: P.R("mxnT", c)
        allx = [xr(c) for c in range(8)]
        P.barrier()
        ar.release(mA)
        if do_sb:
            sb_part(l, xnT, xr, w_in_b)
        else:
            zero_mix(0)
        if do_gdn:
            gdn_part(l, xnT, xr, allx, w_in_b)
        else:
            zero_mix(4)
        ar.release(m0)
        ar.release_hi()
        outproj_part(l, h_src, h_dst)

    def zero_mix(c0):
        m = ar.mark()
        z = ar.alloc("zmix", [128, S], BF16)
        P.op("pool", lambda e: e.memset(z[:], 0.0), w=[P.R("zmix")])
        for c in range(c0, c0 + 4):
            P.dma(mixT_d[c], z[:], r=[P.R("zmix")], w=[P.R("mixT_d", c, g) for g in range(8)])
        P.barrier()
        ar.release(m)

    def sb_part(l, xnT, xr, w_in_b):
        m = ar.mark()
        wqk = ar.alloc("wqk", [128, 8, 1024], BF16)
        wv = ar.alloc("wv", [128, 8, 512], BF16)
        wqk_r = load_w(wqk, w_in_b[:, 0:1024], "wqk")
        wv_r = load_w(wv, w_in_b[:, 1024:1536], "wv")
        v_all = ar.alloc("v_all", [128, NT, 512], BF16)
        qT2 = ar.alloc("qT2", [128, S], BF16)
        kT2 = ar.alloc("kT2", [128, S], BF16)
        Eb = [ar.alloc("Eb", [128, 2, 512], F32) for _ in range(3)]
        SP = [ar.alloc("SP", [128, 2, 512], BF16) for _ in range(3)]
        AT = [ar.alloc("AT", [128, 2, 512], BF16) for _ in range(2)]
        osb = ar.alloc("osb", [128, 512], F32)
        osq = ar.alloc("osq", [128, 512], BF16)
        rsd = ar.alloc("rsd", [128, 512], F32)
        sbo = [ar.alloc("sbo", [128, 512], BF16) for _ in range(2)]
        sbg = ar.alloc("sbg", [128, 4], F32)
        eps64 = ar.alloc("eps64", [128, 1], F32)
        P.op("pool", lambda e: e.memset(eps64[:], RMS_EPS), w=[P.R("eps64")])
        P.dma(sbg[:], wd["sb_out_norm"][l].rearrange("(j p) -> p j", p=128), w=[P.R("sbg")],
              allow_slow_non_contiguous=True)
        for t in range(NT):
            b = t % 2
            for k in range(8):
                mm(ps(b), xnT[:, k, t * 128:(t + 1) * 128], wv[:, k, :], k == 0, k == 7, [xr(t // 4), wv_r(k)], [psr[b]])
            if t % 2:
                P.op("act", lambda e, t=t, b=b: e.copy(v_all[:, t, :], ps(b)), r=[psr[b]], w=[P.R("v_all", t)])
            else:
                P.op("dve", lambda e, t=t, b=b: e.tensor_copy(v_all[:, t, :], ps(b)), r=[psr[b]], w=[P.R("v_all", t)])
        for j in range(cfg.get("sb_pairs", 4)):
            for c in range(8):
                cs = slice(c * 512, (c + 1) * 512)
                b = 0
                for k in range(8):
                    mm(ps(b), wqk[:, k, j * 128:(j + 1) * 128], xnT[:, k, cs], k == 0, k == 7, [wqk_r(k), xr(c)],
                       [psr[b]])
                if cfg.get("sb_proj", 7) & 1:
                    P.op("dve", lambda e, cs=cs, b=b: e.tensor_copy(qT2[:, cs], ps(b)), r=[psr[b]], w=[P.R("qT2", c)])
                b = 1
                for k in range(8):
                    mm(ps(b), wqk[:, k, 512 + j * 128:512 + (j + 1) * 128], xnT[:, k, cs], k == 0, k == 7,
                       [wqk_r(k), xr(c)], [psr[b]])
                if cfg.get("sb_proj", 7) & 2:
                    P.op("act", lambda e, cs=cs, b=b: e.mul(kT2[:, cs], ps(b), float(SBD) ** -0.5),
                         r=[psr[b]], w=[P.R("kT2", c)])
            units = []
            for c in range(cfg.get("sb_chunks", 8)):
                for kb in range(4 * c + 3, -1, -1):
                    units.append((c, kb))
            NU = len(units)

            def geom(u):
                c, kb = units[u]
                i = kb - 4 * c
                col0 = 128 * i if i > 0 else 0
                return c, kb, col0, (i >= 0)

            def rq(c):
                return P.R("qT2", c)

            def rk(kb):
                return P.R("kT2", kb // 4)

            def rnk(kb):
                return P.R("nkT2", kb // 4)

            def do_Z(u):
                c, kb, col0, diag = geom(u)
                zb = 2 * (u % 2)
                for hd in range(2):
                    pp = slice(hd * 64, (hd + 1) * 64)
                    mm(ps(zb + hd, col0, 512), kT2[pp, kb * 128:(kb + 1) * 128],
                       qT2[pp, c * 512 + col0:(c + 1) * 512], True, True, [rk(kb), rq(c)], [psr[zb + hd]])

            def do_ESP(u):
                c, kb, col0, diag = geom(u)
                zb = 2 * (u % 2)
                eb = u % 3
                sp = u % 3
                zin = psall[:, zb * 512:(zb + 2) * 512].rearrange("p (h t) -> p h t", h=2)[:, :, col0:512]
                P.op("act", lambda e: e.activation(Eb[eb][:, :, col0:512], zin, AF.Exp),
                     r=[psr[zb], psr[zb + 1]], w=[P.R("Eb", eb)])
                P.op("act", lambda e: e.activation(SP[sp][:, :, col0:512], Eb[eb][:, :, col0:512], AF.Ln, bias=1.0),
                     r=[P.R("Eb", eb)], w=[P.R("SP", sp)])
                if diag:
                    for hd in range(2):
                        P.op("pool", lambda e, hd=hd: e.tensor_tensor(SP[sp][:, hd, col0:col0 + 128],
                                                                      SP[sp][:, hd, col0:col0 + 128], m01[:], ALU.mult),
                             r=[P.R("SP", sp), CR], w=[P.R("SP", sp)])

            def do_G1(u):
                c, kb, col0, diag = geom(u)
                sp = u % 3
                for hd in range(2):
                    pp = slice(hd * 64, (hd + 1) * 64)
                    mm(ps(4 + hd, col0, 512), ltri[:], SP[sp][:, hd, col0:512], False, True, [CR, P.R("SP", sp)],
                       [psr[4 + hd]], skip=True)

            def do_AT(u):
                c, kb, col0, diag = geom(u)
                a = u % 2
                gin = psall[:, 4 * 512:6 * 512].rearrange("p (h t) -> p h t", h=2)[:, :, col0:512]
                P.op("act", lambda e: e.activation(AT[a][:, :, col0:512], gin, AF.Exp, scale=-1.0),
                     r=[psr[4], psr[5]], w=[P.R("AT", a)])
                eb = u % 3
                P.op("dve", lambda e: e.tensor_tensor(AT[a][:, :, col0:512], Eb[eb][:, :, col0:512],
                                                      AT[a][:, :, col0:512], ALU.mult),
                     r=[P.R("Eb", eb), P.R("AT", a)], w=[P.R("AT", a)])
                if diag:
                    for hd in range(2):
                        P.op("pool", lambda e, hd=hd: e.tensor_tensor(AT[a][:, hd, col0:col0 + 128],
                                                                      AT[a][:, hd, col0:col0 + 128], m01[:], ALU.mult),
                             r=[P.R("AT", a), CR], w=[P.R("AT", a)])

            def do_G2(u):
                c, kb, col0, diag = geom(u)
                sp = u % 3
                for hd in range(2):
                    mm(ps(4 + hd, col0, 512), utri[:], SP[sp][:, hd, col0:512], False, True, [CR, P.R("SP", sp)],
                       [psr[4 + hd]], skip=True)

            def do_AV(u):
                c, kb, col0, diag = geom(u)
                a = u % 2
                ob = 6 + (c % 2)
                for hd in range(2):
                    hcol = (2 * j + hd) * 64
                    mm(psall[hd * 64:(hd + 1) * 64, ob * 512 + col0:ob * 512 + 512],
                       v_all[:, kb, hcol:hcol + 64], AT[a][:, hd, col0:512], False, True,
                       [P.R("v_all", kb), P.R("AT", a)], [psr[ob]], skip=True)

            def chain_init(c):
                ob = 6 + (c % 2)
                for b in (4, 5, ob):
                    mm(ps(b), zerob[:], qT2[:, c * 512:(c + 1) * 512], True, True, [CR, rq(c)], [psr[b]], skip=True)

            def chain_fin(c):
                ob = 6 + (c % 2)
                o = c % 2
                P.op("dve", lambda e: e.tensor_copy(osb[:], ps(ob)), r=[psr[ob]], w=[P.R("osb")])
                P.op("act", lambda e: e.activation(osq[:], ps(ob), AF.Square), r=[psr[ob]], w=[P.R("osq")])
                mm(ps(ob), blk64[:], osq[:], True, True, [CR, P.R("osq")], [psr[ob]])
                P.op("act", lambda e: e.activation(rsd[:], ps(ob), AF.Ln, bias=eps64[:], scale=1.0 / SBD),
                     r=[psr[ob], P.R("eps64")], w=[P.R("rsd")])
                P.op("act", lambda e: e.activation(rsd[:], rsd[:], AF.Exp, scale=-0.5), r=[P.R("rsd")], w=[P.R("rsd")])
                P.op("dve", lambda e, j=j: e.scalar_tensor_tensor(sbo[o][:], osb[:], sbg[:, j:j + 1], rsd[:], ALU.mult,
                                                                  ALU.mult),
                     r=[P.R("osb"), P.R("sbg"), P.R("rsd")], w=[P.R("sbo", o)])
                P.dma(mixT_d[j, :, c * 512:(c + 1) * 512], sbo[o][:], r=[P.R("sbo", o)], w=[P.R("mixT_d", j, c)])

            stg = cfg.get("sb_stage", 9)
            if stg >= 1:
                do_Z(0)
                do_ESP(0)
                if NU > 1:
                    do_Z(1)
                    do_ESP(1)
            for u in range(NU):
                c, kb, col0, diag = geom(u)
                if kb == 4 * c + 3 and stg >= 2:
                    chain_init(c)
                if stg >= 2:
                    do_G1(u)
                    do_AT(u)
                if u + 2 < NU and stg >= 1:
                    do_Z(u + 2)
                    do_ESP(u + 2)
                if stg >= 3:
                    do_G2(u)
                    if kb != 4 * c + 3:
                        do_AV(u - 1)
                    if kb == 0:
                        do_AV(u)
                if kb == 0 and stg >= 4:
                    chain_fin(c)
        P.barrier()
        ar.release(m)

    def gdn_part(l, xnT, xr, allx, w_in_b):
        m = ar.mark()
        qT = ar.alloc("gqT", [128, 4, S], BF16)
        kT = ar.alloc("gkT", [128, 4, S], BF16)
        vT = ar.alloc("gvT", [128, 4, S], BF16)
        beta = ar.alloc("g_beta", [128, 128], F32)
        gcs = ar.alloc("g_gc", [128, 128], F32)
        ebg = ar.alloc("g_ebg", [128, 128], F32)
        ekend = ar.alloc("g_ekend", [128, 128], F32)
        bgs = ar.alloc("g_bg", [128, 128], F32)
        egl = ar.alloc("g_egl", [128, 128], F32)
        gnrep = ar.alloc("g_gn", [128, 128], F32)
        eps128 = ar.alloc("eps128", [128, 1], F32)
        P.op("pool", lambda e: e.memset(eps128[:], L2_EPS), w=[P.R("eps128")])
        P.dma(gnrep[:], wd["gdn_out_norm"][l:l + 1, :].partition_broadcast(128), w=[P.R("g_gn")])
        m1 = ar.mark()
        cw = ar.alloc("cw", [128, 4, 12], F32)
        for jt in range(4):
            P.dma(cw[:, jt, :], wd["conv_w"][l, jt, :].rearrange("(g p) -> p g", p=128), w=[P.R("cw", jt)],
                  allow_slow_non_contiguous=True)
        cwr = [P.R("cw", jt) for jt in range(4)]
        wsl = [ar.alloc("wsl", [128, 8, 128], BF16) for _ in range(2)]
        wz = ar.alloc("wz", [128, 8, 512], BF16)
        wab = ar.alloc("wab", [128, 8, 8], BF16)
        wz_r = load_w(wz, w_in_b[:, OFF_GZ:OFF_GZ + 512], "wz")
        wab_r = load_w(wab, w_in_b[:, OFF_GA:OFF_GA + 8], "wab", kc=8)
        raw = [ar.alloc("raw", [128, 515], F32) for _ in range(2)]
        acc = ar.alloc("acc", [128, 512], F32)
        sil = ar.alloc("sil", [128, 512], F32)
        sqb = ar.alloc("sqb", [128, 512], BF16)
        rinv = ar.alloc("rinv", [128, 512], F32)
        zsb = [ar.alloc("zsb", [128, 512], F32) for _ in range(2)]
        dtb = ar.alloc("dtb", [128, 4], F32)
        nA = ar.alloc("nA", [128, 4], F32)
        t1 = ar.alloc("g_t1", [128, 128], F32)
        t2 = ar.alloc("g_t2", [128, 128], F32)
        t3 = ar.alloc("g_t3", [128, 128], F32)
        P.dma(dtb[:], wd["dt_bias"][l:l + 1, :].partition_broadcast(128), w=[P.R("dtb")])
        P.dma(nA[:], wd["a_log"][l:l + 1, :].partition_broadcast(128), w=[P.R("nA")])
        P.op("act", lambda e: e.activation(nA[:], nA[:], AF.Exp), r=[P.R("nA")], w=[P.R("nA")])
        P.op("dve", lambda e: e.tensor_scalar(nA[:], nA[:], -1.0, None, ALU.mult), r=[P.R("nA")], w=[P.R("nA")])
        wi = 0
        for hg in range(4):
            for X, (off, dstT) in enumerate(((OFF_GQ, qT), (OFF_GK, kT), (OFF_GV, vT))):
                ws = wsl[wi % 2]
                wsr = P.R("wsl", wi % 2)
                wi += 1
                P.dma(ws[:], w_in_b[:, off + hg * 128:off + (hg + 1) * 128].rearrange("(k p) n -> p k n", p=128),
                      w=[wsr])
                gidx = X * 4 + hg
                for c in range(8):
                    b = c % 2
                    rw = raw[c % 2]
                    rwr = P.R("raw", c % 2)
                    for k in range(8):
                        mm(ps(b), ws[:, k, :], xnT[:, k, c * 512:(c + 1) * 512], k == 0, k == 7, [wsr, xr(c)],
                           [psr[b]])
                    if c == 0:
                        P.op("pool", lambda e, rw=rw: e.memset(rw[:, 0:3], 0.0), w=[rwr])
                    else:
                        P.op("pool", lambda e, rw=rw, c=c: e.tensor_copy(rw[:, 0:3], raw[(c - 1) % 2][:, 512:515]),
                             r=[P.R("raw", (c - 1) % 2)], w=[rwr])
                    P.op("act", lambda e, rw=rw, b=b: e.copy(rw[:, 3:515], ps(b)), r=[psr[b]], w=[rwr])
                    P.op("dve", lambda e, rw=rw, gidx=gidx: e.tensor_scalar(acc[:], rw[:, 3:515], cw[:, 3, gidx:gidx + 1], None,
                                                                           ALU.mult),
                         r=[rwr] + cwr, w=[P.R("acc")])
                    for jj in (2, 1, 0):
                        P.op("dve", lambda e, rw=rw, gidx=gidx, jj=jj: e.scalar_tensor_tensor(
                            acc[:], rw[:, jj:jj + 512], cw[:, jj, gidx:gidx + 1], acc[:], ALU.mult, ALU.add),
                            r=[rwr, P.R("acc")] + cwr, w=[P.R("acc")])
                    dst = dstT[:, hg, c * 512:(c + 1) * 512]
                    dres = P.R("gqkv", X, hg, c)
                    if X == 2:
                        P.op("act", lambda e, dst=dst: e.activation(dst, acc[:], AF.Silu), r=[P.R("acc")], w=[dres])
                    else:
                        P.op("act", lambda e: e.activation(sil[:], acc[:], AF.Silu), r=[P.R("acc")], w=[P.R("sil")])
                        P.op("act", lambda e: e.activation(sqb[:], sil[:], AF.Square), r=[P.R("sil")], w=[P.R("sqb")])
                        b2 = 2 + c % 2
                        mm(ps(b2), onesb[:], sqb[:], True, True, [CR, P.R("sqb")], [psr[b2]])
                        P.op("act", lambda e, b2=b2: e.activation(rinv[:], ps(b2), AF.Ln, bias=eps128[:]),
                             r=[psr[b2], P.R("eps128")], w=[P.R("rinv")])
                        P.op("act", lambda e: e.activation(rinv[:], rinv[:], AF.Exp, scale=-0.5),
                             r=[P.R("rinv")], w=[P.R("rinv")])
                        sc = float(GD) ** -0.5 if X == 0 else 1.0
                        P.op("dve", lambda e, dst=dst, sc=sc: e.scalar_tensor_tensor(dst, sil[:], sc, rinv[:], ALU.mult,
                                                                                    ALU.mult),
                             r=[P.R("sil"), P.R("rinv")], w=[dres])
        for t in range(NT):
            b = 4 + t % 2
            for k in range(8):
                mm(ps(b), xnT[:, k, t * 128:(t + 1) * 128], wz[:, k, :], k == 0, k == 7, [xr(t // 4), wz_r(k)], [psr[b]])
            o = t % 2
            P.op("act", lambda e, b=b, o=o: e.activation(zsb[o][:], ps(b), AF.Silu), r=[psr[b]], w=[P.R("zsb", o)])
            P.dma(zs_d[t * 128:(t + 1) * 128, :], zsb[o][:], r=[P.R("zsb", o)], w=[P.R("zs_d", t)])
        for t in range(NT):
            for k in range(8):
                mm(ps(6, t * 8, t * 8 + 8), xnT[:, k, t * 128:(t + 1) * 128], wab[:, k, :], k == 0, k == 7,
                   [xr(t // 4), wab_r(k)], [psr[6]])
        ab = ps(6, 0, 256).rearrange("p (t j) -> p t j", j=8)
        t1v = t1[:].rearrange("p (t h) -> p t h", h=4)
        t2v = t2[:].rearrange("p (t h) -> p t h", h=4)
        P.op("dve", lambda e: e.tensor_tensor(t1v, ab[:, :, 0:4], bmid(dtb[:], NT),
                                              ALU.add), r=[psr[6], P.R("dtb")], w=[P.R("g_t1")])
        P.op("act", lambda e: e.activation(t1[:], t1[:], AF.Exp), r=[P.R("g_t1")], w=[P.R("g_t1")])
        P.op("act", lambda e: e.activation(t1[:], t1[:], AF.Ln, bias=1.0), r=[P.R("g_t1")], w=[P.R("g_t1")])
        P.op("dve", lambda e: e.tensor_tensor(t1v, t1v, bmid(nA[:], NT), ALU.mult),
             r=[P.R("g_t1"), P.R("nA")], w=[P.R("g_t1")])
        P.op("act", lambda e: e.activation(t2v, ab[:, :, 4:8], AF.Exp, scale=-1.0), r=[psr[6]], w=[P.R("g_t2")])
        P.op("act", lambda e: e.activation(t2[:], t2[:], AF.Ln, bias=1.0), r=[P.R("g_t2")], w=[P.R("g_t2")])
        P.op("act", lambda e: e.activation(beta[:], t2[:], AF.Exp, scale=-1.0), r=[P.R("g_t2")], w=[P.R("g_beta")])
        mm(ps(7, 0, 128), triuf[:], t1[:], True, True, [CR, P.R("g_t1")], [psr[7]])
        mm(ps(7, 128, 256), onesf[:], t1[:], True, True, [CR, P.R("g_t1")], [psr[7]])
        P.op("dve", lambda e: e.tensor_copy(gcs[:], ps(7, 0, 128)), r=[psr[7]], w=[P.R("g_gc")])
        P.op("dve", lambda e: e.tensor_tensor(bgs[:], ps(7, 0, 128), t2[:], ALU.subtract),
             r=[psr[7], P.R("g_t2")], w=[P.R("g_bg")])
        P.op("act", lambda e: e.activation(ebg[:], bgs[:], AF.Exp), r=[P.R("g_bg")], w=[P.R("g_ebg")])
        P.op("dve", lambda e: e.tensor_tensor(t3[:], ps(7, 128, 256), gcs[:], ALU.subtract),
             r=[psr[7], P.R("g_gc")], w=[P.R("g_t3")])
        P.op("act", lambda e: e.activation(ekend[:], t3[:], AF.Exp), r=[P.R("g_t3")], w=[P.R("g_ekend")])
        P.op("act", lambda e: e.activation(egl[:], ps(7, 128, 256), AF.Exp), r=[psr[7]], w=[P.R("g_egl")])
        P.barrier()
        ar.release(m1)
        ar.release_hi()
        H4 = range(4)
        Xg = [ar.alloc("Xg", [128, 256], F32) for _ in H4]
        arg = [ar.alloc("arg", [128, 256], F32) for _ in H4]
        Wm = [ar.alloc("Wm", [128, 256], F32) for _ in H4]
        egr = [ar.alloc("egr", [128, 128], F32) for _ in H4]
        QM = [ar.alloc("QM", [128, 256], BF16) for _ in H4]
        qdec = [ar.alloc("qdec", [128, 128], BF16) for _ in H4]
        kvt = [ar.alloc("kvt", [128, 128], BF16) for _ in H4]
        kvf = [ar.alloc("kvf", [128, 256], F32) for _ in H4]
        Mf = [ar.alloc("Mf", [128, 128], F32) for _ in H4]
        PQ = [[ar.alloc("PQ", [128, 256], F32) for _ in range(2)] for _ in H4]
        Y = [[ar.alloc("Y", [128, 128], F32) for _ in range(2)] for _ in H4]
        nwT = [ar.alloc("nwT", [128, 128], BF16) for _ in H4]
        vnew = [ar.alloc("vnew", [128, 128], BF16) for _ in H4]
        st = [ar.alloc("st", [128, 128], F32) for _ in H4]
        stb = [ar.alloc("stb", [128, 128], BF16) for _ in H4]
        got = [ar.alloc("got", [128, 128], F32) for _ in H4]
        gob = [ar.alloc("gob", [128, 128], BF16) for _ in H4]
        goT = [ar.alloc("goT", [128, 4, 512], BF16) for _ in range(2)]
        zt = [ar.alloc("zt", [128, 512], F32) for _ in range(2)]
        gsm = ar.alloc("gsm", [128, 8, 8], F32)
        for h in H4:
            P.op("pool", lambda e, h=h: e.memset(st[h][:], 0.0), w=[P.R("st", h)])
            P.op("pool", lambda e, h=h: e.memset(stb[h][:], 0.0), w=[P.R("stb", h)])

        def R_(n, h):
            return P.R(n, h)

        for T in range(NT):
            cs = slice(T * 128, (T + 1) * 128)
            cc = T // 4
            zi = T % 2
            P.dma(zt[zi][:], zs_d[cs, :], r=[P.R("zs_d", T)], w=[P.R("zt", zi)])
            bank = lambda h: h
            for h in H4:
                col = T * 4 + h
                P.op("dve", lambda e, h=h, col=col: e.tensor_scalar(Xg[h][:, 0:128], onesf[:], gcs[:, col:col + 1], None,
                                                                    ALU.mult),
                     r=[CR, P.R("g_gc")], w=[R_("Xg", h)])
                P.op("dve", lambda e, h=h, col=col: e.tensor_scalar(Xg[h][:, 128:256], onesf[:], bgs[:, col:col + 1],
                                                                    None, ALU.mult),
                     r=[CR, P.R("g_bg")], w=[R_("Xg", h)])
            for h in H4:
                mm(ps(h, 0, 128), Xg[h][:, 0:128], identf[:], True, True, [R_("Xg", h), CR], [psr[h]])
                mm(ps(h, 128, 256), Xg[h][:, 128:256], identf[:], True, True, [R_("Xg", h), CR], [psr[h]])
            for h in H4:
                col = T * 4 + h
                P.op("dve", lambda e, h=h, col=col: e.tensor_scalar(arg[h][:], ps(h, 0, 256), gcs[:, col:col + 1], 0.0,
                                                                    ALU.subtract, ALU.min),
                     r=[psr[h], P.R("g_gc")], w=[R_("arg", h)])
                P.op("act", lambda e, h=h: e.activation(egr[h][:], ps(h, 0, 128), AF.Exp), r=[psr[h]], w=[R_("egr", h)])
            for h in H4:
                P.op("act", lambda e, h=h: e.activation(Wm[h][:], arg[h][:], AF.Exp), r=[R_("arg", h)], w=[R_("Wm", h)])
                P.op("pool", lambda e, h=h: e.affine_select(Wm[h][:, 0:128], Wm[h][:, 0:128], [[1, 128]], ALU.is_ge, 0.0,
                                                            base=0, channel_multiplier=-1),
                     r=[R_("Wm", h)], w=[R_("Wm", h)])
                P.op("pool", lambda e, h=h: e.affine_select(Wm[h][:, 128:256], Wm[h][:, 128:256], [[1, 128]], ALU.is_gt,
                                                            0.0, base=0, channel_multiplier=-1),
                     r=[R_("Wm", h)], w=[R_("Wm", h)])
            for h in H4:
                rr = [P.R("gqkv", 0, h, cc), P.R("gqkv", 1, h, cc), P.R("gqkv", 2, h, cc)]
                mm(ps(4 + h, 0, 128), kT[:, h, cs], qT[:, h, cs], True, True, rr, [psr[4 + h]])
                mm(ps(4 + h, 128, 256), kT[:, h, cs], kT[:, h, cs], True, True, rr, [psr[4 + h]])
                P.op("pe", lambda e, h=h, cs=cs: e.transpose(psb(4 + h, 512, 640), kT[:, h, cs], ident[:]),
                     r=rr + [CR], w=[psr[4 + h]])
                P.op("pe", lambda e, h=h, cs=cs: e.transpose(psb(4 + h, 640, 768), vT[:, h, cs], ident[:]),
                     r=rr + [CR], w=[psr[4 + h]])
            for h in H4:
                col = T * 4 + h
                P.op("dve", lambda e, h=h: e.tensor_tensor(QM[h][:, 0:128], ps(4 + h, 0, 128), Wm[h][:, 0:128], ALU.mult),
                     r=[psr[4 + h], R_("Wm", h)], w=[R_("QM", h)])
                P.op("dve", lambda e, h=h: e.tensor_tensor(Mf[h][:], ps(4 + h, 128, 256), Wm[h][:, 128:256], ALU.mult),
                     r=[psr[4 + h], R_("Wm", h)], w=[R_("Mf", h)])
                P.op("dve", lambda e, h=h, cs=cs: e.tensor_tensor(qdec[h][:], qT[:, h, cs], egr[h][:], ALU.mult),
                     r=[P.R("gqkv", 0, h, cc), R_("egr", h)], w=[R_("qdec", h)])
                P.op("dve", lambda e, h=h, col=col: e.tensor_scalar(kvf[h][:, 0:128], psb(4 + h, 512, 640),
                                                                    ebg[:, col:col + 1], None, ALU.mult),
                     r=[psr[4 + h], P.R("g_ebg")], w=[R_("kvf", h)])
                P.op("dve", lambda e, h=h, col=col: e.tensor_scalar(kvt[h][:], psb(4 + h, 512, 640),
                                                                    ekend[:, col:col + 1], None, ALU.mult),
                     r=[psr[4 + h], P.R("g_ekend")], w=[R_("kvt", h)])
                P.op("dve", lambda e, h=h, col=col: e.tensor_scalar(kvf[h][:, 128:256], psb(4 + h, 640, 768),
                                                                    beta[:, col:col + 1], None, ALU.mult),
                     r=[psr[4 + h], P.R("g_beta")], w=[R_("kvf", h)])
            for h in H4:
                P.op("pe", lambda e, h=h: e.transpose(ps(h, 0, 128), Mf[h][:], identf[:]),
                     r=[R_("Mf", h), CR], w=[psr[h]])
            for h in H4:
                P.op("act", lambda e, h=h: e.copy(PQ[h][0][:, 128:256], ps(h, 0, 128)), r=[psr[h]], w=[R_("PQ0", h)])
                P.op("pool", lambda e, h=h: e.tensor_copy(PQ[h][0][:, 0:128], Mf[h][:]),
                     r=[R_("Mf", h)], w=[R_("PQ0", h)])
                P.op("dve", lambda e, h=h: e.tensor_tensor(Y[h][0][:], identf[:], Mf[h][:], ALU.subtract),
                     r=[CR, R_("Mf", h)], w=[R_("Y0", h)])
            for lv in range(6):
                a, bnx = lv % 2, (lv + 1) % 2
                pa, pb_ = "PQ%d" % a, "PQ%d" % bnx
                ya, yb = "Y%d" % a, "Y%d" % bnx
                for h in H4:
                    mm(ps(h, 0, 128), PQ[h][a][:, 128:256], PQ[h][a][:, 0:128], True, True, [R_(pa, h)], [psr[h]])
                    mm(ps(h, 128, 256), PQ[h][a][:, 0:128], PQ[h][a][:, 128:256], True, True, [R_(pa, h)], [psr[h]])
                for h in H4:
                    if h % 2:
                        P.op("act", lambda e, h=h, bnx=bnx: e.copy(PQ[h][bnx][:], ps(h, 0, 256)),
                             r=[psr[h]], w=[R_(pb_, h)])
                    else:
                        P.op("dve", lambda e, h=h, bnx=bnx: e.tensor_copy(PQ[h][bnx][:], ps(h, 0, 256)),
                             r=[psr[h]], w=[R_(pb_, h)])
                for h in H4:
                    mm(ps(h, 256, 384), PQ[h][bnx][:, 128:256], Y[h][a][:], True, True, [R_(pb_, h), R_(ya, h)],
                       [psr[h]])
                for h in H4:
                    P.op("dve", lambda e, h=h, a=a, bnx=bnx: e.tensor_tensor(Y[h][bnx][:], ps(h, 256, 384), Y[h][a][:],
                                                                             ALU.add),
                         r=[psr[h], R_(ya, h)], w=[R_(yb, h)])
            YF = [Y[h][0] for h in H4]
            yf = "Y0"
            for h in H4:
                mm(ps(h, 0, 128), kvf[h][:, 0:128], YF[h][:], True, True, [R_("kvf", h), R_(yf, h)], [psr[h]])
            for h in H4:
                P.op("act", lambda e, h=h: e.mul(nwT[h][:], ps(h, 0, 128), -1.0), r=[psr[h]], w=[R_("nwT", h)])
            for h in H4:
                mm(ps(h, 128, 256), YF[h][:], kvf[h][:, 128:256], True, False, [R_(yf, h), R_("kvf", h)], [psr[h]])
                mm(ps(h, 128, 256), nwT[h][:], stb[h][:], False, True, [R_("nwT", h), R_("stb", h)], [psr[h]])
            for h in H4:
                if h % 2:
                    P.op("act", lambda e, h=h: e.copy(vnew[h][:], ps(h, 128, 256)), r=[psr[h]], w=[R_("vnew", h)])
                else:
                    P.op("dve", lambda e, h=h: e.tensor_copy(vnew[h][:], ps(h, 128, 256)), r=[psr[h]], w=[R_("vnew", h)])
            for h in H4:
                mm(ps(4 + h, 256, 384), qdec[h][:], stb[h][:], True, False, [R_("qdec", h), R_("stb", h)], [psr[4 + h]])
                mm(ps(4 + h, 256, 384), QM[h][:, 0:128], vnew[h][:], False, True, [R_("QM", h), R_("vnew", h)],
                   [psr[4 + h]])
                mm(ps(4 + h, 384, 512), kvt[h][:], vnew[h][:], True, True, [R_("kvt", h), R_("vnew", h)],
                   [psr[4 + h]])
            for h in H4:
                col = T * 4 + h
                P.op("dve", lambda e, h=h, col=col: e.scalar_tensor_tensor(st[h][:], st[h][:], egl[:, col:col + 1],
                                                                           ps(4 + h, 384, 512), ALU.mult, ALU.add),
                     r=[R_("st", h), P.R("g_egl"), psr[4 + h]], w=[R_("st", h)])
                P.op("act", lambda e, h=h: e.copy(stb[h][:], st[h][:]), r=[R_("st", h)], w=[R_("stb", h)])
            si = T % 8
            for h in H4:
                P.op("act", lambda e, h=h, si=si: e.activation(xn_junk[:, 0:128], ps(4 + h, 256, 384), AF.Square,
                                                               accum_out=gsm[:, si, h:h + 1]),
                     r=[psr[4 + h]], w=[P.R("xn_junk"), P.R("gsm", si, h)])
            P.op("act", lambda e, si=si: e.activation(gsm[:, si, 4:8], gsm[:, si, 0:4], AF.Ln, bias=eps_col[:],
                                                      scale=1.0 / GD),
                 r=[P.R("gsm", si, h) for h in H4] + [CR], w=[P.R("gsmr", si)])
            P.op("act", lambda e, si=si: e.activation(gsm[:, si, 4:8], gsm[:, si, 4:8], AF.Exp, scale=-0.5),
                 r=[P.R("gsmr", si)], w=[P.R("gsmr", si)])
            for h in H4:
                P.op("dve", lambda e, h=h, si=si: e.scalar_tensor_tensor(got[h][:], ps(4 + h, 256, 384),
                                                                         gsm[:, si, 4 + h:5 + h], gnrep[:], ALU.mult,
                                                                         ALU.mult),
                     r=[psr[4 + h], P.R("gsmr", si), P.R("g_gn")], w=[R_("got", h)])
                P.op("pool", lambda e, h=h, zi=zi: e.tensor_tensor(gob[h][:], got[h][:], zt[zi][:, h * 128:(h + 1) * 128],
                                                                   ALU.mult),
                     r=[R_("got", h), P.R("zt", zi)], w=[R_("gob", h)])
            gi = (T // 4) % 2
            tq = T % 4
            for h in H4:
                P.op("pe", lambda e, h=h: e.transpose(psb(h, 768, 896), gob[h][:], ident[:]),
                     r=[R_("gob", h), CR], w=[psr[h]])
            for h in H4:
                if h % 2:
                    P.op("act", lambda e, h=h, gi=gi, tq=tq: e.copy(goT[gi][:, h, tq * 128:(tq + 1) * 128],
                                                                    psb(h, 768, 896)),
                         r=[psr[h]], w=[P.R("goT", gi, h)])
                else:
                    P.op("dve", lambda e, h=h, gi=gi, tq=tq: e.tensor_copy(goT[gi][:, h, tq * 128:(tq + 1) * 128],
                                                                           psb(h, 768, 896)),
                         r=[psr[h]], w=[P.R("goT", gi, h)])
            if tq == 3:
                c = T // 4
                for h in H4:
                    P.dma(mixT_d[4 + h, :, c * 512:(c + 1) * 512], goT[gi][:, h, :], r=[P.R("goT", gi, h)],
                          w=[P.R("mixT_d", 4 + h, c)])
        P.barrier()
        ar.release(m)

    def outproj_part(l, h_src, h_dst):
        m = ar.mark()
        wo = ar.alloc("mwo", [128, 8, D], BF16)
        wo_r = load_w(wo, wb["w_out"][l], "mwo")
        mx = [ar.alloc("mx", [128, 8, 512], BF16) for _ in range(2)]
        htile = [ar.alloc("htile", [128, D], F32) for _ in range(4)]
        hout = [ar.alloc("hout", [128, D], F32) for _ in range(2)]
        for g in range(8):
            gi = g % 2
            for c in range(8):
                P.dma(mx[gi][:, c, :], mixT_d[c, :, g * 512:(g + 1) * 512], r=[P.R("mixT_d", c, g)],
                      w=[P.R("mx", gi, c)])
            for t in range(4):
                T = g * 4 + t
                row0 = T * 128
                ti = T % 4
                P.dma(htile[ti][:], h_src[row0:row0 + 128, :], w=[P.R("htile", ti)], r=[P.R("hdram", row0)])
                o = T % 2
                for hf in range(2):
                    b = (T * 2 + hf) % 4
                    for c in range(8):
                        mm(ps(b), mx[gi][:, c, t * 128:(t + 1) * 128], wo[:, c, hf * 512:(hf + 1) * 512], c == 0, c == 7,
                           [P.R("mx", gi, c), wo_r(c)], [psr[b]])
                    P.op("dve", lambda e, ti=ti, hf=hf, b=b, o=o: e.tensor_tensor(
                        hout[o][:, hf * 512:(hf + 1) * 512], ps(b), htile[ti][:, hf * 512:(hf + 1) * 512], ALU.add),
                        r=[psr[b], P.R("htile", ti)], w=[P.R("hout", o, hf)])
                P.dma(h_dst[row0:row0 + 128, :], hout[o][:], r=[P.R("hout", o, 0), P.R("hout", o, 1)],
                      w=[P.R("hdram", row0)])
        P.barrier()
        ar.release(m)

    def final_phase(h_src):
        m = ar.mark()
        ht = [ar.alloc("fht", [128, D], F32) for _ in range(3)]
        ot = [ar.alloc("fot", [128, D], F32) for _ in range(3)]
        g_t, g_r = load_gain(wd["final_norm"][0:1, :])
        for t in range(NT):
            i = t % 3
            hr = P.R("fht", i)
            P.dma(ht[i][:], h_src[t * 128:(t + 1) * 128, :], w=[hr], r=[P.R("hdram", t * 128)])
            rs, rs_r = rms_rstd(ht[i][:], hr, D)
            P.op("dve", lambda e, i=i, rs=rs: e.scalar_tensor_tensor(ot[i][:], ht[i][:], rs, g_t[:], ALU.mult,
                                                                    ALU.mult),
                 r=[hr, rs_r, g_r], w=[P.R("fot", i)])
            P.dma(out_d[t * 128:(t + 1) * 128, :], ot[i][:], r=[P.R("fot", i)], w=[P.R("out", t)])
        P.barrier()
        ar.release(m)

    P.barrier()
    for l in range(1, n_layers):
        cast_layer(l)
    cur = x_d
    for l in range(n_layers):
        if "ffn1" in phases:
            ffn_phase(l, "ffn1", cur, hbuf)
            cur = hbuf
        if "mix" in phases:
            mixer_phase(l, cur, hbuf)
            cur = hbuf
        if "xattn" in phases:
            xattn_phase(l, cur, hbuf)
            cur = hbuf
        if "ffn2" in phases:
            ffn_phase(l, "ffn2", cur, hbuf)
            cur = hbuf
    final_phase(cur)
    P.finish()
    P.emit(stack)
    stack.close()
    global _LAST_PROG
    _LAST_PROG = P
    return nc


_NC_CACHE = {}
_LAST_PROG = None


def kernel(**inputs):
    cfg = inputs.pop("_cfg", None)
    key = repr(cfg)
    if key not in _NC_CACHE:
        _NC_CACHE[key] = build_nc(cfg)
    nc = _NC_CACHE[key]
    nl = (cfg or {}).get("layers", DEPTH)
    x = np.ascontiguousarray(inputs["x"], dtype=np.float32)
    mem = np.ascontiguousarray(inputs["mem"], dtype=np.float32)
    shared = {}
    for k, v in inputs.items():
        if k in ("x", "mem"):
            continue
        a = np.asarray(v, dtype=np.float32)
        if k == "final_norm":
            a = a.reshape(1, D)
        else:
            a = a[:nl]
        shared[k] = np.ascontiguousarray(a)
    in_maps = []
    for c in range(8):
        mp = dict(shared)
        mp["x"] = x[c]
        mp["mem"] = mem[c]
        in_maps.append(mp)
    res = run_bass_kernel_spmd(nc, in_maps, core_ids=list(range(8)))
    return np.stack([np.asarray(r["out"], dtype=np.float32) for r in res.results], axis=0)
```
